# Optimizing a Trainium2 kernel written in Bass

```python
import math
import jax, jax.numpy as jnp
from jax import lax
import numpy as np

D_MODEL = 1024
BATCH = 4
SEQ = 4096
DEPTH = 4

GRID_W = 64
CTX_LEN = 256
MIX_WIDTH = D_MODEL
S5_WIDTH = MIX_WIDTH // 2
S5_GROUP = 16
S5_GROUPS = S5_WIDTH // S5_GROUP
S5_STATE = 64
HG_WIDTH = MIX_WIDTH - S5_WIDTH
HG_HEAD_DIM = 128
HG_HEADS = HG_WIDTH // HG_HEAD_DIM
HG_CHUNK = 32
D_FF = 11 * D_MODEL // 4
CONV_W = 3
DT_MIN = 1e-3
DT_MAX = 1e-1
ALPHA = (2 * DEPTH) ** 0.25
BETA = (8 * DEPTH) ** -0.25
LN_EPS = 1e-5
RMS_EPS = 1e-6
U_END = S5_WIDTH
FF_END = U_END + HG_WIDTH
FB_END = FF_END + HG_WIDTH
I_END = FB_END + HG_WIDTH
Q_END = I_END + HG_WIDTH
IN_COLS = Q_END + HG_WIDTH

kernel_name = "hybrid_s5_hgrn2_deepnorm_dit"


def layer_norm(x, g, b):
    xf = x.astype(jnp.float32)
    mu = jnp.mean(xf, axis=-1, keepdims=True)
    var = jnp.mean(jnp.square(xf - mu), axis=-1, keepdims=True)
    return ((xf - mu) * lax.rsqrt(var + LN_EPS) * g + b).astype(x.dtype)


def rms_norm(x, g):
    xf = x.astype(jnp.float32)
    return xf * lax.rsqrt(jnp.mean(jnp.square(xf), axis=-1, keepdims=True) + RMS_EPS) * g


def modulate(h, shift, scale):
    return h * (1 + scale) + shift


def cmul(ar, ai, br, bi):
    return ar * br - ai * bi, ar * bi + ai * br


def s5_discretise(lam_re, lam_im, log_dt, b_re, b_im):
    f32 = jnp.float32
    lr, li = lam_re.astype(f32), lam_im.astype(f32)
    dt = jnp.exp(log_dt.astype(f32))[:, None]
    mag, ang = jnp.exp(lr * dt), li * dt
    abar_re, abar_im = mag * jnp.cos(ang), mag * jnp.sin(ang)
    den = lr * lr + li * li
    nr, ni = abar_re - 1.0, abar_im
    coef_re = (nr * lr + ni * li) / den
    coef_im = (ni * lr - nr * li) / den
    bbar_re, bbar_im = cmul(coef_re[..., None], coef_im[..., None], b_re.astype(f32), b_im.astype(f32))
    return abar_re, abar_im, bbar_re, bbar_im


def ssm_combine(e1, e2):
    a1r, a1i, b1r, b1i = e1
    a2r, a2i, b2r, b2i = e2
    ar, ai = cmul(a2r, a2i, a1r, a1i)
    br, bi = cmul(a2r, a2i, b1r, b1i)
    return ar, ai, br + b2r, bi + b2i


def s5_scan(u, abar_re, abar_im, bbar_re, bbar_im, s0):
    bu_re = jnp.einsum('bngh,gph->bngp', u, bbar_re)
    bu_im = jnp.einsum('bngh,gph->bngp', u, bbar_im)
    if s0 is not None:
        init_re, init_im = cmul(abar_re, abar_im, s0[0], s0[1])
        bu_re = bu_re.at[:, 0].add(init_re)
        bu_im = bu_im.at[:, 0].add(init_im)
    a_re = jnp.broadcast_to(abar_re, bu_re.shape)
    a_im = jnp.broadcast_to(abar_im, bu_im.shape)
    _, _, x_re, x_im = lax.associative_scan(ssm_combine, (a_re, a_im, bu_re, bu_im), axis=1)
    return x_re, x_im


def gla_direction(k, v, logf, s0, q):
    bsz, n = k.shape[:2]
    nc = n // HG_CHUNK
    chunk = lambda t: t.reshape(bsz, nc, HG_CHUNK, HG_HEADS, HG_HEAD_DIM)
    k, v, logf = chunk(k), chunk(v), chunk(logf)
    bcum = jnp.cumsum(logf, axis=2)
    bend = bcum[:, :, -1:]
    kd = k * jnp.exp(bend - bcum)
    ds = jnp.einsum('bcshk,bcshv->cbhkv', kd, v)
    dec = jnp.exp(bend[:, :, 0]).transpose(1, 0, 2, 3)
    want_out = q is not None

    def step(s, inp):
        dec_c, ds_c = inp
        return dec_c[..., None] * s + ds_c, (s if want_out else None)

    s_fin, s_start = lax.scan(step, s0, (dec, ds))
    if not want_out:
        return None, s_fin
    q = chunk(q)
    o_inter = jnp.einsum('bclhk,cbhkv->bclhv', q * jnp.exp(bcum), s_start)
    att = jnp.einsum('bclhk,bcshk->bchls', q * jnp.exp(bcum - bend), kd)
    tril = jnp.tril(jnp.ones((HG_CHUNK, HG_CHUNK), dtype=bool))
    att = jnp.where(tril, att, 0.0)
    o_intra = jnp.einsum('bchls,bcshv->bclhv', att, v)
    return (o_inter + o_intra).reshape(bsz, n, HG_HEADS, HG_HEAD_DIM), s_fin


def token_mixer(h, w_in, s5p, hgp, init, with_out, with_states):
    lam_re, lam_im, log_dt, b_re, b_im, c_re, c_im, d_skip, w_glu, b_glu = s5p
    lb, norm_w = hgp
    bsz, n = h.shape[:2]
    cols = IN_COLS if with_out else I_END
    proj = (h @ w_in[:, :cols]).astype(jnp.float32)
    u = proj[..., :U_END]
    f_raws = (proj[..., U_END:FF_END], proj[..., FF_END:FB_END])
    heads = lambda t: t.reshape(bsz, n, HG_HEADS, HG_HEAD_DIM)
    v = heads(proj[..., FB_END:I_END])
    q = heads(jax.nn.silu(proj[..., I_END:Q_END])) if with_out else None

    ug = u.reshape(bsz, n, S5_GROUPS, S5_GROUP)
    s5_ys, s5_fin, hg_os, hg_fin = [], [], [], []
    for dr in range(2):
        flip = (lambda t: t[:, ::-1]) if dr == 1 else (lambda t: t)
        abar_re, abar_im, bbar_re, bbar_im = s5_discretise(lam_re[dr], lam_im[dr], log_dt[dr], b_re[dr], b_im[dr])
        x_re, x_im = s5_scan(flip(ug), abar_re, abar_im, bbar_re, bbar_im, None if init is None else init[0][dr])
        if with_states:
            s5_fin.append((x_re[:, -1], x_im[:, -1]))
        if with_out:
            y = jnp.einsum('bngp,ghp->bngh', x_re, c_re[dr]) - jnp.einsum('bngp,ghp->bngh', x_im, c_im[dr])
            s5_ys.append(flip(y).reshape(bsz, n, S5_WIDTH))
        f = lb[dr] + (1.0 - lb[dr]) * jax.nn.sigmoid(f_raws[dr])
        s0 = jnp.zeros((bsz, HG_HEADS, HG_HEAD_DIM, HG_HEAD_DIM), jnp.float32) if init is None else init[1][dr]
        o, s_fin = gla_direction(flip(heads(1.0 - f)), flip(v), flip(heads(jnp.log(f))), s0,
                                 flip(q) if with_out else None)
        if with_states:
            hg_fin.append(s_fin)
        if with_out:
            hg_os.append(flip(o))
    states = (s5_fin, hg_fin) if with_states else None
    if not with_out:
        return None, states
    s5_y = jax.nn.gelu(s5_ys[0] + s5_ys[1] + u * d_skip)
    s5_out = s5_y * jax.nn.sigmoid(s5_y @ w_glu + b_glu)
    hg_out = rms_norm(hg_os[0] + hg_os[1], norm_w).reshape(bsz, n, HG_WIDTH) * jax.nn.silu(proj[..., Q_END:IN_COLS])
    return jnp.concatenate([s5_out, hg_out], axis=-1), states


def dwconv(u, w, b):
    m = u.shape[-2]
    pad = CONV_W // 2
    up = jnp.pad(u, [(0, 0)] * (u.ndim - 2) + [(pad, pad), (0, 0)])
    out = up[..., 0:m, :] * w[0]
    for j in range(1, CONV_W):
        out = out + up[..., j:j + m, :] * w[j]
    return out + b


def conv_ffn(h, w_up, conv_w, conv_b, w_down, on_grid):
    bsz, n = h.shape[:2]
    up = h @ w_up
    if on_grid:
        rows = n // GRID_W
        up = up.reshape(bsz, rows, GRID_W, 2 * D_FF)
    up = dwconv(up, conv_w, conv_b).reshape(bsz, n, 2 * D_FF)
    a, g = jnp.split(up, 2, axis=-1)
    return (jax.nn.silu(a) * g) @ w_down


def setup_inputs(seed: int = 0) -> dict:
    key = jax.random.key(seed)
    ks = jax.random.split(key, 32)
    f32 = jnp.float32
    L, G, P, H = DEPTH, S5_GROUPS, S5_STATE, S5_GROUP
    nrm = lambda i, shape, scale: scale * jax.random.normal(ks[i], shape, f32)
    return {
        "x": nrm(0, (BATCH, SEQ, D_MODEL), 1.0),
        "c": nrm(1, (BATCH, D_MODEL), 1.0),
        "ctx": nrm(2, (BATCH, CTX_LEN, D_MODEL), 1.0),
        "c_ctx": nrm(3, (D_MODEL,), 1.0),
        "w_mod": nrm(4, (L, D_MODEL, 6 * D_MODEL), 0.5 * D_MODEL ** -0.5),
        "b_mod": nrm(5, (L, 6 * D_MODEL), 0.01),
        "w_in": nrm(6, (L, D_MODEL, IN_COLS), D_MODEL ** -0.5),
        "s5_lam_re": -0.5 + nrm(7, (L, 2, G, P), 0.01),
        "s5_lam_im": jnp.pi * jnp.arange(P, dtype=f32) + nrm(8, (L, 2, G, P), 0.01),
        "s5_log_dt": jax.random.uniform(ks[9], (L, 2, G), f32, math.log(DT_MIN), math.log(DT_MAX)),
        "s5_b_re": nrm(10, (L, 2, G, P, H), (2 * H) ** -0.5),
        "s5_b_im": nrm(11, (L, 2, G, P, H), (2 * H) ** -0.5),
        "s5_c_re": nrm(12, (L, 2, G, H, P), P ** -0.5),
        "s5_c_im": nrm(13, (L, 2, G, H, P), P ** -0.5),
        "s5_d": nrm(14, (L, S5_WIDTH), 1.0),
        "w_glu": nrm(15, (L, S5_WIDTH, S5_WIDTH), S5_WIDTH ** -0.5),
        "b_glu": nrm(16, (L, S5_WIDTH), 0.01),
        "hg_lb": nrm(17, (L, 2, HG_WIDTH), 0.1),
        "hg_norm_w": 1.0 + nrm(18, (L, HG_HEAD_DIM), 0.01),
        "w_out": nrm(19, (L, MIX_WIDTH, D_MODEL), BETA * MIX_WIDTH ** -0.5),
        "ln1_g": 1.0 + nrm(20, (L, D_MODEL), 0.01),
        "ln1_b": nrm(21, (L, D_MODEL), 0.01),
        "w_up": nrm(22, (L, D_MODEL, 2 * D_FF), D_MODEL ** -0.5),
        "conv_w": nrm(23, (L, CONV_W, 2 * D_FF), CONV_W ** -0.5),
        "conv_b": nrm(24, (L, 2 * D_FF), 0.01),
        "w_down": nrm(25, (L, D_FF, D_MODEL), BETA * D_FF ** -0.5),
        "ln2_g": 1.0 + nrm(26, (L, D_MODEL), 0.01),
        "ln2_b": nrm(27, (L, D_MODEL), 0.01),
    }


def reference(x, c, ctx, c_ctx, w_mod, b_mod, w_in, s5_lam_re, s5_lam_im, s5_log_dt, s5_b_re, s5_b_im,
              s5_c_re, s5_c_im, s5_d, w_glu, b_glu, hg_lb, hg_norm_w, w_out, ln1_g, ln1_b,
              w_up, conv_w, conv_b, w_down, ln2_g, ln2_b):
    lb_all = jnp.cumsum(jax.nn.softmax(hg_lb.astype(jnp.float32), axis=0), axis=0)
    lb_all = lb_all - lb_all[:1]
    silu_c = jax.nn.silu(c)
    silu_cc = jax.nn.silu(c_ctx)
    for l in range(DEPTH):
        last = l == DEPTH - 1
        mod_x = (silu_c @ w_mod[l] + b_mod[l])[:, None, :]
        sh1, sc1, g1, sh2, sc2, g2 = jnp.split(mod_x, 6, axis=-1)
        mc = jnp.split(silu_cc @ w_mod[l] + b_mod[l], 6, axis=-1)
        s5p = (s5_lam_re[l], s5_lam_im[l], s5_log_dt[l], s5_b_re[l], s5_b_im[l],
               s5_c_re[l], s5_c_im[l], s5_d[l], w_glu[l], b_glu[l])
        hgp = (lb_all[l], hg_norm_w[l])
        y_c, ctx_states = token_mixer(modulate(ctx, mc[0], mc[1]), w_in[l], s5p, hgp, None, not last, True)
        y_x, _ = token_mixer(modulate(x, sh1, sc1), w_in[l], s5p, hgp, ctx_states, True, False)
        x = layer_norm(ALPHA * x + g1 * (y_x @ w_out[l]), ln1_g[l], ln1_b[l])
        x = layer_norm(ALPHA * x + g2 * conv_ffn(modulate(x, sh2, sc2), w_up[l], conv_w[l], conv_b[l], w_down[l], True),
                       ln2_g[l], ln2_b[l])
        if not last:
            ctx = layer_norm(ALPHA * ctx + mc[2] * (y_c @ w_out[l]), ln1_g[l], ln1_b[l])
            ctx = layer_norm(ALPHA * ctx + mc[5] * conv_ffn(modulate(ctx, mc[3], mc[4]), w_up[l], conv_w[l],
                                                             conv_b[l], w_down[l], False),
                             ln2_g[l], ln2_b[l])
    return x
```

```python
import numpy as np
from contextlib import ExitStack
import concourse.bass as bass
import concourse.mybir as mybir
from concourse.bass_utils import run_bass_kernel_spmd

F32 = mybir.dt.float32
F32R = mybir.dt.float32r
BF16 = mybir.dt.bfloat16
I32 = mybir.dt.int32
AF = mybir.ActivationFunctionType
ALU = mybir.AluOpType

D = 1024
DC = 8
NCTX = 256
NX = 2048
TT = NCTX + NX
DEPTH = 4
DFF = 2816
FC = DFF // 128
ALPHA = (2 * DEPTH) ** 0.25
LN_EPS = 1e-5
RMS_EPS = 1e-6

SELF_SYNC = True
COMPUTE = ("pe", "dve", "act", "pool")


class Sched:
    def __init__(self, nc, es):
        self.nc = nc
        self.es = es
        self.ops = {e: [] for e in ("pe", "dve", "act", "pool", "sp")}
        self.count = {e: 0 for e in COMPUTE}
        self.sems = {}
        self.dma_count = {}
        self.last_write = {}
        self.readers = {}
        self.waited = {e: {} for e in self.ops}
        self.nops = 0
        self.epoch = ""
        self.final_sigs = []

    def sem(self, name):
        if name not in self.sems:
            self.sems[name] = self.es.enter_context(self.nc.semaphore(name))
        return self.sems[name]

    def _deps(self, reads, writes):
        deps = set()
        for t in reads:
            if t in self.last_write:
                deps.add(self.last_write[t])
        for t in writes:
            if t in self.last_write:
                deps.add(self.last_write[t])
            for r in self.readers.get(t, ()):
                deps.add(r)
        return deps

    def _record(self, sig, reads, writes):
        for t in writes:
            self.last_write[t] = sig
            self.readers[t] = []
        for t in reads:
            self.readers.setdefault(t, []).append(sig)

    def _waits(self, eng, deps, own=None):
        waits = []
        for (s, v) in sorted(deps):
            if s == own and not SELF_SYNC:
                continue
            if self.waited[eng].get(s, 0) < v:
                waits.append((s, v))
                self.waited[eng][s] = v
        return waits

    def op(self, eng, fn, reads=(), writes=()):
        reads, writes = tuple(reads), tuple(writes)
        deps = self._deps(reads, writes)
        self.count[eng] += 1
        sname = "c_" + eng + self.epoch
        self.sem(sname)
        sig = (sname, self.count[eng])
        self.ops[eng].append((self._waits(eng, deps, sname), fn, sname, False))
        self._record(sig, reads, writes)
        self.nops += 1

    def dma(self, eng, semname, fn, reads=(), writes=(), ndma=1, inc=16):
        reads, writes = tuple(reads), tuple(writes)
        deps = self._deps(reads, writes)
        self.sem(semname)
        self.dma_count[semname] = self.dma_count.get(semname, 0) + inc * ndma
        sig = (semname, self.dma_count[semname])
        self.ops[eng].append((self._waits(eng, deps), fn, semname, True))
        self._record(sig, reads, writes)
        self.nops += 1

    def barrier(self):
        sigs = [("c_%s%s" % (e, self.epoch), self.count[e]) for e in COMPUTE if self.count[e] > 0]
        sigs += [(k, v) for k, v in self.dma_count.items()]
        sigs += list(self.final_sigs)
        for eng in self.ops:
            w = []
            for (s_, v) in sorted(sigs):
                if self.waited[eng].get(s_, 0) < v:
                    w.append((s_, v))
                    self.waited[eng][s_] = v
            self.ops[eng].append((w, None, None, False))
        self.last_write = {}
        self.readers = {}

    def new_epoch(self, tag):
        self.final_sigs = [("c_%s%s" % (e, self.epoch), self.count[e]) for e in COMPUTE if self.count[e] > 0]
        self.epoch = tag
        self.count = {e: 0 for e in COMPUTE}

    def wait_all(self, eng, tokens):
        deps = self._deps(tuple(tokens), ())
        self.ops[eng].append((self._waits(eng, deps), None, None, False))

    def emit(self):
        nc = self.nc
        with nc.Block() as block:
            def mk(ename):
                def body(e):
                    for (waits, fn, sname, is_dma) in self.ops[ename]:
                        for (s, v) in waits:
                            e.wait_ge(self.sems[s], v)
                        if fn is None:
                            continue
                        if is_dma:
                            fn(e, self.sems[sname])
                        else:
                            fn(e).then_inc(self.sems[sname], 1)
                return body
            block.tensor(mk("pe"))
            block.vector(mk("dve"))
            block.scalar(mk("act"))
            block.gpsimd(mk("pool"))
            block.sync(mk("sp"))


class Builder:
    def __init__(self, nlayers=DEPTH, debug=None):
        self.nl = nlayers
        self.debug = debug or {}
        self.nc = bass.Bass("TRN2", target_bir_lowering=False)
        self.es = ExitStack()
        self.S = Sched(self.nc, self.es)
        self.din = {}
        self.dout = {}
        self.uid = 0

    def inp(self, name, shape, dt=F32):
        t = self.nc.dram_tensor(name, list(shape), dt, kind="ExternalInput").ap()
        self.din[name] = t
        return t

    def outp(self, name, shape, dt=F32):
        t = self.nc.dram_tensor(name, list(shape), dt, kind="ExternalOutput").ap()
        self.dout[name] = t
        return t

    def scratch(self, name, shape, dt=F32):
        return self.nc.dram_tensor(name, list(shape), dt, kind="Internal").ap()

    ARENA = 106000

    def sb(self, name, shape, dt=F32):
        if not hasattr(self, "arena"):
            self.arena = self.es.enter_context(self.nc.sbuf_tensor("arena", [128, self.ARENA], BF16))
            self.apos = 0
            self.amark = 0
        assert shape[0] == 128
        n = int(np.prod(shape[1:]))
        nb = n * (2 if dt == F32 else 1)
        nb = (nb + 15) // 16 * 16
        assert self.apos + nb <= self.ARENA, "SBUF arena overflow at %s (%d + %d)" % (name, self.apos, nb)
        ap = self.arena[:, self.apos:self.apos + n * (2 if dt == F32 else 1)]
        self.apos += nb
        if dt == F32:
            ap = ap.bitcast(F32)
        if len(shape) == 3:
            ap = ap.rearrange("p (a b) -> p a b", b=shape[2])
        elif len(shape) == 4:
            ap = ap.rearrange("p (a b c) -> p a b c", b=shape[2], c=shape[3])
        elif len(shape) == 5:
            ap = ap.rearrange("p (a b c d) -> p a b c d", b=shape[2], c=shape[3], d=shape[4])
        return ap

    def hard_barrier(self):
        self.S.barrier()
        if not hasattr(self, "_hb_dram"):
            self._hb_dram = self.scratch("hb_scratch", [128, 16])
            self._hb_n = 0
        self._hb_n += 1
        self.load("hb_sem", self._hb_dram, self.ones[:, 0:16], [], ["hb%d" % self._hb_n])
        self.S.barrier()

    def stage_mark(self):
        self.amark = self.apos

    def stage_reset(self):
        self.S.barrier()
        self.apos = self.amark

    def ps(self, name, shape, dt=F32):
        return self.es.enter_context(self.nc.psum_tensor(name, list(shape), dt))

    def mm(self, out, lhsT, rhs, start, stop, reads, writes):
        self.S.op("pe", lambda e: e.matmul(out, lhsT=lhsT, rhs=rhs, start=start, stop=stop), reads, writes)

    def act(self, out, in_, func, reads, writes, bias=None, scale=None):
        kw = {}
        if bias is not None:
            kw["bias"] = bias
        if scale is not None:
            kw["scale"] = scale
        self.S.op("act", lambda e: e.activation(out=out, in_=in_, func=func, **kw), reads, writes)

    def tt(self, eng, out, in0, in1, op, reads, writes):
        self.S.op(eng, lambda e: e.tensor_tensor(out=out, in0=in0, in1=in1, op=op), reads, writes)

    def ts(self, eng, out, in0, s1, s2, op0, op1, reads, writes):
        if s2 is None:
            self.S.op(eng, lambda e: e.tensor_scalar(out=out, in0=in0, scalar1=s1, scalar2=None, op0=op0), reads, writes)
        else:
            self.S.op(eng, lambda e: e.tensor_scalar(out=out, in0=in0, scalar1=s1, scalar2=s2, op0=op0, op1=op1), reads, writes)

    def stt(self, eng, out, in0, scalar, in1, op0, op1, reads, writes):
        self.S.op(eng, lambda e: e.scalar_tensor_tensor(out=out, in0=in0, scalar=scalar, in1=in1, op0=op0, op1=op1), reads, writes)

    def copy(self, eng, out, in_, reads, writes):
        if eng == "act":
            self.S.op("act", lambda e: e.copy(out=out, in_=in_), reads, writes)
        else:
            self.S.op(eng, lambda e: e.tensor_copy(out=out, in_=in_), reads, writes)


    def sin_turns(self, out, T, TI, TN, rd_tok, ttok, otok):
        self.copy("dve", TI, T, [ttok], [ttok + "i"])
        self.copy("dve", TN, TI, [ttok + "i"], [ttok + "n"])
        self.tt("dve", T, T, TN, ALU.subtract, [ttok, ttok + "n"], [ttok])
        self.act(out, T, AF.Sin, [ttok], [otok], scale=2.0 * float(np.pi))

    def load(self, semname, out, in_, reads, writes, eng="sp"):
        self.S.dma(eng, semname, lambda e, s: e.dma_start(out=out, in_=in_).then_inc(s, 16), reads, writes)

    def dbg(self, name, ap_sb, shape, reads):
        if name not in self.debug:
            return
        o = self.outp("dbg_" + name, shape)
        self.uid += 1
        self.load("dbg%d" % self.uid, o, ap_sb, reads, ["dbgout_" + name])
        self.dbg_tokens.append("dbgout_" + name)


def _pc_layout():
    off = {}
    o = 0
    for name, n in (("b_mod", 48), ("ln1_g", 8), ("ln1_b", 8), ("ln2_g", 8), ("ln2_b", 8),
                    ("cw0", 44), ("cw1", 44), ("cw2", 44), ("cb", 44), ("s5d", 4), ("bglu", 4),
                    ("hgn", 1)):
        off[name] = (o, n)
        o += n
    return off, o


PC_OFF, PC_N = _pc_layout()


def cols(v):
    v = np.asarray(v, np.float32)
    n = v.shape[0] // 128
    return np.ascontiguousarray(v.reshape(n, 128).T)


class Model(Builder):
    def setup_common(self):
        nc = self.nc
        self.PS = [self.ps("psb%d" % b, [128, 512]) for b in range(8)]
        self.ones = self.sb("ones", [128, 128], F32)
        self.S.op("dve", lambda e: e.memset(self.ones[:], 1.0), (), ["ones"])
        self.epscol = self.sb("epscol", [128, 2], F32)
        self.S.op("dve", lambda e: e.memset(self.epscol[:, 0:1], LN_EPS), (), ["ones"])
        self.S.op("dve", lambda e: e.memset(self.epscol[:, 1:2], RMS_EPS), (), ["ones"])
        self.dbg_tokens = []

    def load_weight(self, name, src, dst, K, N, nstage_cols, stage):
        nst = len(stage)
        i = getattr(self, "_stg_i", 0)
        for k in range(K):
            for c0 in range(0, N, nstage_cols):
                cw = min(nstage_cols, N - c0)
                st = stage[i % nst]
                tok = "stage%d" % (i % nst)
                self.load("ld_" + tok, st[:, 0:cw], src[k * 128:(k + 1) * 128, c0:c0 + cw], [], [tok],
                          eng=("sp" if i % 2 == 0 else "sp"))
                self.copy("pool", dst[:, k, c0:c0 + cw], st[:, 0:cw], [tok], [name])
                i += 1
        self._stg_i = i

    def layer_norm(self, Z, ZSQ, nt, gcol, bcol, out_aps, ps1, ps2, tmp, ztok, outtoks, tag, zsqtok=None):
        PS = self.PS
        t1, t2 = "ps%d" % ps1, "ps%d" % ps2
        zq = (lambda k: zsqtok) if zsqtok else (lambda k: tag + "zsq%d" % k)
        for k in range(DC):
            self.mm(PS[ps1][:, 0:nt], self.ones[:], Z[:, k, 0:nt], k == 0, k == DC - 1, [ztok, "ones"], [t1])
        for k in range(DC):
            self.act(ZSQ[:, k, 0:nt], Z[:, k, 0:nt], AF.Square, [ztok], [zq(k)])
            self.mm(PS[ps2][:, 0:nt], self.ones[:], ZSQ[:, k, 0:nt], k == 0, k == DC - 1,
                    [zq(k), "ones"], [t2])
        mean, var, rstd, nmr = tmp["mean"], tmp["var"], tmp["rstd"], tmp["nmr"]
        mt = tag + "lnstat"
        self.act(mean[:, 0:nt], PS[ps1][:, 0:nt], AF.Copy, [t1], [mt + "m"], scale=1.0 / D)
        self.tt("dve", var[:, 0:nt], mean[:, 0:nt], mean[:, 0:nt], ALU.mult, [mt + "m"], [mt + "v"])
        self.stt("dve", var[:, 0:nt], PS[ps2][:, 0:nt], 1.0 / D, var[:, 0:nt], ALU.mult, ALU.subtract,
                 [t2, mt + "v"], [mt + "v"])
        self.act(var[:, 0:nt], var[:, 0:nt], AF.Sqrt, [mt + "v"], [mt + "v"], bias=self.epscol[:, 0:1])
        self.S.op("dve", lambda e: e.reciprocal(out=rstd[:, 0:nt], in_=var[:, 0:nt]), [mt + "v"], [mt + "r"])
        self.stt("dve", nmr[:, 0:nt], mean[:, 0:nt], -1.0, rstd[:, 0:nt], ALU.mult, ALU.mult,
                 [mt + "m", mt + "r"], [mt + "n"])
        for k in range(DC):
            eng = "dve" if k % 2 == 0 else "pool"
            self.tt(eng, ZSQ[:, k, 0:nt], Z[:, k, 0:nt], rstd[:, 0:nt], ALU.mult, [ztok, mt + "r"], [zq(k)])
            self.tt(eng, ZSQ[:, k, 0:nt], ZSQ[:, k, 0:nt], nmr[:, 0:nt], ALU.add, [zq(k), mt + "n"],
                    [zq(k)])
            self.act(out_aps[k], ZSQ[:, k, 0:nt], AF.Identity, [zq(k)], [outtoks[k]],
                     bias=bcol[:, k:k + 1], scale=gcol[:, k:k + 1])

    def alloc_ffn(self):
        self.WUP = self.sb("WUP", [128, DC, 2 * DFF], BF16)
        self.WDN = self.sb("WDN", [128, FC, D], BF16)
        self.stage = [self.sb("stage%d" % i, [128, 1408], F32) for i in range(2)]
        NT = 256
        self.f_X1 = [self.sb("fX1_%d" % i, [128, DC, NT], F32) for i in range(2)]
        self.f_H2 = [self.sb("fH2_%d" % i, [128, DC, NT], BF16) for i in range(2)]
        self.f_CVA = [self.sb("fCVA_%d" % i, [128, NT], F32) for i in range(2)]
        self.f_CVG = [self.sb("fCVG_%d" % i, [128, NT], F32) for i in range(2)]
        self.f_ACT = self.sb("fACT", [128, FC, NT], BF16)
        self.f_T = [self.sb("fT_%d" % i, [128, NT], F32) for i in range(2)]
        self.f_Z = self.sb("fZ", [128, DC, NT], F32)
        self.f_ZSQ = self.sb("fZSQ", [128, DC, NT], F32)
        self.lntmp = {n: self.sb("ln_" + n, [128, 256], F32) for n in ("mean", "var", "rstd", "nmr")}

    def ffn_stage(self, l, src, dst_fn, PC, MOD, MOD1, tiles):
        PS = self.PS
        o = PC_OFF
        for it, (t0, nt, rw, mc) in enumerate(tiles):
            pb = it % 2
            X1, H2 = self.f_X1[pb], self.f_H2[pb]
            xtok, htok = "fX1_%d" % pb, "fH2_%d" % pb
            self.load("ld_" + xtok, X1[:, :, 0:nt], src[:, t0:t0 + nt].rearrange("(k p) t -> p k t", p=128),
                      ["xs_%d" % t0], [xtok])
            for k in range(DC):
                self.act(H2[:, k, 0:nt], X1[:, k, 0:nt], AF.Identity, [xtok, "mod"], [htok],
                         bias=MOD[:, 24 + k, mc:mc + 1], scale=MOD1[:, 32 + k, mc:mc + 1])
            nr = nt // rw
            for m in range(FC):
                pp = m % 2
                for half, (mm_, cv, cvt, bank) in enumerate(((m, self.f_CVA[pp], "fCVA_%d" % pp, 0 + pp),
                                                              (m + FC, self.f_CVG[pp], "fCVG_%d" % pp, 2 + pp))):
                    pt = "ps%d" % bank
                    for k in range(DC):
                        self.mm(PS[bank][:, 0:nt], self.WUP[:, k, mm_ * 128:(mm_ + 1) * 128], H2[:, k, 0:nt],
                                k == 0, k == DC - 1, ["WUP", htok], [pt])
                    c0 = o["cw0"][0] + mm_
                    c1 = o["cw1"][0] + mm_
                    c2 = o["cw2"][0] + mm_
                    cb = o["cb"][0] + mm_
                    self.act(cv[:, 0:nt], PS[bank][:, 0:nt], AF.Identity, [pt, "pc"], [cvt],
                             bias=PC[:, cb:cb + 1], scale=PC[:, c1:c1 + 1])
                    pv = PS[bank][:, 0:nt].rearrange("p (r w) -> p r w", w=rw)
                    cvv = cv[:, 0:nt].rearrange("p (r w) -> p r w", w=rw)
                    self.stt("dve", cvv[:, :, 1:rw], pv[:, :, 0:rw - 1], PC[:, c0:c0 + 1], cvv[:, :, 1:rw],
                             ALU.mult, ALU.add, [pt, cvt, "pc"], [cvt])
                    self.stt("dve", cvv[:, :, 0:rw - 1], pv[:, :, 1:rw], PC[:, c2:c2 + 1], cvv[:, :, 0:rw - 1],
                             ALU.mult, ALU.add, [pt, cvt, "pc"], [cvt])
                T = self.f_T[pp]
                self.act(T[:, 0:nt], self.f_CVA[pp][:, 0:nt], AF.Silu, ["fCVA_%d" % pp], ["fT_%d" % pp])
                self.tt("pool", self.f_ACT[:, m, 0:nt], T[:, 0:nt], self.f_CVG[pp][:, 0:nt], ALU.mult,
                        ["fT_%d" % pp, "fCVG_%d" % pp], ["fACT"])
            for mo in range(DC):
                bank = 4 + mo % 2
                pt = "ps%d" % bank
                for k in range(FC):
                    self.mm(PS[bank][:, 0:nt], self.WDN[:, k, mo * 128:(mo + 1) * 128], self.f_ACT[:, k, 0:nt],
                            k == 0, k == FC - 1, ["WDN", "fACT"], [pt])
                T = self.f_T[mo % 2]
                self.act(T[:, 0:nt], PS[bank][:, 0:nt], AF.Copy, [pt, "mod"], ["fT_%d" % (mo % 2)],
                         scale=MOD[:, 40 + mo, mc:mc + 1])
                self.stt("dve", self.f_Z[:, mo, 0:nt], X1[:, mo, 0:nt], ALPHA, T[:, 0:nt], ALU.mult, ALU.add,
                         [xtok, "fT_%d" % (mo % 2)], ["fZ"])
            OUT = X1
            otok = xtok
            self.layer_norm(self.f_Z, self.f_ZSQ, nt, PC[:, o["ln2_g"][0]:o["ln2_g"][0] + 8],
                            PC[:, o["ln2_b"][0]:o["ln2_b"][0] + 8],
                            [OUT[:, k, 0:nt] for k in range(DC)], 6, 7, self.lntmp, "fZ", [otok] * DC, "f", zsqtok="fACT")
            for (dap, wtok) in dst_fn(t0, nt):
                self.load("st_" + otok, dap, OUT[:, :, 0:nt], [otok], [wtok])

    def alloc_mixer(self):
        NT = 512
        self.WIN = self.sb("WIN", [128, DC, 3072], BF16)
        self.WOUT = self.sb("WOUT", [128, DC, D], BF16)
        self.WGLU = self.sb("WGLU", [128, 4, 512], BF16)
        self.stage = [self.sb("stage%d" % i, [128, 1024], F32) for i in range(2)]
        self.m_X = [self.sb("mX_%d" % i, [128, DC, NT], F32) for i in range(2)]
        self.m_H = [self.sb("mH_%d" % i, [128, DC, NT], BF16) for i in range(2)]
        self.OF = self.sb("OF", [128, 4, TT], BF16)
        self.S5O = self.sb("S5O", [128, 4, TT], BF16)
        self.VT = self.sb("VT", [128, 4, 512], BF16)
        names = ("SG", "FF", "LF", "BC", "D2", "EX", "QQ", "KK")
        self.g_t = [{n: self.sb("g%s_%d" % (n, i), [128, NT], F32) for n in names} for i in range(2)]
        self.g_b = [{n: self.sb("g%s_%d" % (n, i), [128, NT], BF16) for n in ("QD1", "QD2", "KD")} for i in range(2)]
        self.g_KDT = [self.sb("gKDT_%d" % i, [128, 4, 128], BF16) for i in range(2)]
        self.g_DEC = [self.sb("gDEC_%d" % i, [128, 16], F32) for i in range(2)]
        self.g_ATT = [self.sb("gATT_%d" % i, [128, 128], BF16) for i in range(2)]
        self.S32 = [self.sb("S32_%d" % h, [128, 128], F32) for h in range(4)]
        self.SBF = [self.sb("SBF_%d" % h, [128, 128], BF16) for h in range(4)]
        self.m_HG = self.sb("mHG", [128, 4, NT], BF16)
        self.m_OS = [self.sb("mOS_%d" % i, [128, NT], F32) for i in range(2)]
        self.m_T = [self.sb("mT_%d" % i, [128, NT], F32) for i in range(2)]
        self.m_Z = self.sb("mZ", [128, DC, NT], F32)
        self.m_ZSQ = self.sb("mZSQ", [128, DC, NT], F32)
        self.lntmp = {n: self.sb("ln_" + n, [128, NT], F32) for n in ("mean", "var", "rstd", "nmr")}

    def modulate_tile(self, X, H, nt, xtok, htok, MOD, MOD1, sh0, sc0, mc, rev):
        for k in range(DC):
            out = H[:, k, nt - 1::-1] if False else H[:, k, 0:nt]
            src = X[:, k, 0:nt]
            if rev:
                src = X[:, k, 0:nt][:, ::-1]
            self.act(out, src, AF.Identity, [xtok, "mod"], [htok],
                     bias=MOD[:, sh0 + k, mc:mc + 1], scale=MOD1[:, sc0 + k, mc:mc + 1])

    def proj_fm(self, bank, H, htok, nt, col0):
        pt = "ps%d" % bank
        for k in range(DC):
            self.mm(self.PS[bank][:, 0:nt], self.WIN[:, k, col0:col0 + 128], H[:, k, 0:nt], k == 0, k == DC - 1,
                    ["WIN", htok], [pt])
        return pt

    def gla_tile(self, H, htok, nt, d, fcol0, LBT, consts, o_sink):
        PS = self.PS
        nb = nt // 128
        nch = nt // 32
        VCOL, QCOL = 1536, 2048
        SEG, GMASK, IDB = consts["seg"], consts["gmask"], consts["identb"]
        for b in range(nb):
            for k in range(DC):
                self.mm(PS[2][:, 0:512], H[:, k, b * 128:(b + 1) * 128], self.WIN[:, k, VCOL:VCOL + 512],
                        k == 0, k == DC - 1, ["WIN", htok], ["ps2"])
            self.copy("act", self.VT[:, b, :], PS[2][:, 0:512], ["ps2"], ["VT"])
        for h in range(4):
            hp = h % 2
            T, B = self.g_t[hp], self.g_b[hp]
            tk = lambda n: "g%s_%d" % (n, hp)
            pf = self.proj_fm(0, H, htok, nt, fcol0 + h * 128)
            self.act(T["SG"][:, 0:nt], PS[0][:, 0:nt], AF.Sigmoid, [pf], [tk("SG")])
            pq = self.proj_fm(1, H, htok, nt, QCOL + h * 128)
            self.act(T["QQ"][:, 0:nt], PS[1][:, 0:nt], AF.Silu, [pq], [tk("QQ")])
            lbc = d * 4 + h
            self.ts("dve", T["FF"][:, 0:nt], T["SG"][:, 0:nt], LBT["oml"][:, lbc:lbc + 1], LBT["lb"][:, lbc:lbc + 1],
                    ALU.mult, ALU.add, [tk("SG"), "lbt"], [tk("FF")])
            self.act(T["LF"][:, 0:nt], T["FF"][:, 0:nt], AF.Ln, [tk("FF")], [tk("LF")])
            self.S.op("dve", lambda e, T=T: e.tensor_tensor_scan(out=T["BC"][:, 0:nt], data0=SEG[:, 0:nt],
                                                                 data1=T["LF"][:, 0:nt], initial=0.0,
                                                                 op0=ALU.mult, op1=ALU.add),
                      [tk("LF"), "consts"], [tk("BC")])
            self.ts("pool", T["KK"][:, 0:nt], T["FF"][:, 0:nt], -1.0, 1.0, ALU.mult, ALU.add, [tk("FF")], [tk("KK")])
            bc3 = T["BC"][:, 0:nt].rearrange("p (c w) -> p c w", w=32)
            d23 = T["D2"][:, 0:nt].rearrange("p (c w) -> p c w", w=32)
            self.tt("dve", d23, bc3, bc3[:, :, 31:32].to_broadcast([128, nch, 32]), ALU.subtract, [tk("BC")], [tk("D2")])
            DEC = self.g_DEC[hp]
            self.act(DEC[:, 0:nch], T["BC"][:, 31:nt:32], AF.Exp, [tk("BC")], ["gDEC_%d" % hp])
            self.act(T["EX"][:, 0:nt], T["D2"][:, 0:nt], AF.Exp, [tk("D2")], [tk("EX")])
            self.tt("dve", B["QD2"][:, 0:nt], T["QQ"][:, 0:nt], T["EX"][:, 0:nt], ALU.mult, [tk("QQ"), tk("EX")], [tk("QD2")])
            self.act(T["EX"][:, 0:nt], T["D2"][:, 0:nt], AF.Exp, [tk("D2")], [tk("EX")], scale=-1.0)
            self.tt("pool", B["KD"][:, 0:nt], T["KK"][:, 0:nt], T["EX"][:, 0:nt], ALU.mult, [tk("KK"), tk("EX")], [tk("KD")])
            self.act(T["EX"][:, 0:nt], T["BC"][:, 0:nt], AF.Exp, [tk("BC")], [tk("EX")])
            self.tt("dve", B["QD1"][:, 0:nt], T["QQ"][:, 0:nt], T["EX"][:, 0:nt], ALU.mult, [tk("QQ"), tk("EX")], [tk("QD1")])
            ob = 6 + hp
            ot = "ps%d" % ob
            psb = PS[3][:].bitcast(BF16)
            KDT = self.g_KDT[hp]
            for b in range(nb):
                bs = slice(b * 128, (b + 1) * 128)
                self.S.op("pe", lambda e, bs=bs, B=B: e.transpose(psb[:, 0:128], B["KD"][:, bs], IDB[:]),
                          [tk("KD"), "consts"], ["ps3"])
                self.copy("act", KDT[:, b, :], psb[:, 0:128], ["ps3"], ["gKDT_%d" % hp])
                KDZ = self.g_KDZ[hp]
                self.ts("dve", KDZ[64:128, b, :], psb[64:128, 0:128], consts["m96"][64:128, 0:1], None, ALU.mult, None,
                        ["ps3", "consts"], ["gKDT_%d" % hp])
                self.mm(PS[4][:, 0:128], B["KD"][:, bs], B["QD2"][:, bs], True, True, [tk("KD"), tk("QD2")], ["ps4"])
                ATT = self.g_ATT[b % 2]
                at = "gATT_%d" % (b % 2)
                self.tt("dve", ATT[:], PS[4][:, 0:128], GMASK[:], ALU.mult, ["ps4", "consts"], [at])
                for ci in range(4):
                    cs = slice(b * 128 + ci * 32, b * 128 + ci * 32 + 32)
                    rs = slice(ci * 32, ci * 32 + 32)
                    self.mm(PS[ob][:, cs], self.SBF[h][:], B["QD1"][:, cs], b == 0 and ci == 0, False,
                            ["SBF_%d" % h, tk("QD1")], [ot])
                    if ci == 3:
                        r64 = slice(64, 128)
                        self.mm(PS[5][:, 0:128], self.g_KDZ[hp][r64, b, :], self.VT[r64, b, h * 128:(h + 1) * 128], True, True,
                                ["gKDT_%d" % hp, "VT"], ["ps5"])
                    else:
                        self.mm(PS[5][:, 0:128], KDT[rs, b, :], self.VT[rs, b, h * 128:(h + 1) * 128], True, True,
                                ["gKDT_%d" % hp, "VT"], ["ps5"])
                    cc = b * 4 + ci
                    self.stt("dve", self.S32[h][:], self.S32[h][:], DEC[:, cc:cc + 1], PS[5][:, 0:128], ALU.mult, ALU.add,
                             ["S32_%d" % h, "gDEC_%d" % hp, "ps5"], ["S32_%d" % h])
                    self.copy("act", self.SBF[h][:], self.S32[h][:], ["S32_%d" % h], ["SBF_%d" % h])
                self.mm(PS[ob][:, bs], self.VT[:, b, h * 128:(h + 1) * 128], ATT[:], False, b == nb - 1, ["VT", at], [ot])
            o_sink(h, ob, ot)

    NCH = TT // 8

    def alloc_s5(self):
        NCH = self.NCH
        self.UP = self.sb("UP", [128, 4, 8, NCH], BF16)
        self.YV = self.UP[:].rearrange("p o j c -> p (o j) c")
        self.V = self.sb("Vs5", [128, 32, NCH], BF16)
        self.FF = self.sb("FFs5", [128, 2, 2, 16, NCH], BF16)
        self.WZT = self.sb("WZT", [128, 32, 2, 128], BF16)
        self.KTOE = self.sb("KTOE", [128, 32, 128], BF16)
        self.TCA = self.sb("TCA", [128, 32, 2, 128], BF16)
        self.SELB = self.sb("SELB", [128, 64, 128], BF16)
        self.GY = self.sb("GY", [128, 4, TT], BF16)
        self.S5P = self.sb("S5P", [128, 2144], F32)
        sm = lambda n, w: self.sb("s5_" + n, [128, w], F32)
        self.p_ = {n: sm(n, 32) for n in ("DT", "LRDT", "ANG", "DEN", "NR", "CRE", "CIM", "T1", "T2", "RHO", "PHI")}
        self.p_.update({n: sm(n, 512) for n in ("KL", "KA", "MAGK", "SINK", "COSK", "PRE", "PIM", "BBR", "BBI", "X1", "X2")})
        self.p_.update({n: sm(n, 2048) for n in ("TA", "TB", "TC")})
        self.TWZ = self.sb("TWZ", [128, 16, 2, 128], BF16)
        self.TM2 = self.sb("TM2", [128, 16, 2, 128], BF16)
        self.l2 = {n: self.sb("l2_" + n, [128, NCH], F32) for n in
                   ("ZR", "ZI", "CS", "SN", "XA", "A", "B", "GR", "GI", "ER", "EI")}
        self.EFIN = self.sb("EFIN", [128, 2, 16], F32)
        self.PST = self.sb("PST", [128, 2, 16], F32)
        self.APT = self.sb("APT", [128, 2, 16], F32)
        self.KS = self.sb("KS", [128, 2, 128], F32)

    def s5_prep(self, l, s5p_dram, PC, consts):
        P = self.p_
        PS = self.PS
        S5P = self.S5P
        KV = consts["kv"]
        self.load("ld_s5p", S5P[:], s5p_dram, [], ["s5p"])
        LR, LI, LOGDT = S5P[:, 0:32], S5P[:, 32:64], S5P[:, 64:96]
        v3 = lambda a: a.rearrange("p (a h) -> p a h", h=16)
        BRE, BIM = v3(S5P[:, 96:608]), v3(S5P[:, 608:1120])
        CRE_, CIM_ = v3(S5P[:, 1120:1632]), v3(S5P[:, 1632:2144])
        tkn = lambda n: "s5_" + n
        PI = float(np.pi)
        self.act(P["DT"][:], LOGDT, AF.Exp, ["s5p"], [tkn("DT")])
        self.tt("dve", P["LRDT"][:], LR, P["DT"][:], ALU.mult, ["s5p", tkn("DT")], [tkn("LRDT")])
        self.tt("dve", P["ANG"][:], LI, P["DT"][:], ALU.mult, ["s5p", tkn("DT")], [tkn("ANG")])
        b16 = lambda a: a.unsqueeze(2).to_broadcast([128, 32, 16])
        kvb = KV[:, 0:16].unsqueeze(1).to_broadcast([128, 32, 16])
        self.tt("dve", v3(P["KL"][:]), b16(P["LRDT"][:]), kvb, ALU.mult, [tkn("LRDT"), "consts"], [tkn("KL")])
        self.tt("dve", v3(P["KA"][:]), b16(P["ANG"][:]), kvb, ALU.mult, [tkn("ANG"), "consts"], [tkn("KA")])
        self.act(P["MAGK"][:], P["KL"][:], AF.Exp, [tkn("KL")], [tkn("MAGK")])
        I2P = 1.0 / (2.0 * PI)
        self.ts("dve", P["X1"][:], P["KA"][:], I2P, None, ALU.mult, None, [tkn("KA")], [tkn("X1")])
        self.sin_turns(P["SINK"][:], P["X1"][:], self.XI[:], P["X2"][:], None, tkn("X1"), tkn("SINK"))
        self.ts("dve", P["X1"][:], P["KA"][:], I2P, 0.25, ALU.mult, ALU.add, [tkn("KA")], [tkn("X1")])
        self.sin_turns(P["COSK"][:], P["X1"][:], self.XI[:], P["X2"][:], None, tkn("X1"), tkn("COSK"))
        self.tt("dve", P["PRE"][:], P["MAGK"][:], P["COSK"][:], ALU.mult, [tkn("MAGK"), tkn("COSK")], [tkn("PRE")])
        self.tt("dve", P["PIM"][:], P["MAGK"][:], P["SINK"][:], ALU.mult, [tkn("MAGK"), tkn("SINK")], [tkn("PIM")])
        PRE3, PIM3 = v3(P["PRE"][:]), v3(P["PIM"][:])
        AR, AI = PRE3[:, :, 8], PIM3[:, :, 8]
        self.copy("dve", P["RHO"][:], v3(P["MAGK"][:])[:, :, 15], [tkn("MAGK")], [tkn("RHO")])
        self.ts("dve", P["PHI"][:], P["ANG"][:], 8.0 * I2P, None, ALU.mult, None, [tkn("ANG")], [tkn("PHI")])
        self.copy("dve", self.XI[:, 0:32], P["PHI"][:], [tkn("PHI")], ["s5_XI"])
        self.copy("dve", P["T1"][:], self.XI[:, 0:32], ["s5_XI"], [tkn("T1")])
        self.tt("dve", P["PHI"][:], P["PHI"][:], P["T1"][:], ALU.subtract, [tkn("PHI"), tkn("T1")], [tkn("PHI")])
        self.tt("dve", P["DEN"][:], LR, LR, ALU.mult, ["s5p"], [tkn("DEN")])
        self.tt("dve", P["T1"][:], LI, LI, ALU.mult, ["s5p"], [tkn("T1")])
        self.tt("dve", P["DEN"][:], P["DEN"][:], P["T1"][:], ALU.add, [tkn("DEN"), tkn("T1")], [tkn("DEN")])
        self.S.op("dve", lambda e: e.reciprocal(out=P["DEN"][:], in_=P["DEN"][:]), [tkn("DEN")], [tkn("DEN")])
        self.ts("dve", P["NR"][:], AR, -1.0, None, ALU.add, None, [tkn("PRE")], [tkn("NR")])
        self.tt("dve", P["T1"][:], P["NR"][:], LR, ALU.mult, [tkn("NR"), "s5p"], [tkn("T1")])
        self.tt("dve", P["T2"][:], AI, LI, ALU.mult, [tkn("PIM"), "s5p"], [tkn("T2")])
        self.tt("dve", P["T1"][:], P["T1"][:], P["T2"][:], ALU.add, [tkn("T1"), tkn("T2")], [tkn("T1")])
        self.tt("dve", P["CRE"][:], P["T1"][:], P["DEN"][:], ALU.mult, [tkn("T1"), tkn("DEN")], [tkn("CRE")])
        self.tt("dve", P["T1"][:], AI, LR, ALU.mult, [tkn("PIM"), "s5p"], [tkn("T1")])
        self.tt("dve", P["T2"][:], P["NR"][:], LI, ALU.mult, [tkn("NR"), "s5p"], [tkn("T2")])
        self.tt("dve", P["T1"][:], P["T1"][:], P["T2"][:], ALU.subtract, [tkn("T1"), tkn("T2")], [tkn("T1")])
        self.tt("dve", P["CIM"][:], P["T1"][:], P["DEN"][:], ALU.mult, [tkn("T1"), tkn("DEN")], [tkn("CIM")])
        cre_b, cim_b = b16(P["CRE"][:]), b16(P["CIM"][:])
        ta, tb = v3(P["TA"][:, 0:512]), v3(P["TB"][:, 0:512])
        self.tt("dve", ta, cre_b, BRE, ALU.mult, [tkn("CRE"), "s5p"], [tkn("TA")])
        self.tt("dve", tb, cim_b, BIM, ALU.mult, [tkn("CIM"), "s5p"], [tkn("TB")])
        self.tt("dve", v3(P["BBR"][:]), ta, tb, ALU.subtract, [tkn("TA"), tkn("TB")], [tkn("BBR")])
        self.tt("dve", ta, cre_b, BIM, ALU.mult, [tkn("CRE"), "s5p"], [tkn("TA")])
        self.tt("dve", tb, cim_b, BRE, ALU.mult, [tkn("CIM"), "s5p"], [tkn("TB")])
        self.tt("dve", v3(P["BBI"][:]), ta, tb, ALU.add, [tkn("TA"), tkn("TB")], [tkn("BBI")])
        BBR3, BBI3 = v3(P["BBR"][:]), v3(P["BBI"][:])

        def cplx_table(dst, dtok, d, ksl, XR, XI, neg_im):
            ds_ = slice(d * 16, d * 16 + 16)
            pr_ = PRE3[:, ds_, ksl].unsqueeze(3).to_broadcast([128, 16, 8, 16])
            pi_ = PIM3[:, ds_, ksl].unsqueeze(3).to_broadcast([128, 16, 8, 16])
            xr_ = XR[:, ds_, :].unsqueeze(2).to_broadcast([128, 16, 8, 16])
            xi_ = XI[:, ds_, :].unsqueeze(2).to_broadcast([128, 16, 8, 16])
            v4 = lambda a: a.rearrange("p (a j h) -> p a j h", j=8, h=16)
            A_, B_, C_ = v4(P["TA"][:]), v4(P["TB"][:]), v4(P["TC"][:])
            rd = [tkn("PRE"), tkn("PIM"), tkn("BBR"), tkn("BBI"), "s5p"]
            dre = dst[:, :, 0, :].rearrange("p a (j h) -> p a j h", h=16)
            dim = dst[:, :, 1, :].rearrange("p a (j h) -> p a j h", h=16)
            self.tt("dve", A_, pr_, xr_, ALU.mult, rd, [tkn("TA")])
            self.tt("pool", B_, pi_, xi_, ALU.mult, rd, [tkn("TB")])
            self.tt("dve", dre, A_, B_, ALU.subtract, [tkn("TA"), tkn("TB")], [dtok])
            self.tt("dve", A_, pr_, xi_, ALU.mult, rd, [tkn("TA")])
            self.tt("pool", B_, pi_, xr_, ALU.mult, rd, [tkn("TB")])
            if neg_im:
                self.stt("dve", dim, A_, -1.0, B_, ALU.mult, ALU.subtract, [tkn("TA"), tkn("TB")], [dtok])
            else:
                self.tt("dve", dim, A_, B_, ALU.add, [tkn("TA"), tkn("TB")], [dtok])

        IDB = consts["identb"]
        for d in range(2):
            wz_sl = slice(14, 6, -1) if d == 0 else slice(7, 15)
            m2_sl = slice(0, 8) if d == 0 else slice(7, None, -1)
            ca_sl = slice(8, 16) if d == 0 else slice(15, 7, -1)
            cplx_table(self.TWZ, "TWZ", d, wz_sl, BBR3, BBI3, False)
            cplx_table(self.TM2, "TM2", d, m2_sl, CRE_, CIM_, True)
            cplx_table(self.TCA[:, d * 16:(d + 1) * 16], "TCA", d, ca_sl, CRE_, CIM_, True)
            psb = PS[3][:].bitcast(BF16)
            for pr in range(16):
                for ri in range(2):
                    self.S.op("pe", lambda e, pr=pr, ri=ri: e.transpose(psb[:, 0:128], self.TWZ[:, pr, ri, :], IDB[:]),
                              ["TWZ", "consts"], ["ps3"])
                    self.copy("act", self.WZT[:, d * 16 + pr, ri, :], psb[:, 0:128], ["ps3"], ["WZT"])
                for g2 in range(2):
                    rs = slice(g2 * 64, g2 * 64 + 64)
                    g = 2 * pr + g2
                    bank = 4 + (g % 2)
                    self.mm(PS[bank][:, 0:128], self.TWZ[rs, pr, 0, :], self.TM2[rs, pr, 0, :], True, False,
                            ["TWZ", "TM2"], ["ps%d" % bank])
                    self.mm(PS[bank][:, 0:128], self.TWZ[rs, pr, 1, :], self.TM2[rs, pr, 1, :], False, True,
                            ["TWZ", "TM2"], ["ps%d" % bank])
                    msk = consts["maskF"] if d == 0 else consts["maskB"]
                    if d == 0:
                        self.tt("dve", self.KS[:, g % 2, :], PS[bank][:, 0:128], msk[:], ALU.mult,
                                ["ps%d" % bank, "consts"], ["KS%d" % (g % 2)])
                        self.copy("act", self.KTOE[:, g, :], self.KS[:, g % 2, :], ["KS%d" % (g % 2)], ["KTOE%d" % g])
                    else:
                        self.tt("dve", self.KS[:, g % 2, :], PS[bank][:, 0:128], msk[:], ALU.mult,
                                ["ps%d" % bank, "consts"], ["KS%d" % (g % 2)])
                        self.tt("dve", self.KS[:, g % 2, :], self.KS[:, g % 2, :], self.KTOE[:, g, :], ALU.add,
                                ["KS%d" % (g % 2), "KTOE%d" % g], ["KS%d" % (g % 2)])
                        self.stt("dve", self.KTOE[:, g, :], consts["eye"][:], consts_dcol(self, PC, g), self.KS[:, g % 2, :],
                                 ALU.mult, ALU.add, ["KS%d" % (g % 2), "consts", "pcd"], ["KTOE%d" % g])


def consts_dcol(self, PC, g):
    return self.DSK[:, g:g + 1]


def _s5_main(self, l, PC, consts, exchange_fn):
    PS = self.PS
    NCH = self.NCH
    P = self.p_
    L2 = self.l2
    PI = float(np.pi)
    MIDX = consts["midx"]
    for g in range(32):
        oc, gi = g // 8, g % 8
        bank = g % 2
        for j in range(8):
            self.mm(PS[bank][:, 0:NCH], self.SELB[:, gi * 8 + j, :], self.UP[:, oc, j, :], j == 0, j == 7,
                    ["SELB", "UP"], ["ps%d" % bank])
        self.copy("act", self.V[:, g, :], PS[bank][:, 0:NCH], ["ps%d" % bank], ["V"])

    def level2(d, pr, segs):
        dp = d * 16 + pr
        for ri, bank in ((0, 2), (1, 3)):
            for g2 in range(2):
                self.mm(PS[bank][g2 * 64:(g2 + 1) * 64, 0:NCH], self.WZT[:, dp, ri, g2 * 64:(g2 + 1) * 64],
                        self.V[:, 2 * pr + g2, :], True, True, ["WZT", "V"], ["ps%d" % bank])
        self.copy("act", L2["ZR"][:], PS[2][:, 0:NCH], ["ps2"], ["l2ZR"])
        self.copy("act", L2["ZI"][:], PS[3][:, 0:NCH], ["ps3"], ["l2ZI"])
        phi = P["PHI"][:, dp:dp + 1]
        rho = P["RHO"][:, dp:dp + 1]
        for (zs, n, init, fsl, fin) in segs:
            ZR, ZI = L2["ZR"][:, zs], L2["ZI"][:, zs]
            if init == "P":
                self.tt("dve", ZR[:, 0:1], ZR[:, 0:1], self.APT[:, 0, pr:pr + 1], ALU.add, ["l2ZR", "APT"], ["l2ZR"])
                self.tt("dve", ZI[:, 0:1], ZI[:, 0:1], self.APT[:, 1, pr:pr + 1], ALU.add, ["l2ZI", "APT"], ["l2ZI"])
            self.ts("dve", L2["XA"][:, 0:n], MIDX[:, 0:n], phi, None, ALU.mult, None, ["consts", "s5_PHI"], ["l2XA"])
            self.sin_turns(L2["SN"][:, 0:n], L2["XA"][:, 0:n], self.l2I[:, 0:n], L2["TN"][:, 0:n], None, "l2XA", "l2SN")
            self.ts("dve", L2["XA"][:, 0:n], MIDX[:, 0:n], phi, 0.25, ALU.mult, ALU.add, ["consts", "s5_PHI"], ["l2XA"])
            self.sin_turns(L2["CS"][:, 0:n], L2["XA"][:, 0:n], self.l2I[:, 0:n], L2["TN"][:, 0:n], None, "l2XA", "l2CS")
            CS, SN = L2["CS"][:, 0:n], L2["SN"][:, 0:n]
            A, B, GR, GI = L2["A"][:, 0:n], L2["B"][:, 0:n], L2["GR"][:, 0:n], L2["GI"][:, 0:n]
            self.tt("dve", A, CS, ZR, ALU.mult, ["l2CS", "l2ZR"], ["l2A"])
            self.tt("pool", B, SN, ZI, ALU.mult, ["l2SN", "l2ZI"], ["l2B"])
            self.tt("dve", GR, A, B, ALU.add, ["l2A", "l2B"], ["l2GR"])
            self.tt("dve", A, CS, ZI, ALU.mult, ["l2CS", "l2ZI"], ["l2A"])
            self.tt("pool", B, SN, ZR, ALU.mult, ["l2SN", "l2ZR"], ["l2B"])
            self.tt("dve", GI, A, B, ALU.subtract, ["l2A", "l2B"], ["l2GI"])
            rb = rho.to_broadcast([128, n])
            self.S.op("dve", lambda e, GR=GR, rb=rb: e.tensor_tensor_scan(out=GR, data0=rb, data1=GR, initial=0.0,
                                                                         op0=ALU.mult, op1=ALU.add),
                      ["l2GR", "s5_RHO"], ["l2GR"])
            self.S.op("dve", lambda e, GI=GI, rb=rb: e.tensor_tensor_scan(out=GI, data0=rb, data1=GI, initial=0.0,
                                                                         op0=ALU.mult, op1=ALU.add),
                      ["l2GI", "s5_RHO"], ["l2GI"])
            ER, EI = L2["ER"][:, 0:n], L2["EI"][:, 0:n]
            self.tt("dve", A, CS, GR, ALU.mult, ["l2CS", "l2GR"], ["l2A"])
            self.tt("pool", B, SN, GI, ALU.mult, ["l2SN", "l2GI"], ["l2B"])
            self.tt("dve", ER, A, B, ALU.subtract, ["l2A", "l2B"], ["l2ER"])
            self.tt("dve", A, SN, GR, ALU.mult, ["l2SN", "l2GR"], ["l2A"])
            self.tt("pool", B, CS, GI, ALU.mult, ["l2CS", "l2GI"], ["l2B"])
            self.tt("dve", EI, A, B, ALU.add, ["l2A", "l2B"], ["l2EI"])
            FR = self.FF[:, d, 0, pr, :][:, fsl]
            FI = self.FF[:, d, 1, pr, :][:, fsl]
            self.copy("act", FR[:, 1:n], ER[:, 0:n - 1], ["l2ER"], ["FF"])
            self.copy("act", FI[:, 1:n], EI[:, 0:n - 1], ["l2EI"], ["FF"])
            if init == "P":
                self.copy("act", FR[:, 0:1], self.PST[:, 0, pr:pr + 1], ["PST"], ["FF"])
                self.copy("act", FI[:, 0:1], self.PST[:, 1, pr:pr + 1], ["PST"], ["FF"])
            if fin:
                self.copy("act", self.EFIN[:, 0, pr:pr + 1], ER[:, n - 1:n], ["l2ER"], ["EFIN"])
                self.copy("act", self.EFIN[:, 1, pr:pr + 1], EI[:, n - 1:n], ["l2EI"], ["EFIN"])

    self.S.op("pool", lambda e: e.memset(self.FF[:], 0.0), (), ["FF"])
    for pr in range(16):
        level2(0, pr, [(slice(0, NCH), NCH, None, slice(0, NCH), True)])
    exchange_fn()
    v3 = lambda a: a.rearrange("p (a h) -> p a h", h=16)
    ARb, AIb = v3(P["PRE"][:])[:, 16:32, 15], v3(P["PIM"][:])[:, 16:32, 15]
    T1, T2 = P["T1"][:, 0:16], P["T2"][:, 0:16]
    self.tt("dve", T1, ARb, self.PST[:, 0, :], ALU.mult, ["s5_PRE", "PST"], ["s5_T1"])
    self.tt("dve", T2, AIb, self.PST[:, 1, :], ALU.mult, ["s5_PIM", "PST"], ["s5_T2"])
    self.tt("dve", self.APT[:, 0, :], T1, T2, ALU.subtract, ["s5_T1", "s5_T2"], ["APT"])
    self.tt("dve", T1, ARb, self.PST[:, 1, :], ALU.mult, ["s5_PRE", "PST"], ["s5_T1"])
    self.tt("dve", T2, AIb, self.PST[:, 0, :], ALU.mult, ["s5_PIM", "PST"], ["s5_T2"])
    self.tt("dve", self.APT[:, 1, :], T1, T2, ALU.add, ["s5_T1", "s5_T2"], ["APT"])
    NCC = NCTX // 8
    for pr in range(16):
        level2(1, pr, [(slice(NCH - 1, NCC - 1, -1), NCH - NCC, "P", slice(NCH - 1, NCC - 1, -1), False),
                       (slice(NCC - 1, None, -1), NCC, None, slice(NCC - 1, None, -1), False)])
    for g in range(32):
        pr, g2 = g // 2, g % 2
        rs = slice(g2 * 64, g2 * 64 + 64)
        bank = 4 + g % 2
        pt = "ps%d" % bank
        self.mm(PS[bank][:, 0:NCH], self.KTOE[:, g, :], self.V[:, g, :], True, False, ["KTOE%d" % g, "V"], [pt])
        for d in range(2):
            for ri in range(2):
                self.mm(PS[bank][:, 0:NCH], self.TCA[rs, d * 16 + pr, ri, :], self.FF[rs, d, ri, pr, :], False,
                        d == 1 and ri == 1, ["TCA", "FF"], [pt])
        self.copy("act", self.YV[:, g, :], PS[bank][:, 0:NCH], [pt], ["UP"])


Model.s5_main = _s5_main


def _exchange(self, name, src_ap, ncols, dst_sb, groups, outtok):
    self.uid += 1
    u = self.uid
    dsrc = self.scratch("xsrc%d" % u, [128, ncols])
    ddst = self.scratch("xdst%d" % u, [256, ncols])
    both = self.sb("xboth%d" % u, [128, 2, ncols], F32)
    t = "x%d" % u
    self.load("xs%d" % u, dsrc, src_ap, [name], [t + "a"], eng="pool")
    self.S.dma("pool", "xc%d" % u,
               lambda e, s: e.collective_compute("AllGather", ALU.bypass, replica_groups=groups, ins=[dsrc],
                                                 outs=[ddst]).then_inc(s, 1),
               [t + "a"], [t + "b"], inc=1)
    self.load("xl%d" % u, both[:], ddst.rearrange("(k p) f -> p k f", p=128), [t + "b"], [t + "c"], eng="pool")
    PSEL = self.consts["pairsel"]
    self.ts("dve", dst_sb, both[:, 0, :], PSEL[:, 0:1], None, ALU.mult, None, [t + "c", "consts"], [outtok])
    self.stt("dve", dst_sb, both[:, 1, :], PSEL[:, 1:2], dst_sb, ALU.mult, ALU.add, [t + "c", "consts", outtok],
             [outtok])


Model.exchange = _exchange


def _load_consts(self, cd):
    C = {}
    for n, w in (("seg", 512), ("gmask", 128), ("eye", 128), ("maskF", 128), ("maskB", 128), ("kv", 16),
                 ("midx", 288), ("negpi", 1), ("pairsel", 2), ("m96", 1)):
        C[n] = self.sb("c_" + n, [128, w], F32)
        self.load("ld_c_" + n, C[n][:], cd[n], [], ["consts"])
    C["identb"] = self.sb("c_identb", [128, 128], BF16)
    self.copy("dve", C["identb"][:], C["eye"][:], ["consts"], ["consts"])
    self.consts = C
    return C


Model.load_consts = _load_consts


def _mod_layer(self, l, w_mod, PC, SC, MOD, MOD1, stage):
    PS = self.PS
    CB = 1024
    i = 0
    for cb in range(6):
        bank = cb % 2
        for k in range(DC):
            st = stage[i % 2]
            tok = "stage%d" % (i % 2)
            self.load("ld_" + tok, st[:, 0:CB], w_mod[l, k * 128:(k + 1) * 128, cb * CB:(cb + 1) * CB], [], [tok])
            for mt in range(8):
                self.mm(PS[bank][:, mt * 2:mt * 2 + 2], st[:, mt * 128:(mt + 1) * 128], SC[:, k, :], k == 0 and mt == 0,
                        k == DC - 1 and mt == 7, [tok, "SC"], ["ps%d" % bank])
            i += 1
        pv = PS[bank][:, 0:16].rearrange("p (m c) -> p m c", c=2)
        bm = PC[:, cb * 8:(cb + 1) * 8].unsqueeze(2).to_broadcast([128, 8, 2])
        self.tt("dve", MOD[:, cb * 8:(cb + 1) * 8, :], pv, bm, ALU.add, ["ps%d" % bank, "pc"], ["mod0"])
    self.ts("dve", MOD1[:], MOD[:], 1.0, None, ALU.add, None, ["mod0"], ["mod"])


Model.mod_layer = _mod_layer


def _load_win(self, l, w_in, WIN, c0, c1, stage):
    i = 0
    for k in range(DC):
        for cc in range(c0, c1, 1024):
            cw = min(1024, c1 - cc)
            st = stage[i % 2]
            tok = "stage%d" % (i % 2)
            self.load("ld_" + tok, st[:, 0:cw], w_in[l, k * 128:(k + 1) * 128, cc:cc + cw], [], [tok])
            self.copy("pool", WIN[:, k, cc:cc + cw], st[:, 0:cw], [tok], ["WIN"])
            i += 1


Model.load_win = _load_win

TILES = [(0, 256, 256, 1)] + [(256 + 256 * i, 256, 64, 0) for i in range(8)]


def _stage0(self, l, xs, MOD, MOD1):
    X, H = self.m_X[0], self.m_H
    for it, (t0, nt, rw, mc) in enumerate(TILES):
        hb = it % 2
        self.load("ld_mX", X[:, :, 0:nt], xs[:, t0:t0 + nt].rearrange("(k p) t -> p k t", p=128), ["xs"], ["mX"])
        self.modulate_tile(X, H[hb], nt, "mX", "mH%d" % hb, MOD, MOD1, 0, 8, mc, False)
        c0, ncn = t0 // 8, nt // 8
        for oc in range(4):
            bank = oc % 2
            pt = self.proj_fm(bank, H[hb], "mH%d" % hb, nt, oc * 128)
            self.copy("act", self.UP[:, oc, :, c0:c0 + ncn].rearrange("p j c -> p c j"),
                      self.PS[bank][:, 0:nt].rearrange("p (c j) -> p c j", j=8), [pt], ["UP"])


Model.stage0 = _stage0


def _s5_out(self, l, PC):
    PS = self.PS
    NCH = self.NCH
    GY = self.GY
    for oc in range(4):
        for j in range(8):
            bank = j % 2
            for gi in range(8):
                self.mm(PS[bank][:, 0:NCH], self.SELB[:, gi * 8 + j, :], self.YV[:, oc * 8 + gi, :], gi == 0, gi == 7,
                        ["SELB", "UP"], ["ps%d" % bank])
            self.act(GY[:, oc, j:TT:8], PS[bank][:, 0:NCH], AF.Gelu, ["ps%d" % bank], ["GY"])
    bg = PC_OFF["bglu"][0]
    for (t0, nt, rw, mc) in TILES:
        for mo in range(4):
            bank = 2 + mo % 2
            for k in range(4):
                self.mm(PS[bank][:, 0:nt], self.WGLU[:, k, mo * 128:(mo + 1) * 128], GY[:, k, t0:t0 + nt], k == 0, k == 3,
                        ["WGLU", "GY"], ["ps%d" % bank])
            T = self.s_T[mo % 2]
            self.act(T[:, 0:nt], PS[bank][:, 0:nt], AF.Sigmoid, ["ps%d" % bank, "pc"], ["sT%d" % (mo % 2)],
                     bias=PC[:, bg + mo:bg + mo + 1])
            self.tt("dve", self.S5O[:, mo, t0:t0 + nt], GY[:, mo, t0:t0 + nt], T[:, 0:nt], ALU.mult,
                    ["GY", "sT%d" % (mo % 2)], ["S5O"])


Model.s5_out = _s5_out


def _stage1(self, l, xs, MOD, MOD1, LBT):
    X, H = self.m_X[0], self.m_H
    for h in range(4):
        self.S.op("pool", lambda e, h=h: e.memset(self.S32[h][:], 0.0), (), ["S32_%d" % h])
        self.S.op("pool", lambda e, h=h: e.memset(self.SBF[h][:], 0.0), (), ["SBF_%d" % h])
    for it, (t0, nt, rw, mc) in enumerate(TILES):
        hb = it % 2
        self.load("ld_mX", X[:, :, 0:nt], xs[:, t0:t0 + nt].rearrange("(k p) t -> p k t", p=128), ["xs"], ["mX"])
        self.modulate_tile(X, H[hb], nt, "mX", "mH%d" % hb, MOD, MOD1, 0, 8, mc, False)

        def sink(h, ob, ot, t0=t0, nt=nt):
            self.copy("act", self.OF[:, h, t0:t0 + nt], self.PS[ob][:, 0:nt], [ot], ["OF"])
        self.gla_tile(H[hb], "mH%d" % hb, nt, 0, 512, LBT, self.consts, sink)


Model.stage1 = _stage1


def _stage3(self, l, xs, x1dst, MOD, MOD1, LBT, PC, last):
    PS = self.PS
    X, H = self.m_X[0], self.m_H
    o = PC_OFF
    hgn = PC[:, o["hgn"][0]:o["hgn"][0] + 1]
    order = list(range(len(TILES) - 1, 0, -1)) + [0]
    for ii, it in enumerate(order):
        (t0, nt, rw, mc) = TILES[it]
        hb = ii % 2
        if it == 0:
            for h in range(4):
                self.S.op("pool", lambda e, h=h: e.memset(self.S32[h][:], 0.0), (), ["S32_%d" % h])
                self.S.op("pool", lambda e, h=h: e.memset(self.SBF[h][:], 0.0), (), ["SBF_%d" % h])
        self.load("ld_mX", X[:, :, 0:nt], xs[:, t0:t0 + nt].rearrange("(k p) t -> p k t", p=128), ["xs"], ["mX"])
        self.modulate_tile(X, H[hb], nt, "mX", "mH%d" % hb, MOD, MOD1, 0, 8, mc, True)
        htok = "mH%d" % hb

        def sink(h, ob, ot, t0=t0, nt=nt, H=H[hb], htok=htok):
            OS = self.m_OS[h % 2]
            ost = "mOS%d" % (h % 2)
            self.tt("dve", OS[:, 0:nt], PS[ob][:, 0:nt], self.OF[:, h, t0:t0 + nt][:, ::-1], ALU.add, [ot, "OF"], [ost])
            T = self.m_T[h % 2]
            tt_ = "mT%d" % (h % 2)
            self.act(T[:, 0:nt], OS[:, 0:nt], AF.Square, [ost], [tt_])
            self.mm(PS[2][:, 0:nt], self.ones[:], T[:, 0:nt], True, True, [tt_, "ones"], ["ps2"])
            self.act(T[:, 0:nt], PS[2][:, 0:nt], AF.Sqrt, ["ps2"], [tt_], bias=self.epscol[:, 1:2], scale=1.0 / 128)
            self.S.op("dve", lambda e, T=T: e.reciprocal(out=T[:, 0:nt], in_=T[:, 0:nt]), [tt_], [tt_])
            self.stt("dve", OS[:, 0:nt], OS[:, 0:nt], hgn, T[:, 0:nt], ALU.mult, ALU.mult, [ost, tt_, "pc"], [ost])
            pg = self.proj_fm(3, H, htok, nt, 2560 + h * 128)
            self.act(T[:, 0:nt], PS[3][:, 0:nt], AF.Silu, [pg], [tt_])
            self.tt("dve", self.m_HG[:, h, 0:nt], OS[:, 0:nt], T[:, 0:nt], ALU.mult, [ost, tt_], ["mHG"])
        fcol = 1024
        self.gla_tile(H[hb], htok, nt, 1, fcol, LBT, self.consts, sink)
        for mo in range(DC):
            bank = mo % 2
            pt = "ps%d" % bank
            for k in range(4):
                self.mm(PS[bank][:, 0:nt], self.WOUT[:, k, mo * 128:(mo + 1) * 128], self.S5O[:, k, t0:t0 + nt][:, ::-1],
                        k == 0, False, ["WOUT", "S5O"], [pt])
            for k in range(4):
                self.mm(PS[bank][:, 0:nt], self.WOUT[:, 4 + k, mo * 128:(mo + 1) * 128], self.m_HG[:, k, 0:nt],
                        False, k == 3, ["WOUT", "mHG"], [pt])
            T = self.m_T[mo % 2]
            tt_ = "mT%d" % (mo % 2)
            self.act(T[:, 0:nt], PS[bank][:, 0:nt], AF.Copy, [pt, "mod"], [tt_], scale=MOD[:, 16 + mo, mc:mc + 1])
            self.stt("dve", self.m_Z[:, mo, 0:nt], X[:, mo, 0:nt][:, ::-1], ALPHA, T[:, 0:nt], ALU.mult, ALU.add,
                     ["mX", tt_], ["mZ"])
        self.layer_norm(self.m_Z, self.m_ZSQ, nt, PC[:, o["ln1_g"][0]:o["ln1_g"][0] + 8],
                        PC[:, o["ln1_b"][0]:o["ln1_b"][0] + 8],
                        [X[:, k, 0:nt][:, ::-1] for k in range(DC)], 2, 3, self.lntmp, "mZ", ["mX"] * DC, "m")
        self.load("st_mX", x1dst[:, t0:t0 + nt].rearrange("(k p) t -> p k t", p=128), X[:, :, 0:nt], ["mX"], ["x1"])


Model.stage3 = _stage3


def _alloc_stage13(self):
    NT = 256
    self.WIN = self.sb("WIN", [128, DC, 3072], BF16)
    self.WOUT = self.sb("WOUT", [128, DC, D], BF16)
    self.m_X = [self.sb("mX_0", [128, DC, NT], F32)]
    self.m_H = [self.sb("mH_%d" % i, [128, DC, NT], BF16) for i in range(2)]
    self.VT = self.sb("VT", [128, 2, 512], BF16)
    names = ("SG", "FF", "LF", "BC", "D2", "EX", "QQ", "KK")
    self.g_t = [{n: self.sb("g%s_%d" % (n, i), [128, NT], F32) for n in names} for i in range(2)]
    self.g_b = [{n: self.sb("g%s_%d" % (n, i), [128, NT], BF16) for n in ("QD1", "QD2", "KD")} for i in range(2)]
    self.g_KDT = [self.sb("gKDT_%d" % i, [128, 2, 128], BF16) for i in range(2)]
    self.g_KDZ = [self.sb("gKDZ_%d" % i, [128, 2, 128], BF16) for i in range(2)]
    self.g_DEC = [self.sb("gDEC_%d" % i, [128, 8], F32) for i in range(2)]
    self.g_ATT = [self.sb("gATT_%d" % i, [128, 128], BF16) for i in range(2)]
    self.m_HG = self.sb("mHG", [128, 4, NT], BF16)
    self.m_OS = [self.sb("mOS_%d" % i, [128, NT], F32) for i in range(2)]
    self.m_T = [self.sb("mT_%d" % i, [128, NT], F32) for i in range(2)]
    self.m_Z = self.sb("mZ", [128, DC, NT], F32)
    self.m_ZSQ = self.sb("mZSQ", [128, DC, NT], F32)
    self.lntmp = {n: self.sb("ln_" + n, [128, NT], F32) for n in ("mean", "var", "rstd", "nmr")}


Model.alloc_stage13 = _alloc_stage13


def _alloc_ffn2(self):
    NT = 256
    self.WUP = self.sb("WUP", [128, DC, 2 * DFF], BF16)
    self.WDN = self.sb("WDN", [128, FC, D], BF16)
    self.f_X1 = [self.sb("fX1_0", [128, DC, NT], F32)] * 2
    self.f_H2 = [self.sb("fH2_0", [128, DC, NT], BF16)] * 2
    self.f_CVA = [self.sb("fCVA_%d" % i, [128, NT], F32) for i in range(2)]
    self.f_CVG = [self.sb("fCVG_%d" % i, [128, NT], F32) for i in range(2)]
    self.f_ACT = self.sb("fACT", [128, FC, NT], BF16)
    self.f_T = [self.sb("fT_%d" % i, [128, NT], F32) for i in range(2)]
    self.f_Z = self.sb("fZ", [128, DC, NT], F32)
    self.f_ZSQ = self.f_ACT.rearrange("p a b -> p (a b)")[:, 0:DC * NT * 2].bitcast(F32).rearrange("p (a b) -> p a b", b=NT)
    self.lntmp = {n: self.sb("ln_" + n, [128, NT], F32) for n in ("mean", "var", "rstd", "nmr")}


Model.alloc_ffn2 = _alloc_ffn2


def _alloc_s5a(self):
    self.WZT = self.sb("WZT", [128, 32, 2, 128], BF16)
    self.KTOE = self.sb("KTOE", [128, 32, 128], BF16)
    self.TCA = self.sb("TCA", [128, 32, 2, 128], BF16)
    sm = lambda n, w: self.sb("s5_" + n, [128, w], F32)
    self.p_ = {n: sm(n, 32) for n in ("RHO", "PHI", "T1", "T2")}
    self.p_.update({n: sm(n, 512) for n in ("PRE", "PIM")})


def _alloc_s5a_tmp(self):
    sm = lambda n, w: self.sb("s5_" + n, [128, w], F32)
    self.S5P = self.sb("S5P", [128, 2144], F32)
    self.p_.update({n: sm(n, 32) for n in ("DT", "LRDT", "ANG", "DEN", "NR", "CRE", "CIM")})
    self.p_.update({n: sm(n, 512) for n in ("KL", "KA", "MAGK", "SINK", "COSK", "BBR", "BBI", "X1", "X2")})
    self.p_.update({n: sm(n, 2048) for n in ("TA", "TB", "TC")})
    self.TWZ = self.sb("TWZ", [128, 16, 2, 128], BF16)
    self.TM2 = self.sb("TM2", [128, 16, 2, 128], BF16)
    self.KS = self.sb("KS", [128, 2, 128], F32)
    self.XI = self.sb("s5_XI", [128, 512], F32).bitcast(I32)


def _alloc_s5b(self):
    NCH = self.NCH
    self.V = self.sb("Vs5", [128, 32, NCH], BF16)
    self.GY = self.V.rearrange("p g c -> p (g c)").rearrange("p (o t) -> p o t", o=4)
    self.FF = self.sb("FFs5", [128, 2, 2, 16, NCH], BF16)
    self.SELB = self.sb("SELB", [128, 64, 128], BF16)
    self.l2 = {n: self.sb("l2_" + n, [128, NCH], F32) for n in
               ("ZR", "ZI", "CS", "SN", "XA", "A", "B", "GR", "GI", "ER", "EI", "TN")}
    self.s_T = [self.sb("sT_%d" % i, [128, 256], F32) for i in range(2)]
    self.l2I = self.sb("l2_I", [128, NCH], F32).bitcast(I32)
    self.WGLU = self.sb("WGLU", [128, 4, 512], BF16)


Model.alloc_s5a, Model.alloc_s5a_tmp, Model.alloc_s5b = _alloc_s5a, _alloc_s5a_tmp, _alloc_s5b


def _load_sel(self, sel_dram):
    flat = self.SELB.rearrange("p a b -> p (a b)")
    for i in range(8):
        st = self.stage[i % 2]
        tok = "stage%d" % (i % 2)
        self.load("ld_" + tok, st[:, 0:1024], sel_dram[:, i * 1024:(i + 1) * 1024], [], [tok])
        self.copy("pool", flat[:, i * 1024:(i + 1) * 1024], st[:, 0:1024], [tok], ["SELB"])


Model.load_sel = _load_sel


def build_program(ncores=8, nlayers=DEPTH, debug=None, stop_after=None):
    m = Model(nlayers, debug)
    m.setup_common()
    L = DEPTH
    LW = nlayers
    xT = m.inp("xT", [D, TT])
    w_in = m.inp("w_in", [LW, D, 3072]); w_out = m.inp("w_out", [LW, D, D]); w_glu = m.inp("w_glu", [LW, 512, 512])
    w_up = m.inp("w_up", [LW, D, 2 * DFF]); w_down = m.inp("w_down", [LW, DFF, D]); w_mod = m.inp("w_mod", [LW, D, 6 * D])
    pc = m.inp("pc", [L, 128, PC_N]); dsk = m.inp("dsk", [L, 128, 32]); s5p = m.inp("s5p", [L, 128, 2144])
    cin = m.inp("cin", [128, DC, 2]); hglb = m.inp("hglb", [128, L, 8])
    sel = m.inp("sel", [128, 8192]); selT = m.inp("selT", [128, 8192])
    cd = {n: m.inp("k_" + n, [128, w]) for n, w in (("seg", 512), ("gmask", 128), ("eye", 128), ("maskF", 128),
                                                      ("maskB", 128), ("kv", 16), ("midx", 288), ("negpi", 1),
                                                      ("pairsel", 2), ("m96", 1))}
    out = m.outp("outT", [D, NX])
    XS = m.scratch("XS", [D, TT])
    X1S = m.scratch("X1S", [D, TT])
    groups = [[2 * i, 2 * i + 1] for i in range(ncores // 2)]
    C = m.load_consts(cd)
    PC = m.sb("PC", [128, PC_N]); MOD = m.sb("MOD", [128, 48, 2]); MOD1 = m.sb("MOD1", [128, 48, 2])
    m.DSK = m.sb("DSK", [128, 32])
    SC = m.sb("SC", [128, DC, 2]); LBA = m.sb("LBA", [128, L, 8]); OMA = m.sb("OMA", [128, L, 8])
    LSUM = m.sb("LSUM", [128, 8])
    m.S32 = [m.sb("S32_%d" % h, [128, 128], F32) for h in range(4)]
    m.SBF = [m.sb("SBF_%d" % h, [128, 128], BF16) for h in range(4)]
    m.EFIN = m.sb("EFIN", [128, 2, 16]); m.PST = m.sb("PST", [128, 2, 16]); m.APT = m.sb("APT", [128, 2, 16])
    GSRC = m.sb("GSRC", [128, 512]); GDST = m.sb("GDST", [128, 512])
    m.stage = [m.sb("stage%d" % i, [128, 1408], F32) for i in range(2)]
    m.load("ld_sc", SC[:], cin, [], ["SC0"])
    m.act(SC[:], SC[:], AF.Silu, ["SC0"], ["SC"])
    m.load("ld_lb", LBA[:], hglb, [], ["LBA0"])
    m.act(LBA[:], LBA[:], AF.Exp, ["LBA0"], ["LBA0"])
    m.copy("dve", LSUM[:], LBA[:, 0, :], ["LBA0"], ["LSUM"])
    for l in range(1, L):
        m.tt("dve", LSUM[:], LSUM[:], LBA[:, l, :], ALU.add, ["LSUM", "LBA0"], ["LSUM"])
    m.S.op("dve", lambda e: e.reciprocal(out=LSUM[:], in_=LSUM[:]), ["LSUM"], ["LSUM"])
    for l in range(L):
        m.tt("dve", LBA[:, l, :], LBA[:, l, :], LSUM[:], ALU.mult, ["LSUM", "LBA0"], ["LBA0"])
    m.S.op("dve", lambda e: e.memset(LBA[:, 0, :], 0.0), ["LBA0"], ["LBA0"])
    for l in range(2, L):
        m.tt("dve", LBA[:, l, :], LBA[:, l, :], LBA[:, l - 1, :], ALU.add, ["LBA0"], ["LBA0"])
    m.ts("dve", OMA[:], LBA[:], -1.0, 1.0, ALU.mult, ALU.add, ["LBA0"], ["lbt"])
    m.stage_mark()
    base = m.apos

    def finish(dumps):
        m.S.barrier()
        for i, (nm, ap, shape, dt) in enumerate(dumps):
            o = m.outp("dbg_" + nm, shape, dt)
            m.load("dbgs%d" % i, o, ap, [], ["dbgo%d" % i])
        m.S.barrier()
        m.S.emit()
        return m

    for l in range(nlayers):
        last = l == DEPTH - 1
        m.hard_barrier()
        m.S.new_epoch("_L%d" % l)
        src = xT if l == 0 else XS
        LBT = {"lb": LBA[:, l, :], "oml": OMA[:, l, :]}
        m.load("ld_pc", PC[:], pc[l], [], ["pc"])
        m.load("ld_dsk", m.DSK[:], dsk[l], [], ["pcd"])
        m.mod_layer(l, w_mod, PC, SC, MOD, MOD1, m.stage)
        if stop_after == "mod":
            return finish([("MOD", MOD, [128, 48, 2], F32), ("LBA", LBA, [128, L, 8], F32), ("SC", SC, [128, DC, 2], F32)])
        m.S5O = m.sb("S5O", [128, 4, TT], BF16)
        mark1 = m.apos
        m.alloc_s5a()
        mark2 = m.apos
        m.alloc_s5a_tmp()
        m.s5_prep(l, s5p[l], PC, C)
        if stop_after == "s5prep":
            return finish([("WZT", m.WZT, [128, 32, 2, 128], BF16), ("KTOE", m.KTOE, [128, 32, 128], BF16),
                           ("TCA", m.TCA, [128, 32, 2, 128], BF16), ("PRE", m.p_["PRE"], [128, 512], F32),
                           ("PIM", m.p_["PIM"], [128, 512], F32), ("RHO", m.p_["RHO"], [128, 32], F32),
                           ("PHI", m.p_["PHI"], [128, 32], F32), ("BBR", m.p_["BBR"], [128, 512], F32)])
        m.hard_barrier(); m.apos = mark2
        m.UP = m.sb("UP", [128, 4, 8, m.NCH], BF16)
        m.YV = m.UP.rearrange("p o j c -> p (o j) c")
        mark3 = m.apos
        m.WIN = m.sb("WINu", [128, DC, 512], BF16)
        m.m_X = [m.sb("mX_0", [128, DC, 256], F32)]
        m.m_H = [m.sb("mH_%d" % i, [128, DC, 256], BF16) for i in range(2)]
        m.load_win(l, w_in, m.WIN, 0, 512, m.stage)
        m.stage0(l, src, MOD, MOD1)
        if stop_after == "stage0":
            return finish([("UP", m.UP, [128, 4, 8, m.NCH], BF16)])
        m.hard_barrier(); m.apos = mark3
        m.alloc_s5b()
        m.load_sel(sel)
        m.load_weight("WGLU", w_glu[l], m.WGLU, 4, 512, 512, m.stage)

        def xch():
            m.exchange("EFIN", m.EFIN.rearrange("p a b -> p (a b)"), 32, m.PST.rearrange("p a b -> p (a b)"), groups, "PST")
        m.s5_main(l, PC, C, xch)
        if stop_after == "s5main":
            return finish([("YV", m.YV, [128, 32, m.NCH], BF16), ("FF", m.FF, [128, 2, 2, 16, m.NCH], BF16),
                           ("V", m.V, [128, 32, m.NCH], BF16), ("PST", m.PST, [128, 2, 16], F32),
                           ("EFIN", m.EFIN, [128, 2, 16], F32)])
        m.load_sel(selT)
        m.s5_out(l, PC)
        if stop_after == "s5out":
            return finish([("S5O", m.S5O, [128, 4, TT], BF16), ("GY", m.GY, [128, 4, TT], BF16),
                           ("YV", m.YV, [128, 32, m.NCH], BF16)])
        m.hard_barrier(); m.apos = mark1
        if stop_after == "stage1":
            oe = m.outp("dbg_S5Oearly", [128, 4, TT], BF16)
            m.load("dbgearly", oe, m.S5O, [], ["dbgearly"])
            m.S.barrier()
        m.OF = m.sb("OF", [128, 4, TT], BF16)
        m.alloc_stage13()
        m.load_win(l, w_in, m.WIN, 512, 3072, m.stage)
        m.load_weight("WOUT", w_out[l], m.WOUT, DC, D, 1024, m.stage)
        m.stage1(l, src, MOD, MOD1, LBT)
        if stop_after == "stage1":
            return finish([("OF", m.OF, [128, 4, TT], BF16), ("S5O", m.S5O, [128, 4, TT], BF16)] +
                          [("S32_%d" % h, m.S32[h], [128, 128], F32) for h in range(4)])
        for h in range(4):
            m.copy("dve", GSRC[:, h * 128:(h + 1) * 128], m.S32[h][:], ["S32_%d" % h], ["GSRC"])
        m.exchange("GSRC", GSRC[:], 512, GDST[:], groups, "GSRC_p")
        for h in range(4):
            m.copy("dve", m.S32[h][:], GDST[:, h * 128:(h + 1) * 128], ["GSRC_p"], ["S32_%d" % h])
            m.copy("act", m.SBF[h][:], GDST[:, h * 128:(h + 1) * 128], ["GSRC_p"], ["SBF_%d" % h])
        m.stage3(l, src, X1S, MOD, MOD1, LBT, PC, last)
        m.hard_barrier(); m.apos = base
        m.alloc_ffn2()
        m.load_weight("WUP", w_up[l], m.WUP, DC, 2 * DFF, 1408, m.stage)
        m.load_weight("WDN", w_down[l], m.WDN, FC, D, 1024, m.stage)
        tiles = TILES[1:] if last else TILES
        if last:
            dst_fn = lambda t0, nt: [(out[:, t0 - NCTX:t0 - NCTX + nt].rearrange("(k p) t -> p k t", p=128), "xs")]
        else:
            dst_fn = lambda t0, nt: [(XS[:, t0:t0 + nt].rearrange("(k p) t -> p k t", p=128), "xs")]
        m.ffn_stage(l, X1S, dst_fn, PC, MOD, MOD1, tiles)
        m.hard_barrier(); m.apos = base
    if nlayers < DEPTH:
        dx = m.outp("dbgXS", [D, TT]); d1 = m.outp("dbgX1", [D, TT])
        m.load("dbg_a", dx, XS, ["xs"], ["dbg1"])
        m.load("dbg_b", d1, X1S, ["x1"], ["dbg2"])
    m.S.barrier()
    m.S.emit()
    return m


def _const_tables():
    t = np.arange(512)
    seg = np.broadcast_to((t % 32 != 0).astype(np.float32), (128, 512)).copy()
    s = np.arange(128)
    gmask = ((s[:, None] // 32 == s[None, :] // 32) & (s[None, :] >= s[:, None])).astype(np.float32)
    eye = np.eye(128, dtype=np.float32)
    jj = s // 16
    maskF = (jj[None, :] >= jj[:, None]).astype(np.float32)
    maskB = (jj[None, :] <= jj[:, None]).astype(np.float32)
    kv = np.broadcast_to(np.arange(-7, 9, dtype=np.float32), (128, 16)).copy()
    midx = np.broadcast_to(np.arange(1, 289, dtype=np.float32), (128, 288)).copy()
    negpi = np.full((128, 1), -np.pi, np.float32)
    m96 = (np.arange(128) >= 96).astype(np.float32).reshape(128, 1)
    sel = np.zeros((128, 8, 8, 128), np.float32)
    selT = np.zeros((128, 8, 8, 128), np.float32)
    for gi in range(8):
        for j in range(8):
            for h in range(16):
                sel[gi * 16 + h, gi, j, j * 16 + h] = 1.0
                selT[j * 16 + h, gi, j, gi * 16 + h] = 1.0
    return dict(seg=seg, gmask=gmask, eye=eye, maskF=maskF, maskB=maskB, kv=kv, midx=midx, negpi=negpi, m96=m96), \
        sel.reshape(128, 8192), selT.reshape(128, 8192)


def prepare_core_inputs(inputs, b, s):
    f32 = lambda a: np.ascontiguousarray(np.asarray(a, np.float32))
    L = DEPTH
    x, c, ctx, c_ctx = inputs["x"], inputs["c"], inputs["ctx"], inputs["c_ctx"]
    xl = np.asarray(x[b, s * NX:(s + 1) * NX])
    cl = np.asarray(ctx[b])
    if s == 1:
        xl, cl = xl[::-1], cl[::-1]
    dd = [0, 1] if s == 0 else [1, 0]
    m = {}
    m["xT"] = f32(np.concatenate([cl, xl], axis=0).T)
    w_in = np.asarray(inputs["w_in"])
    if s == 1:
        w_in = np.concatenate([w_in[:, :, 0:512], w_in[:, :, 1024:1536], w_in[:, :, 512:1024], w_in[:, :, 1536:]], axis=2)
    m["w_in"] = f32(w_in)
    for n in ("w_out", "w_glu", "w_up", "w_down", "w_mod"):
        m[n] = f32(inputs[n])
    pc = np.zeros((L, 128, PC_N), np.float32)
    dsk = np.zeros((L, 128, 32), np.float32)
    s5p = np.zeros((L, 128, 2144), np.float32)
    for l in range(L):
        cw = np.asarray(inputs["conv_w"][l])
        taps = [cw[0], cw[1], cw[2]] if s == 0 else [cw[2], cw[1], cw[0]]
        vec = {"b_mod": inputs["b_mod"][l], "ln1_g": inputs["ln1_g"][l], "ln1_b": inputs["ln1_b"][l],
               "ln2_g": inputs["ln2_g"][l], "ln2_b": inputs["ln2_b"][l], "cw0": taps[0], "cw1": taps[1],
               "cw2": taps[2], "cb": inputs["conv_b"][l], "s5d": inputs["s5_d"][l], "bglu": inputs["b_glu"][l],
               "hgn": inputs["hg_norm_w"][l]}
        for n, (o_, k) in PC_OFF.items():
            pc[l, :, o_:o_ + k] = cols(vec[n])
        sd = np.asarray(inputs["s5_d"][l]).reshape(32, 16)
        dsk[l] = np.tile(sd.T, (8, 1))
        for dl in range(2):
            d = dd[dl]
            for nm, off in (("s5_lam_re", 0), ("s5_lam_im", 32)):
                a = np.asarray(inputs[nm][l, d]).reshape(16, 2, 64)
                s5p[l, :, off + dl * 16:off + dl * 16 + 16] = a.transpose(1, 2, 0).reshape(128, 16)
            ld = np.asarray(inputs["s5_log_dt"][l, d]).reshape(16, 2)
            s5p[l, :, 64 + dl * 16:64 + dl * 16 + 16] = np.repeat(ld.T[:, None, :], 64, axis=1).reshape(128, 16)
            for nm, off in (("s5_b_re", 96), ("s5_b_im", 608)):
                a = np.asarray(inputs[nm][l, d]).reshape(16, 2, 64, 16)
                s5p[l, :, off + dl * 256:off + dl * 256 + 256] = a.transpose(1, 2, 0, 3).reshape(128, 256)
            for nm, off in (("s5_c_re", 1120), ("s5_c_im", 1632)):
                a = np.asarray(inputs[nm][l, d]).reshape(16, 2, 16, 64)
                s5p[l, :, off + dl * 256:off + dl * 256 + 256] = a.transpose(1, 3, 0, 2).reshape(128, 256)
    m["pc"], m["dsk"], m["s5p"] = pc, dsk, s5p
    m["cin"] = f32(np.stack([cols(c[b]), cols(c_ctx)], axis=-1))
    hg = np.asarray(inputs["hg_lb"])
    hglb = np.zeros((128, L, 8), np.float32)
    for l in range(L):
        for dl in range(2):
            hglb[:, l, dl * 4:dl * 4 + 4] = cols(hg[l, dd[dl]])
    m["hglb"] = hglb
    ct, sel, selT = _const_tables()
    for n, v in ct.items():
        m["k_" + n] = v
    m["k_pairsel"] = np.broadcast_to(np.array([0.0, 1.0] if s == 0 else [1.0, 0.0], np.float32), (128, 2)).copy()
    m["sel"], m["selT"] = sel, selT
    return m


_PROG = {}


def kernel(**inputs):
    ncores = 8
    if "p" not in _PROG:
        _PROG["p"] = build_program(ncores, DEPTH)
    prog = _PROG["p"]
    in_maps = [prepare_core_inputs(inputs, cid // 2, cid % 2) for cid in range(ncores)]
    res = run_bass_kernel_spmd(prog.nc, in_maps, core_ids=list(range(ncores)))
    B = 4
    outp = np.zeros((B, 2 * NX, D), np.float32)
    for cid in range(ncores):
        b, s = cid // 2, cid % 2
        o = np.asarray(res.results[cid]["outT"]).T
        if s == 1:
            o = o[::-1]
        outp[b, s * NX:(s + 1) * NX] = o
    return outp
```

```python
import numpy as np
from contextlib import ExitStack
import concourse.bass as bass
import concourse.mybir as mybir
from concourse.bass_utils import run_bass_kernel_spmd

F32 = mybir.dt.float32
F32R = mybir.dt.float32r
BF16 = mybir.dt.bfloat16
I32 = mybir.dt.int32
AF = mybir.ActivationFunctionType
ALU = mybir.AluOpType

D = 1024
DC = 8
NCTX = 256
NX = 2048
TT = NCTX + NX
DEPTH = 4
DFF = 2816
FC = DFF // 128
ALPHA = (2 * DEPTH) ** 0.25
LN_EPS = 1e-5
RMS_EPS = 1e-6

SELF_SYNC = True
COMPUTE = ("pe", "dve", "act", "pool")


class Sched:
    def __init__(self, nc, es):
        self.nc = nc
        self.es = es
        self.ops = {e: [] for e in ("pe", "dve", "act", "pool", "sp")}
        self.count = {e: 0 for e in COMPUTE}
        self.sems = {}
        self.dma_count = {}
        self.last_write = {}
        self.readers = {}
        self.waited = {e: {} for e in self.ops}
        self.nops = 0
        self.epoch = ""
        self.final_sigs = []

    def sem(self, name):
        if name not in self.sems:
            self.sems[name] = self.es.enter_context(self.nc.semaphore(name))
        return self.sems[name]

    def _deps(self, reads, writes):
        deps = set()
        for t in reads:
            if t in self.last_write:
                deps.add(self.last_write[t])
        for t in writes:
            if t in self.last_write:
                deps.add(self.last_write[t])
            for r in self.readers.get(t, ()):
                deps.add(r)
        return deps

    def _record(self, sig, reads, writes):
        for t in writes:
            self.last_write[t] = sig
            self.readers[t] = []
        for t in reads:
            self.readers.setdefault(t, []).append(sig)

    def _waits(self, eng, deps, own=None):
        waits = []
        for (s, v) in sorted(deps):
            if s == own and (not SELF_SYNC or eng == "pe"):
                continue
            if self.waited[eng].get(s, 0) < v:
                waits.append((s, v))
                self.waited[eng][s] = v
        return waits

    def op(self, eng, fn, reads=(), writes=()):
        reads, writes = tuple(reads), tuple(writes)
        deps = self._deps(reads, writes)
        self.count[eng] += 1
        sname = "c_" + eng + self.epoch
        self.sem(sname)
        sig = (sname, self.count[eng])
        self.ops[eng].append((self._waits(eng, deps, sname), fn, sname, False))
        self._record(sig, reads, writes)
        self.nops += 1

    def dma(self, eng, semname, fn, reads=(), writes=(), ndma=1, inc=16):
        reads, writes = tuple(reads), tuple(writes)
        deps = self._deps(reads, writes)
        self.sem(semname)
        self.dma_count[semname] = self.dma_count.get(semname, 0) + inc * ndma
        sig = (semname, self.dma_count[semname])
        self.ops[eng].append((self._waits(eng, deps), fn, semname, True))
        self._record(sig, reads, writes)
        self.nops += 1

    def barrier(self):
        sigs = [("c_%s%s" % (e, self.epoch), self.count[e]) for e in COMPUTE if self.count[e] > 0]
        sigs += [(k, v) for k, v in self.dma_count.items()]
        sigs += list(self.final_sigs)
        for eng in self.ops:
            w = []
            for (s_, v) in sorted(sigs):
                if self.waited[eng].get(s_, 0) < v:
                    w.append((s_, v))
                    self.waited[eng][s_] = v
            self.ops[eng].append((w, None, None, False))
        self.last_write = {}
        self.readers = {}

    def new_epoch(self, tag):
        self.final_sigs = [("c_%s%s" % (e, self.epoch), self.count[e]) for e in COMPUTE if self.count[e] > 0]
        self.epoch = tag
        self.count = {e: 0 for e in COMPUTE}

    def wait_all(self, eng, tokens):
        deps = self._deps(tuple(tokens), ())
        self.ops[eng].append((self._waits(eng, deps), None, None, False))

    def emit(self):
        nc = self.nc
        with nc.Block() as block:
            def mk(ename):
                def body(e):
                    for (waits, fn, sname, is_dma) in self.ops[ename]:
                        for (s, v) in waits:
                            e.wait_ge(self.sems[s], v)
                        if fn is None:
                            continue
                        if is_dma:
                            fn(e, self.sems[sname])
                        else:
                            fn(e).then_inc(self.sems[sname], 1)
                return body
            block.tensor(mk("pe"))
            block.vector(mk("dve"))
            block.scalar(mk("act"))
            block.gpsimd(mk("pool"))
            block.sync(mk("sp"))


class Builder:
    def __init__(self, nlayers=DEPTH, debug=None):
        self.nl = nlayers
        self.debug = debug or {}
        self.nc = bass.Bass("TRN2", target_bir_lowering=False)
        self.es = ExitStack()
        self.S = Sched(self.nc, self.es)
        self.din = {}
        self.dout = {}
        self.uid = 0

    def inp(self, name, shape, dt=F32):
        t = self.nc.dram_tensor(name, list(shape), dt, kind="ExternalInput").ap()
        self.din[name] = t
        return t

    def outp(self, name, shape, dt=F32):
        t = self.nc.dram_tensor(name, list(shape), dt, kind="ExternalOutput").ap()
        self.dout[name] = t
        return t

    def scratch(self, name, shape, dt=F32):
        return self.nc.dram_tensor(name, list(shape), dt, kind="Internal").ap()

    ARENA = 106000

    def sb(self, name, shape, dt=F32):
        if not hasattr(self, "arena"):
            self.arena = self.es.enter_context(self.nc.sbuf_tensor("arena", [128, self.ARENA], BF16))
            self.apos = 0
            self.amark = 0
        assert shape[0] == 128
        n = int(np.prod(shape[1:]))
        nb = n * (2 if dt == F32 else 1)
        nb = (nb + 15) // 16 * 16
        assert self.apos + nb <= self.ARENA, "SBUF arena overflow at %s (%d + %d)" % (name, self.apos, nb)
        ap = self.arena[:, self.apos:self.apos + n * (2 if dt == F32 else 1)]
        self.apos += nb
        if dt == F32:
            ap = ap.bitcast(F32)
        if len(shape) == 3:
            ap = ap.rearrange("p (a b) -> p a b", b=shape[2])
        elif len(shape) == 4:
            ap = ap.rearrange("p (a b c) -> p a b c", b=shape[2], c=shape[3])
        elif len(shape) == 5:
            ap = ap.rearrange("p (a b c d) -> p a b c d", b=shape[2], c=shape[3], d=shape[4])
        return ap

    def hard_barrier(self):
        self.S.barrier()
        if not hasattr(self, "_hb_dram"):
            self._hb_dram = self.scratch("hb_scratch", [128, 16])
            self._hb_n = 0
        self._hb_n += 1
        self.load("hb_sem", self._hb_dram, self.ones[:, 0:16], [], ["hb%d" % self._hb_n])
        self.S.barrier()

    def stage_mark(self):
        self.amark = self.apos

    def stage_reset(self):
        self.S.barrier()
        self.apos = self.amark

    def ps(self, name, shape, dt=F32):
        return self.es.enter_context(self.nc.psum_tensor(name, list(shape), dt))

    def mm(self, out, lhsT, rhs, start, stop, reads, writes):
        self.S.op("pe", lambda e: e.matmul(out, lhsT=lhsT, rhs=rhs, start=start, stop=stop), reads, writes)

    def act(self, out, in_, func, reads, writes, bias=None, scale=None):
        kw = {}
        if bias is not None:
            kw["bias"] = bias
        if scale is not None:
            kw["scale"] = scale
        self.S.op("act", lambda e: e.activation(out=out, in_=in_, func=func, **kw), reads, writes)

    def tt(self, eng, out, in0, in1, op, reads, writes):
        self.S.op(eng, lambda e: e.tensor_tensor(out=out, in0=in0, in1=in1, op=op), reads, writes)

    def ts(self, eng, out, in0, s1, s2, op0, op1, reads, writes):
        if s2 is None:
            self.S.op(eng, lambda e: e.tensor_scalar(out=out, in0=in0, scalar1=s1, scalar2=None, op0=op0), reads, writes)
        else:
            self.S.op(eng, lambda e: e.tensor_scalar(out=out, in0=in0, scalar1=s1, scalar2=s2, op0=op0, op1=op1), reads, writes)

    def stt(self, eng, out, in0, scalar, in1, op0, op1, reads, writes):
        self.S.op(eng, lambda e: e.scalar_tensor_tensor(out=out, in0=in0, scalar=scalar, in1=in1, op0=op0, op1=op1), reads, writes)

    def copy(self, eng, out, in_, reads, writes):
        if eng == "act":
            self.S.op("act", lambda e: e.copy(out=out, in_=in_), reads, writes)
        else:
            self.S.op(eng, lambda e: e.tensor_copy(out=out, in_=in_), reads, writes)


    def sin_turns(self, out, T, TI, TN, rd_tok, ttok, otok):
        self.copy("dve", TI, T, [ttok], [ttok + "i"])
        self.copy("dve", TN, TI, [ttok + "i"], [ttok + "n"])
        self.tt("dve", T, T, TN, ALU.subtract, [ttok, ttok + "n"], [ttok])
        self.act(out, T, AF.Sin, [ttok], [otok], scale=2.0 * float(np.pi))

    def load(self, semname, out, in_, reads, writes, eng="sp"):
        self.S.dma(eng, semname, lambda e, s: e.dma_start(out=out, in_=in_).then_inc(s, 16), reads, writes)

    def dbg(self, name, ap_sb, shape, reads):
        if name not in self.debug:
            return
        o = self.outp("dbg_" + name, shape)
        self.uid += 1
        self.load("dbg%d" % self.uid, o, ap_sb, reads, ["dbgout_" + name])
        self.dbg_tokens.append("dbgout_" + name)


def _pc_layout():
    off = {}
    o = 0
    for name, n in (("b_mod", 48), ("ln1_g", 8), ("ln1_b", 8), ("ln2_g", 8), ("ln2_b", 8),
                    ("cw0", 44), ("cw1", 44), ("cw2", 44), ("cb", 44), ("s5d", 4), ("bglu", 4),
                    ("hgn", 1)):
        off[name] = (o, n)
        o += n
    return off, o


PC_OFF, PC_N = _pc_layout()


def cols(v):
    v = np.asarray(v, np.float32)
    n = v.shape[0] // 128
    return np.ascontiguousarray(v.reshape(n, 128).T)


class Model(Builder):
    def setup_common(self):
        nc = self.nc
        self.PS = [self.ps("psb%d" % b, [128, 512]) for b in range(8)]
        self.ones = self.sb("ones", [128, 128], F32)
        self.S.op("dve", lambda e: e.memset(self.ones[:], 1.0), (), ["ones"])
        self.epscol = self.sb("epscol", [128, 2], F32)
        self.S.op("dve", lambda e: e.memset(self.epscol[:, 0:1], LN_EPS), (), ["ones"])
        self.S.op("dve", lambda e: e.memset(self.epscol[:, 1:2], RMS_EPS), (), ["ones"])
        self.dbg_tokens = []

    def load_weight(self, name, src, dst, K, N, nstage_cols, stage):
        nst = len(stage)
        i = getattr(self, "_stg_i", 0)
        for k in range(K):
            for c0 in range(0, N, nstage_cols):
                cw = min(nstage_cols, N - c0)
                st = stage[i % nst]
                tok = "stage%d" % (i % nst)
                self.load("ld_" + tok, st[:, 0:cw], src[k * 128:(k + 1) * 128, c0:c0 + cw], [], [tok],
                          eng=("sp" if i % 2 == 0 else "sp"))
                self.copy(("pool", "act", "dve")[i % 3], dst[:, k, c0:c0 + cw], st[:, 0:cw], [tok], [name])
                i += 1
        self._stg_i = i

    def layer_norm(self, Z, ZSQ, nt, gcol, bcol, out_aps, ps1, ps2, tmp, ztok, outtoks, tag, zsqtok=None):
        PS = self.PS
        t1, t2 = "ps%d" % ps1, "ps%d" % ps2
        zq = (lambda k: zsqtok) if zsqtok else (lambda k: tag + "zsq%d" % k)
        for k in range(DC):
            self.mm(PS[ps1][:, 0:nt], self.ones[:], Z[:, k, 0:nt], k == 0, k == DC - 1, [ztok, "ones"], [t1])
        for k in range(DC):
            self.act(ZSQ[:, k, 0:nt], Z[:, k, 0:nt], AF.Square, [ztok], [zq(k)])
            self.mm(PS[ps2][:, 0:nt], self.ones[:], ZSQ[:, k, 0:nt], k == 0, k == DC - 1,
                    [zq(k), "ones"], [t2])
        mean, var, rstd, nmr = tmp["mean"], tmp["var"], tmp["rstd"], tmp["nmr"]
        mt = tag + "lnstat"
        self.act(mean[:, 0:nt], PS[ps1][:, 0:nt], AF.Copy, [t1], [mt + "m"], scale=1.0 / D)
        self.tt("dve", var[:, 0:nt], mean[:, 0:nt], mean[:, 0:nt], ALU.mult, [mt + "m"], [mt + "v"])
        self.stt("dve", var[:, 0:nt], PS[ps2][:, 0:nt], 1.0 / D, var[:, 0:nt], ALU.mult, ALU.subtract,
                 [t2, mt + "v"], [mt + "v"])
        self.act(var[:, 0:nt], var[:, 0:nt], AF.Sqrt, [mt + "v"], [mt + "v"], bias=self.epscol[:, 0:1])
        self.S.op("dve", lambda e: e.reciprocal(out=rstd[:, 0:nt], in_=var[:, 0:nt]), [mt + "v"], [mt + "r"])
        self.stt("dve", nmr[:, 0:nt], mean[:, 0:nt], -1.0, rstd[:, 0:nt], ALU.mult, ALU.mult,
                 [mt + "m", mt + "r"], [mt + "n"])
        for k in range(DC):
            eng = "dve" if k % 2 == 0 else "pool"
            self.tt(eng, ZSQ[:, k, 0:nt], Z[:, k, 0:nt], rstd[:, 0:nt], ALU.mult, [ztok, mt + "r"], [zq(k)])
            self.tt(eng, ZSQ[:, k, 0:nt], ZSQ[:, k, 0:nt], nmr[:, 0:nt], ALU.add, [zq(k), mt + "n"],
                    [zq(k)])
            self.act(out_aps[k], ZSQ[:, k, 0:nt], AF.Identity, [zq(k)], [outtoks[k]],
                     bias=bcol[:, k:k + 1], scale=gcol[:, k:k + 1])

    def alloc_ffn(self):
        self.WUP = self.sb("WUP", [128, DC, 2 * DFF], BF16)
        self.WDN = self.sb("WDN", [128, FC, D], BF16)
        self.stage = [self.sb("stage%d" % i, [128, 1408], F32) for i in range(2)]
        NT = 256
        self.f_X1 = [self.sb("fX1_%d" % i, [128, DC, NT], F32) for i in range(2)]
        self.f_H2 = [self.sb("fH2_%d" % i, [128, DC, NT], BF16) for i in range(2)]
        self.f_CVA = [self.sb("fCVA_%d" % i, [128, NT], F32) for i in range(2)]
        self.f_CVG = [self.sb("fCVG_%d" % i, [128, NT], F32) for i in range(2)]
        self.f_ACT = self.sb("fACT", [128, FC, NT], BF16)
        self.f_T = [self.sb("fT_%d" % i, [128, NT], F32) for i in range(2)]
        self.f_Z = self.sb("fZ", [128, DC, NT], F32)
        self.f_ZSQ = self.sb("fZSQ", [128, DC, NT], F32)
        self.lntmp = {n: self.sb("ln_" + n, [128, 256], F32) for n in ("mean", "var", "rstd", "nmr")}

    def ffn_stage(self, l, src, dst_fn, PC, MOD, MOD1, tiles):
        PS = self.PS
        o = PC_OFF
        for it, (t0, nt, rw, mc) in enumerate(tiles):
            pb = it % 2
            X1, H2 = self.f_X1[pb], self.f_H2[pb]
            xtok, htok = "fX1_%d" % pb, "fH2_%d" % pb
            self.load("ld_" + xtok, X1[:, :, 0:nt], src[:, t0:t0 + nt].rearrange("(k p) t -> p k t", p=128),
                      ["xs_%d" % t0], [xtok])
            for k in range(DC):
                self.act(H2[:, k, 0:nt], X1[:, k, 0:nt], AF.Identity, [xtok, "mod"], [htok],
                         bias=MOD[:, 24 + k, mc:mc + 1], scale=MOD1[:, 32 + k, mc:mc + 1])
            nr = nt // rw
            for m in range(FC):
                pp = m % 2
                for half, (mm_, cv, cvt, bank) in enumerate(((m, self.f_CVA[pp], "fCVA_%d" % pp, 0 + pp),
                                                              (m + FC, self.f_CVG[pp], "fCVG_%d" % pp, 2 + pp))):
                    pt = "ps%d" % bank
                    for k in range(DC):
                        self.mm(PS[bank][:, 0:nt], self.WUP[:, k, mm_ * 128:(mm_ + 1) * 128], H2[:, k, 0:nt],
                                k == 0, k == DC - 1, ["WUP", htok], [pt])
                    c0 = o["cw0"][0] + mm_
                    c1 = o["cw1"][0] + mm_
                    c2 = o["cw2"][0] + mm_
                    cb = o["cb"][0] + mm_
                    self.act(cv[:, 0:nt], PS[bank][:, 0:nt], AF.Identity, [pt, "pc"], [cvt],
                             bias=PC[:, cb:cb + 1], scale=PC[:, c1:c1 + 1])
                    pv = PS[bank][:, 0:nt].rearrange("p (r w) -> p r w", w=rw)
                    cvv = cv[:, 0:nt].rearrange("p (r w) -> p r w", w=rw)
                    self.stt("dve", cvv[:, :, 1:rw], pv[:, :, 0:rw - 1], PC[:, c0:c0 + 1], cvv[:, :, 1:rw],
                             ALU.mult, ALU.add, [pt, cvt, "pc"], [cvt])
                    self.stt("dve", cvv[:, :, 0:rw - 1], pv[:, :, 1:rw], PC[:, c2:c2 + 1], cvv[:, :, 0:rw - 1],
                             ALU.mult, ALU.add, [pt, cvt, "pc"], [cvt])
                T = self.f_T[pp]
                self.act(T[:, 0:nt], self.f_CVA[pp][:, 0:nt], AF.Silu, ["fCVA_%d" % pp], ["fT_%d" % pp])
                self.tt("pool", self.f_ACT[:, m, 0:nt], T[:, 0:nt], self.f_CVG[pp][:, 0:nt], ALU.mult,
                        ["fT_%d" % pp, "fCVG_%d" % pp], ["fACT"])
            for mo in range(DC):
                bank = 4 + mo % 2
                pt = "ps%d" % bank
                for k in range(FC):
                    self.mm(PS[bank][:, 0:nt], self.WDN[:, k, mo * 128:(mo + 1) * 128], self.f_ACT[:, k, 0:nt],
                            k == 0, k == FC - 1, ["WDN", "fACT"], [pt])
                T = self.f_T[mo % 2]
                self.act(T[:, 0:nt], PS[bank][:, 0:nt], AF.Copy, [pt, "mod"], ["fT_%d" % (mo % 2)],
                         scale=MOD[:, 40 + mo, mc:mc + 1])
                self.stt("dve", self.f_Z[:, mo, 0:nt], X1[:, mo, 0:nt], ALPHA, T[:, 0:nt], ALU.mult, ALU.add,
                         [xtok, "fT_%d" % (mo % 2)], ["fZ"])
            OUT = X1
            otok = xtok
            self.layer_norm(self.f_Z, self.f_ZSQ, nt, PC[:, o["ln2_g"][0]:o["ln2_g"][0] + 8],
                            PC[:, o["ln2_b"][0]:o["ln2_b"][0] + 8],
                            [OUT[:, k, 0:nt] for k in range(DC)], 6, 7, self.lntmp, "fZ", [otok] * DC, "f", zsqtok="fACT")
            for (dap, wtok) in dst_fn(t0, nt):
                self.load("st_" + otok, dap, OUT[:, :, 0:nt], [otok], [wtok])

    def alloc_mixer(self):
        NT = 512
        self.WIN = self.sb("WIN", [128, DC, 3072], BF16)
        self.WOUT = self.sb("WOUT", [128, DC, D], BF16)
        self.WGLU = self.sb("WGLU", [128, 4, 512], BF16)
        self.stage = [self.sb("stage%d" % i, [128, 1024], F32) for i in range(2)]
        self.m_X = [self.sb("mX_%d" % i, [128, DC, NT], F32) for i in range(2)]
        self.m_H = [self.sb("mH_%d" % i, [128, DC, NT], BF16) for i in range(2)]
        self.OF = self.sb("OF", [128, 4, TT], BF16)
        self.S5O = self.sb("S5O", [128, 4, TT], BF16)
        self.VT = self.sb("VT", [128, 4, 512], BF16)
        names = ("SG", "FF", "LF", "BC", "D2", "EX", "QQ", "KK")
        self.g_t = [{n: self.sb("g%s_%d" % (n, i), [128, NT], F32) for n in names} for i in range(2)]
        self.g_b = [{n: self.sb("g%s_%d" % (n, i), [128, NT], BF16) for n in ("QD1", "QD2", "KD")} for i in range(2)]
        self.g_KDT = [self.sb("gKDT_%d" % i, [128, 4, 128], BF16) for i in range(2)]
        self.g_DEC = [self.sb("gDEC_%d" % i, [128, 16], F32) for i in range(2)]
        self.g_ATT = [self.sb("gATT_%d" % i, [128, 128], BF16) for i in range(2)]
        self.S32 = [self.sb("S32_%d" % h, [128, 128], F32) for h in range(4)]
        self.SBF = [self.sb("SBF_%d" % h, [128, 128], BF16) for h in range(4)]
        self.m_HG = self.sb("mHG", [128, 4, NT], BF16)
        self.m_OS = [self.sb("mOS_%d" % i, [128, NT], F32) for i in range(2)]
        self.m_T = [self.sb("mT_%d" % i, [128, NT], F32) for i in range(2)]
        self.m_Z = self.sb("mZ", [128, DC, NT], F32)
        self.m_ZSQ = self.sb("mZSQ", [128, DC, NT], F32)
        self.lntmp = {n: self.sb("ln_" + n, [128, NT], F32) for n in ("mean", "var", "rstd", "nmr")}

    def modulate_tile(self, X, H, nt, xtok, htok, MOD, MOD1, sh0, sc0, mc, rev):
        for k in range(DC):
            out = H[:, k, nt - 1::-1] if False else H[:, k, 0:nt]
            src = X[:, k, 0:nt]
            if rev:
                src = X[:, k, 0:nt][:, ::-1]
            self.act(out, src, AF.Identity, [xtok, "mod"], [htok],
                     bias=MOD[:, sh0 + k, mc:mc + 1], scale=MOD1[:, sc0 + k, mc:mc + 1])

    def proj_fm(self, bank, H, htok, nt, col0):
        pt = "ps%d" % bank
        for k in range(DC):
            self.mm(self.PS[bank][:, 0:nt], self.WIN[:, k, col0:col0 + 128], H[:, k, 0:nt], k == 0, k == DC - 1,
                    ["WIN", htok], [pt])
        return pt

    def gla_tile(self, H, htok, nt, d, fcol0, LBT, consts, o_sink):
        PS = self.PS
        nb = nt // 128
        nch = nt // 32
        VCOL, QCOL = 1536, 2048
        SEG, GMASK, IDB = consts["seg"], consts["gmask"], consts["identb"]
        for b in range(nb):
            for k in range(DC):
                self.mm(PS[2][:, 0:512], H[:, k, b * 128:(b + 1) * 128], self.WIN[:, k, VCOL:VCOL + 512],
                        k == 0, k == DC - 1, ["WIN", htok], ["ps2"])
            self.copy("act", self.VT[:, b, :], PS[2][:, 0:512], ["ps2"], ["VT"])
        psb = PS[3][:].bitcast(BF16)
        DSB = (5, 2)
        for h0 in (0, 2):
            for h in (h0, h0 + 1):
                hp = h % 2
                T, B = self.g_t[hp], self.g_b[hp]
                tk = lambda n: "g%s_%d" % (n, hp)
                pf = self.proj_fm(0, H, htok, nt, fcol0 + h * 128)
                self.act(T["SG"][:, 0:nt], PS[0][:, 0:nt], AF.Sigmoid, [pf], [tk("SG")])
                pq = self.proj_fm(1, H, htok, nt, QCOL + h * 128)
                self.act(T["QQ"][:, 0:nt], PS[1][:, 0:nt], AF.Silu, [pq], [tk("QQ")])
                lbc = d * 4 + h
                self.ts("dve", T["FF"][:, 0:nt], T["SG"][:, 0:nt], LBT["oml"][:, lbc:lbc + 1], LBT["lb"][:, lbc:lbc + 1],
                        ALU.mult, ALU.add, [tk("SG"), "lbt"], [tk("FF")])
                self.act(T["LF"][:, 0:nt], T["FF"][:, 0:nt], AF.Ln, [tk("FF")], [tk("LF")])
                self.S.op("dve", lambda e, T=T: e.tensor_tensor_scan(out=T["BC"][:, 0:nt], data0=SEG[:, 0:nt],
                                                                     data1=T["LF"][:, 0:nt], initial=0.0,
                                                                     op0=ALU.mult, op1=ALU.add),
                          [tk("LF"), "consts"], [tk("BC")])
                self.ts("pool", T["KK"][:, 0:nt], T["FF"][:, 0:nt], -1.0, 1.0, ALU.mult, ALU.add, [tk("FF")], [tk("KK")])
                bc3 = T["BC"][:, 0:nt].rearrange("p (c w) -> p c w", w=32)
                d23 = T["D2"][:, 0:nt].rearrange("p (c w) -> p c w", w=32)
                self.tt("dve", d23, bc3, bc3[:, :, 31:32].to_broadcast([128, nch, 32]), ALU.subtract, [tk("BC")], [tk("D2")])
                DEC = self.g_DEC[hp]
                self.act(DEC[:, 0:nch], T["BC"][:, 31:nt:32], AF.Exp, [tk("BC")], ["gDEC_%d" % hp])
                self.act(T["EX"][:, 0:nt], T["D2"][:, 0:nt], AF.Exp, [tk("D2")], [tk("EX")])
                self.tt("dve", B["QD2"][:, 0:nt], T["QQ"][:, 0:nt], T["EX"][:, 0:nt], ALU.mult, [tk("QQ"), tk("EX")], [tk("QD2")])
                self.act(T["EX"][:, 0:nt], T["D2"][:, 0:nt], AF.Exp, [tk("D2")], [tk("EX")], scale=-1.0)
                self.tt("pool", B["KD"][:, 0:nt], T["KK"][:, 0:nt], T["EX"][:, 0:nt], ALU.mult, [tk("KK"), tk("EX")], [tk("KD")])
                self.act(T["EX"][:, 0:nt], T["BC"][:, 0:nt], AF.Exp, [tk("BC")], [tk("EX")])
                self.tt("dve", B["QD1"][:, 0:nt], T["QQ"][:, 0:nt], T["EX"][:, 0:nt], ALU.mult, [tk("QQ"), tk("EX")], [tk("QD1")])
            for b in range(nb):
                bs = slice(b * 128, (b + 1) * 128)
                for h in (h0, h0 + 1):
                    hp = h % 2
                    B = self.g_b[hp]
                    tk = lambda n: "g%s_%d" % (n, hp)
                    KDT = self.g_KDT[hp]
                    self.S.op("pe", lambda e, bs=bs, B=B: e.transpose(psb[:, 0:128], B["KD"][:, bs], IDB[:]),
                              [tk("KD"), "consts"], ["ps3"])
                    self.copy("act", KDT[:, b, :], psb[:, 0:128], ["ps3"], ["gKDT_%d" % hp])
                    KDZ = self.g_KDZ[hp]
                    self.ts("dve", KDZ[64:128, b, :], psb[64:128, 0:128], consts["m96"][64:128, 0:1], None, ALU.mult, None,
                            ["ps3", "consts"], ["gKDT_%d" % hp])
                    self.mm(PS[4][:, 0:128], B["KD"][:, bs], B["QD2"][:, bs], True, True, [tk("KD"), tk("QD2")], ["ps4"])
                    ATT = self.g_ATT[hp]
                    self.tt("dve", ATT[:], PS[4][:, 0:128], GMASK[:], ALU.mult, ["ps4", "consts"], ["gATT_%d" % hp])
                for ci in range(4):
                    cs = slice(b * 128 + ci * 32, b * 128 + ci * 32 + 32)
                    rs = slice(ci * 32, ci * 32 + 32)
                    cc = b * 4 + ci
                    for h in (h0, h0 + 1):
                        hp = h % 2
                        B = self.g_b[hp]
                        tk = lambda n: "g%s_%d" % (n, hp)
                        ob, ot = 6 + hp, "ps%d" % (6 + hp)
                        db, dt_ = DSB[hp], "ps%d" % DSB[hp]
                        self.mm(PS[ob][:, cs], self.SBF[h][:], B["QD1"][:, cs], b == 0 and ci == 0, False,
                                ["SBF_%d" % h, tk("QD1")], [ot])
                        if ci == 3:
                            r64 = slice(64, 128)
                            self.mm(PS[db][:, 0:128], self.g_KDZ[hp][r64, b, :], self.VT[r64, b, h * 128:(h + 1) * 128],
                                    True, True, ["gKDT_%d" % hp, "VT"], [dt_])
                        else:
                            self.mm(PS[db][:, 0:128], self.g_KDT[hp][rs, b, :], self.VT[rs, b, h * 128:(h + 1) * 128],
                                    True, True, ["gKDT_%d" % hp, "VT"], [dt_])
                        self.stt("dve", self.S32[h][:], self.S32[h][:], self.g_DEC[hp][:, cc:cc + 1], PS[db][:, 0:128],
                                 ALU.mult, ALU.add, ["S32_%d" % h, "gDEC_%d" % hp, dt_], ["S32_%d" % h])
                        self.copy("act", self.SBF[h][:], self.S32[h][:], ["S32_%d" % h], ["SBF_%d" % h])
                for h in (h0, h0 + 1):
                    hp = h % 2
                    self.mm(PS[6 + hp][:, bs], self.VT[:, b, h * 128:(h + 1) * 128], self.g_ATT[hp][:], False, b == nb - 1,
                            ["VT", "gATT_%d" % hp], ["ps%d" % (6 + hp)])
            for h in (h0, h0 + 1):
                o_sink(h, 6 + h % 2, "ps%d" % (6 + h % 2))

    NCH = TT // 8

    def alloc_s5(self):
        NCH = self.NCH
        self.UP = self.sb("UP", [128, 4, 8, NCH], BF16)
        self.YV = self.UP[:].rearrange("p o j c -> p (o j) c")
        self.V = self.sb("Vs5", [128, 32, NCH], BF16)
        self.FF = self.sb("FFs5", [128, 2, 2, 16, NCH], BF16)
        self.WZT = self.sb("WZT", [128, 32, 2, 128], BF16)
        self.KTOE = self.sb("KTOE", [128, 32, 128], BF16)
        self.TCA = self.sb("TCA", [128, 32, 2, 128], BF16)
        self.SELB = self.sb("SELB", [128, 64, 128], BF16)
        self.GY = self.sb("GY", [128, 4, TT], BF16)
        self.S5P = self.sb("S5P", [128, 2144], F32)
        sm = lambda n, w: self.sb("s5_" + n, [128, w], F32)
        self.p_ = {n: sm(n, 32) for n in ("DT", "LRDT", "ANG", "DEN", "NR", "CRE", "CIM", "T1", "T2", "RHO", "PHI")}
        self.p_.update({n: sm(n, 512) for n in ("KL", "KA", "MAGK", "SINK", "COSK", "PRE", "PIM", "BBR", "BBI", "X1", "X2")})
        self.p_.update({n: sm(n, 2048) for n in ("TA", "TB", "TC")})
        self.TWZ = self.sb("TWZ", [128, 16, 2, 128], BF16)
        self.TM2 = self.sb("TM2", [128, 16, 2, 128], BF16)
        self.l2 = {n: self.sb("l2_" + n, [128, NCH], F32) for n in
                   ("ZR", "ZI", "CS", "SN", "XA", "A", "B", "GR", "GI", "ER", "EI")}
        self.EFIN = self.sb("EFIN", [128, 2, 16], F32)
        self.PST = self.sb("PST", [128, 2, 16], F32)
        self.APT = self.sb("APT", [128, 2, 16], F32)
        self.KS = self.sb("KS", [128, 2, 128], F32)

    def s5_prep(self, l, s5p_dram, PC, consts):
        P = self.p_
        PS = self.PS
        S5P = self.S5P
        KV = consts["kv"]
        self.load("ld_s5p", S5P[:], s5p_dram, [], ["s5p"])
        LR, LI, LOGDT = S5P[:, 0:32], S5P[:, 32:64], S5P[:, 64:96]
        v3 = lambda a: a.rearrange("p (a h) -> p a h", h=16)
        BRE, BIM = v3(S5P[:, 96:608]), v3(S5P[:, 608:1120])
        CRE_, CIM_ = v3(S5P[:, 1120:1632]), v3(S5P[:, 1632:2144])
        tkn = lambda n: "s5_" + n
        PI = float(np.pi)
        self.act(P["DT"][:], LOGDT, AF.Exp, ["s5p"], [tkn("DT")])
        self.tt("dve", P["LRDT"][:], LR, P["DT"][:], ALU.mult, ["s5p", tkn("DT")], [tkn("LRDT")])
        self.tt("dve", P["ANG"][:], LI, P["DT"][:], ALU.mult, ["s5p", tkn("DT")], [tkn("ANG")])
        b16 = lambda a: a.unsqueeze(2).to_broadcast([128, 32, 16])
        kvb = KV[:, 0:16].unsqueeze(1).to_broadcast([128, 32, 16])
        self.tt("dve", v3(P["KL"][:]), b16(P["LRDT"][:]), kvb, ALU.mult, [tkn("LRDT"), "consts"], [tkn("KL")])
        self.tt("dve", v3(P["KA"][:]), b16(P["ANG"][:]), kvb, ALU.mult, [tkn("ANG"), "consts"], [tkn("KA")])
        self.act(P["MAGK"][:], P["KL"][:], AF.Exp, [tkn("KL")], [tkn("MAGK")])
        I2P = 1.0 / (2.0 * PI)
        self.ts("dve", P["X1"][:], P["KA"][:], I2P, None, ALU.mult, None, [tkn("KA")], [tkn("X1")])
        self.sin_turns(P["SINK"][:], P["X1"][:], self.XI[:], P["X2"][:], None, tkn("X1"), tkn("SINK"))
        self.ts("dve", P["X1"][:], P["KA"][:], I2P, 0.25, ALU.mult, ALU.add, [tkn("KA")], [tkn("X1")])
        self.sin_turns(P["COSK"][:], P["X1"][:], self.XI[:], P["X2"][:], None, tkn("X1"), tkn("COSK"))
        self.tt("dve", P["PRE"][:], P["MAGK"][:], P["COSK"][:], ALU.mult, [tkn("MAGK"), tkn("COSK")], [tkn("PRE")])
        self.tt("dve", P["PIM"][:], P["MAGK"][:], P["SINK"][:], ALU.mult, [tkn("MAGK"), tkn("SINK")], [tkn("PIM")])
        PRE3, PIM3 = v3(P["PRE"][:]), v3(P["PIM"][:])
        AR, AI = PRE3[:, :, 8], PIM3[:, :, 8]
        self.copy("dve", P["RHO"][:], v3(P["MAGK"][:])[:, :, 15], [tkn("MAGK")], [tkn("RHO")])
        self.ts("dve", P["PHI"][:], P["ANG"][:], 8.0 * I2P, None, ALU.mult, None, [tkn("ANG")], [tkn("PHI")])
        self.copy("dve", self.XI[:, 0:32], P["PHI"][:], [tkn("PHI")], ["s5_XI"])
        self.copy("dve", P["T1"][:], self.XI[:, 0:32], ["s5_XI"], [tkn("T1")])
        self.tt("dve", P["PHI"][:], P["PHI"][:], P["T1"][:], ALU.subtract, [tkn("PHI"), tkn("T1")], [tkn("PHI")])
        self.tt("dve", P["DEN"][:], LR, LR, ALU.mult, ["s5p"], [tkn("DEN")])
        self.tt("dve", P["T1"][:], LI, LI, ALU.mult, ["s5p"], [tkn("T1")])
        self.tt("dve", P["DEN"][:], P["DEN"][:], P["T1"][:], ALU.add, [tkn("DEN"), tkn("T1")], [tkn("DEN")])
        self.S.op("dve", lambda e: e.reciprocal(out=P["DEN"][:], in_=P["DEN"][:]), [tkn("DEN")], [tkn("DEN")])
        self.ts("dve", P["NR"][:], AR, -1.0, None, ALU.add, None, [tkn("PRE")], [tkn("NR")])
        self.tt("dve", P["T1"][:], P["NR"][:], LR, ALU.mult, [tkn("NR"), "s5p"], [tkn("T1")])
        self.tt("dve", P["T2"][:], AI, LI, ALU.mult, [tkn("PIM"), "s5p"], [tkn("T2")])
        self.tt("dve", P["T1"][:], P["T1"][:], P["T2"][:], ALU.add, [tkn("T1"), tkn("T2")], [tkn("T1")])
        self.tt("dve", P["CRE"][:], P["T1"][:], P["DEN"][:], ALU.mult, [tkn("T1"), tkn("DEN")], [tkn("CRE")])
        self.tt("dve", P["T1"][:], AI, LR, ALU.mult, [tkn("PIM"), "s5p"], [tkn("T1")])
        self.tt("dve", P["T2"][:], P["NR"][:], LI, ALU.mult, [tkn("NR"), "s5p"], [tkn("T2")])
        self.tt("dve", P["T1"][:], P["T1"][:], P["T2"][:], ALU.subtract, [tkn("T1"), tkn("T2")], [tkn("T1")])
        self.tt("dve", P["CIM"][:], P["T1"][:], P["DEN"][:], ALU.mult, [tkn("T1"), tkn("DEN")], [tkn("CIM")])
        cre_b, cim_b = b16(P["CRE"][:]), b16(P["CIM"][:])
        ta, tb = v3(P["TA"][:, 0:512]), v3(P["TB"][:, 0:512])
        self.tt("dve", ta, cre_b, BRE, ALU.mult, [tkn("CRE"), "s5p"], [tkn("TA")])
        self.tt("dve", tb, cim_b, BIM, ALU.mult, [tkn("CIM"), "s5p"], [tkn("TB")])
        self.tt("dve", v3(P["BBR"][:]), ta, tb, ALU.subtract, [tkn("TA"), tkn("TB")], [tkn("BBR")])
        self.tt("dve", ta, cre_b, BIM, ALU.mult, [tkn("CRE"), "s5p"], [tkn("TA")])
        self.tt("dve", tb, cim_b, BRE, ALU.mult, [tkn("CIM"), "s5p"], [tkn("TB")])
        self.tt("dve", v3(P["BBI"][:]), ta, tb, ALU.add, [tkn("TA"), tkn("TB")], [tkn("BBI")])
        BBR3, BBI3 = v3(P["BBR"][:]), v3(P["BBI"][:])

        def cplx_table(dst, dtok, d, ksl, XR, XI, neg_im):
            ds_ = slice(d * 16, d * 16 + 16)
            pr_ = PRE3[:, ds_, ksl].unsqueeze(3).to_broadcast([128, 16, 8, 16])
            pi_ = PIM3[:, ds_, ksl].unsqueeze(3).to_broadcast([128, 16, 8, 16])
            xr_ = XR[:, ds_, :].unsqueeze(2).to_broadcast([128, 16, 8, 16])
            xi_ = XI[:, ds_, :].unsqueeze(2).to_broadcast([128, 16, 8, 16])
            v4 = lambda a: a.rearrange("p (a j h) -> p a j h", j=8, h=16)
            A_, B_, C_ = v4(P["TA"][:]), v4(P["TB"][:]), v4(P["TC"][:])
            rd = [tkn("PRE"), tkn("PIM"), tkn("BBR"), tkn("BBI"), "s5p"]
            dre = dst[:, :, 0, :].rearrange("p a (j h) -> p a j h", h=16)
            dim = dst[:, :, 1, :].rearrange("p a (j h) -> p a j h", h=16)
            self.tt("dve", A_, pr_, xr_, ALU.mult, rd, [tkn("TA")])
            self.tt("pool", B_, pi_, xi_, ALU.mult, rd, [tkn("TB")])
            self.tt("dve", dre, A_, B_, ALU.subtract, [tkn("TA"), tkn("TB")], [dtok])
            self.tt("dve", A_, pr_, xi_, ALU.mult, rd, [tkn("TA")])
            self.tt("pool", B_, pi_, xr_, ALU.mult, rd, [tkn("TB")])
            if neg_im:
                self.stt("dve", dim, A_, -1.0, B_, ALU.mult, ALU.subtract, [tkn("TA"), tkn("TB")], [dtok])
            else:
                self.tt("dve", dim, A_, B_, ALU.add, [tkn("TA"), tkn("TB")], [dtok])

        IDB = consts["identb"]
        for d in range(2):
            wz_sl = slice(14, 6, -1) if d == 0 else slice(7, 15)
            m2_sl = slice(0, 8) if d == 0 else slice(7, None, -1)
            ca_sl = slice(8, 16) if d == 0 else slice(15, 7, -1)
            cplx_table(self.TWZ, "TWZ", d, wz_sl, BBR3, BBI3, False)
            cplx_table(self.TM2, "TM2", d, m2_sl, CRE_, CIM_, True)
            cplx_table(self.TCA[:, d * 16:(d + 1) * 16], "TCA", d, ca_sl, CRE_, CIM_, True)
            psb = PS[3][:].bitcast(BF16)
            for pr in range(16):
                for ri in range(2):
                    self.S.op("pe", lambda e, pr=pr, ri=ri: e.transpose(psb[:, 0:128], self.TWZ[:, pr, ri, :], IDB[:]),
                              ["TWZ", "consts"], ["ps3"])
                    self.copy("act", self.WZT[:, d * 16 + pr, ri, :], psb[:, 0:128], ["ps3"], ["WZT"])
                for g2 in range(2):
                    rs = slice(g2 * 64, g2 * 64 + 64)
                    g = 2 * pr + g2
                    bank = 4 + (g % 2)
                    self.mm(PS[bank][:, 0:128], self.TWZ[rs, pr, 0, :], self.TM2[rs, pr, 0, :], True, False,
                            ["TWZ", "TM2"], ["ps%d" % bank])
                    self.mm(PS[bank][:, 0:128], self.TWZ[rs, pr, 1, :], self.TM2[rs, pr, 1, :], False, True,
                            ["TWZ", "TM2"], ["ps%d" % bank])
                    msk = consts["maskF"] if d == 0 else consts["maskB"]
                    if d == 0:
                        self.tt("dve", self.KS[:, g % 2, :], PS[bank][:, 0:128], msk[:], ALU.mult,
                                ["ps%d" % bank, "consts"], ["KS%d" % (g % 2)])
                        self.copy("act", self.KTOE[:, g, :], self.KS[:, g % 2, :], ["KS%d" % (g % 2)], ["KTOE%d" % g])
                    else:
                        self.tt("dve", self.KS[:, g % 2, :], PS[bank][:, 0:128], msk[:], ALU.mult,
                                ["ps%d" % bank, "consts"], ["KS%d" % (g % 2)])
                        self.tt("dve", self.KS[:, g % 2, :], self.KS[:, g % 2, :], self.KTOE[:, g, :], ALU.add,
                                ["KS%d" % (g % 2), "KTOE%d" % g], ["KS%d" % (g % 2)])
                        self.stt("dve", self.KTOE[:, g, :], consts["eye"][:], consts_dcol(self, PC, g), self.KS[:, g % 2, :],
                                 ALU.mult, ALU.add, ["KS%d" % (g % 2), "consts", "pcd"], ["KTOE%d" % g])


def consts_dcol(self, PC, g):
    return self.DSK[:, g:g + 1]


def _s5_main(self, l, PC, consts, exchange_fn):
    PS = self.PS
    NCH = self.NCH
    P = self.p_
    L2 = self.l2
    PI = float(np.pi)
    MIDX = consts["midx"]
    for g in range(32):
        oc, gi = g // 8, g % 8
        bank = g % 2
        for j in range(8):
            self.mm(PS[bank][:, 0:NCH], self.SELB[:, gi * 8 + j, :], self.UP[:, oc, j, :], j == 0, j == 7,
                    ["SELB", "UP"], ["ps%d" % bank])
        self.copy("act", self.V[:, g, :], PS[bank][:, 0:NCH], ["ps%d" % bank], ["V"])

    def level2(d, pr, segs):
        dp = d * 16 + pr
        for ri, bank in ((0, 2), (1, 3)):
            for g2 in range(2):
                self.mm(PS[bank][g2 * 64:(g2 + 1) * 64, 0:NCH], self.WZT[:, dp, ri, g2 * 64:(g2 + 1) * 64],
                        self.V[:, 2 * pr + g2, :], True, True, ["WZT", "V"], ["ps%d" % bank])
        self.copy("act", L2["ZR"][:], PS[2][:, 0:NCH], ["ps2"], ["l2ZR"])
        self.copy("act", L2["ZI"][:], PS[3][:, 0:NCH], ["ps3"], ["l2ZI"])
        phi = P["PHI"][:, dp:dp + 1]
        rho = P["RHO"][:, dp:dp + 1]
        for (zs, n, init, fsl, fin) in segs:
            ZR, ZI = L2["ZR"][:, zs], L2["ZI"][:, zs]
            if init == "P":
                self.tt("dve", ZR[:, 0:1], ZR[:, 0:1], self.APT[:, 0, pr:pr + 1], ALU.add, ["l2ZR", "APT"], ["l2ZR"])
                self.tt("dve", ZI[:, 0:1], ZI[:, 0:1], self.APT[:, 1, pr:pr + 1], ALU.add, ["l2ZI", "APT"], ["l2ZI"])
            self.ts("dve", L2["XA"][:, 0:n], MIDX[:, 0:n], phi, None, ALU.mult, None, ["consts", "s5_PHI"], ["l2XA"])
            self.sin_turns(L2["SN"][:, 0:n], L2["XA"][:, 0:n], self.l2I[:, 0:n], L2["TN"][:, 0:n], None, "l2XA", "l2SN")
            self.ts("dve", L2["XA"][:, 0:n], MIDX[:, 0:n], phi, 0.25, ALU.mult, ALU.add, ["consts", "s5_PHI"], ["l2XA"])
            self.sin_turns(L2["CS"][:, 0:n], L2["XA"][:, 0:n], self.l2I[:, 0:n], L2["TN"][:, 0:n], None, "l2XA", "l2CS")
            CS, SN = L2["CS"][:, 0:n], L2["SN"][:, 0:n]
            A, B, GR, GI = L2["A"][:, 0:n], L2["B"][:, 0:n], L2["GR"][:, 0:n], L2["GI"][:, 0:n]
            self.tt("dve", A, CS, ZR, ALU.mult, ["l2CS", "l2ZR"], ["l2A"])
            self.tt("pool", B, SN, ZI, ALU.mult, ["l2SN", "l2ZI"], ["l2B"])
            self.tt("dve", GR, A, B, ALU.add, ["l2A", "l2B"], ["l2GR"])
            self.tt("dve", A, CS, ZI, ALU.mult, ["l2CS", "l2ZI"], ["l2A"])
            self.tt("pool", B, SN, ZR, ALU.mult, ["l2SN", "l2ZR"], ["l2B"])
            self.tt("dve", GI, A, B, ALU.subtract, ["l2A", "l2B"], ["l2GI"])
            rb = rho.to_broadcast([128, n])
            self.S.op("dve", lambda e, GR=GR, rb=rb: e.tensor_tensor_scan(out=GR, data0=rb, data1=GR, initial=0.0,
                                                                         op0=ALU.mult, op1=ALU.add),
                      ["l2GR", "s5_RHO"], ["l2GR"])
            self.S.op("dve", lambda e, GI=GI, rb=rb: e.tensor_tensor_scan(out=GI, data0=rb, data1=GI, initial=0.0,
                                                                         op0=ALU.mult, op1=ALU.add),
                      ["l2GI", "s5_RHO"], ["l2GI"])
            ER, EI = L2["ER"][:, 0:n], L2["EI"][:, 0:n]
            self.tt("dve", A, CS, GR, ALU.mult, ["l2CS", "l2GR"], ["l2A"])
            self.tt("pool", B, SN, GI, ALU.mult, ["l2SN", "l2GI"], ["l2B"])
            self.tt("dve", ER, A, B, ALU.subtract, ["l2A", "l2B"], ["l2ER"])
            self.tt("dve", A, SN, GR, ALU.mult, ["l2SN", "l2GR"], ["l2A"])
            self.tt("pool", B, CS, GI, ALU.mult, ["l2CS", "l2GI"], ["l2B"])
            self.tt("dve", EI, A, B, ALU.add, ["l2A", "l2B"], ["l2EI"])
            FR = self.FF[:, d, 0, pr, :][:, fsl]
            FI = self.FF[:, d, 1, pr, :][:, fsl]
            self.copy("act", FR[:, 1:n], ER[:, 0:n - 1], ["l2ER"], ["FF"])
            self.copy("act", FI[:, 1:n], EI[:, 0:n - 1], ["l2EI"], ["FF"])
            if init == "P":
                self.copy("act", FR[:, 0:1], self.PST[:, 0, pr:pr + 1], ["PST"], ["FF"])
                self.copy("act", FI[:, 0:1], self.PST[:, 1, pr:pr + 1], ["PST"], ["FF"])
            if fin:
                self.copy("act", self.EFIN[:, 0, pr:pr + 1], ER[:, n - 1:n], ["l2ER"], ["EFIN"])
                self.copy("act", self.EFIN[:, 1, pr:pr + 1], EI[:, n - 1:n], ["l2EI"], ["EFIN"])

    self.S.op("pool", lambda e: e.memset(self.FF[:], 0.0), (), ["FF"])
    for pr in range(16):
        level2(0, pr, [(slice(0, NCH), NCH, None, slice(0, NCH), True)])
    exchange_fn()
    v3 = lambda a: a.rearrange("p (a h) -> p a h", h=16)
    ARb, AIb = v3(P["PRE"][:])[:, 16:32, 15], v3(P["PIM"][:])[:, 16:32, 15]
    T1, T2 = P["T1"][:, 0:16], P["T2"][:, 0:16]
    self.tt("dve", T1, ARb, self.PST[:, 0, :], ALU.mult, ["s5_PRE", "PST"], ["s5_T1"])
    self.tt("dve", T2, AIb, self.PST[:, 1, :], ALU.mult, ["s5_PIM", "PST"], ["s5_T2"])
    self.tt("dve", self.APT[:, 0, :], T1, T2, ALU.subtract, ["s5_T1", "s5_T2"], ["APT"])
    self.tt("dve", T1, ARb, self.PST[:, 1, :], ALU.mult, ["s5_PRE", "PST"], ["s5_T1"])
    self.tt("dve", T2, AIb, self.PST[:, 0, :], ALU.mult, ["s5_PIM", "PST"], ["s5_T2"])
    self.tt("dve", self.APT[:, 1, :], T1, T2, ALU.add, ["s5_T1", "s5_T2"], ["APT"])
    NCC = NCTX // 8
    for pr in range(16):
        level2(1, pr, [(slice(NCH - 1, NCC - 1, -1), NCH - NCC, "P", slice(NCH - 1, NCC - 1, -1), False),
                       (slice(NCC - 1, None, -1), NCC, None, slice(NCC - 1, None, -1), False)])
    for g in range(32):
        pr, g2 = g // 2, g % 2
        rs = slice(g2 * 64, g2 * 64 + 64)
        bank = 4 + g % 2
        pt = "ps%d" % bank
        self.mm(PS[bank][:, 0:NCH], self.KTOE[:, g, :], self.V[:, g, :], True, False, ["KTOE%d" % g, "V"], [pt])
        for d in range(2):
            for ri in range(2):
                self.mm(PS[bank][:, 0:NCH], self.TCA[rs, d * 16 + pr, ri, :], self.FF[rs, d, ri, pr, :], False,
                        d == 1 and ri == 1, ["TCA", "FF"], [pt])
        self.copy("act", self.YV[:, g, :], PS[bank][:, 0:NCH], [pt], ["UP"])


Model.s5_main = _s5_main


def _exchange(self, name, src_ap, ncols, dst_sb, groups, outtok):
    self.uid += 1
    u = self.uid
    dsrc = self.scratch("xsrc%d" % u, [128, ncols])
    ddst = self.scratch("xdst%d" % u, [256, ncols])
    both = self.sb("xboth%d" % u, [128, 2, ncols], F32)
    t = "x%d" % u
    self.load("xs%d" % u, dsrc, src_ap, [name], [t + "a"], eng="pool")
    self.S.dma("pool", "xc%d" % u,
               lambda e, s: e.collective_compute("AllGather", ALU.bypass, replica_groups=groups, ins=[dsrc],
                                                 outs=[ddst]).then_inc(s, 1),
               [t + "a"], [t + "b"], inc=1)
    self.load("xl%d" % u, both[:], ddst.rearrange("(k p) f -> p k f", p=128), [t + "b"], [t + "c"], eng="pool")
    PSEL = self.consts["pairsel"]
    self.ts("dve", dst_sb, both[:, 0, :], PSEL[:, 0:1], None, ALU.mult, None, [t + "c", "consts"], [outtok])
    self.stt("dve", dst_sb, both[:, 1, :], PSEL[:, 1:2], dst_sb, ALU.mult, ALU.add, [t + "c", "consts", outtok],
             [outtok])


Model.exchange = _exchange


def _load_consts(self, cd):
    C = {}
    for n, w in (("seg", 512), ("gmask", 128), ("eye", 128), ("maskF", 128), ("maskB", 128), ("kv", 16),
                 ("midx", 288), ("negpi", 1), ("pairsel", 2), ("m96", 1)):
        C[n] = self.sb("c_" + n, [128, w], F32)
        self.load("ld_c_" + n, C[n][:], cd[n], [], ["consts"])
    C["identb"] = self.sb("c_identb", [128, 128], BF16)
    self.copy("dve", C["identb"][:], C["eye"][:], ["consts"], ["consts"])
    self.consts = C
    return C


Model.load_consts = _load_consts


def _mod_layer(self, l, w_mod, PC, SC, MOD, MOD1, stage):
    PS = self.PS
    CB = 1024
    i = 0
    for cb in range(6):
        bank = cb % 2
        for k in range(DC):
            st = stage[i % 2]
            tok = "stage%d" % (i % 2)
            self.load("ld_" + tok, st[:, 0:CB], w_mod[l, k * 128:(k + 1) * 128, cb * CB:(cb + 1) * CB], [], [tok])
            for mt in range(8):
                self.mm(PS[bank][:, mt * 2:mt * 2 + 2], st[:, mt * 128:(mt + 1) * 128], SC[:, k, :], k == 0 and mt == 0,
                        k == DC - 1 and mt == 7, [tok, "SC"], ["ps%d" % bank])
            i += 1
        pv = PS[bank][:, 0:16].rearrange("p (m c) -> p m c", c=2)
        bm = PC[:, cb * 8:(cb + 1) * 8].unsqueeze(2).to_broadcast([128, 8, 2])
        self.tt("dve", MOD[:, cb * 8:(cb + 1) * 8, :], pv, bm, ALU.add, ["ps%d" % bank, "pc"], ["mod0"])
    self.ts("dve", MOD1[:], MOD[:], 1.0, None, ALU.add, None, ["mod0"], ["mod"])


Model.mod_layer = _mod_layer


def _load_win(self, l, w_in, WIN, c0, c1, stage):
    i = 0
    for k in range(DC):
        for cc in range(c0, c1, 1024):
            cw = min(1024, c1 - cc)
            st = stage[i % 2]
            tok = "stage%d" % (i % 2)
            self.load("ld_" + tok, st[:, 0:cw], w_in[l, k * 128:(k + 1) * 128, cc:cc + cw], [], [tok])
            self.copy(("pool", "act", "dve")[i % 3], WIN[:, k, cc:cc + cw], st[:, 0:cw], [tok], ["WIN"])
            i += 1


Model.load_win = _load_win

TILES = [(0, 256, 256, 1)] + [(256 + 256 * i, 256, 64, 0) for i in range(8)]


def _stage0(self, l, xs, MOD, MOD1):
    X, H = self.m_X[0], self.m_H
    for it, (t0, nt, rw, mc) in enumerate(TILES):
        hb = it % 2
        self.load("ld_mX", X[:, :, 0:nt], xs[:, t0:t0 + nt].rearrange("(k p) t -> p k t", p=128), ["xs"], ["mX"])
        self.modulate_tile(X, H[hb], nt, "mX", "mH%d" % hb, MOD, MOD1, 0, 8, mc, False)
        c0, ncn = t0 // 8, nt // 8
        for oc in range(4):
            bank = oc % 2
            pt = self.proj_fm(bank, H[hb], "mH%d" % hb, nt, oc * 128)
            self.copy("act", self.UP[:, oc, :, c0:c0 + ncn].rearrange("p j c -> p c j"),
                      self.PS[bank][:, 0:nt].rearrange("p (c j) -> p c j", j=8), [pt], ["UP"])


Model.stage0 = _stage0


def _s5_out(self, l, PC):
    PS = self.PS
    NCH = self.NCH
    GY = self.GY
    for oc in range(4):
        for j in range(8):
            bank = j % 2
            for gi in range(8):
                self.mm(PS[bank][:, 0:NCH], self.SELB[:, gi * 8 + j, :], self.YV[:, oc * 8 + gi, :], gi == 0, gi == 7,
                        ["SELB", "UP"], ["ps%d" % bank])
            self.act(GY[:, oc, j:TT:8], PS[bank][:, 0:NCH], AF.Gelu, ["ps%d" % bank], ["GY"])
    bg = PC_OFF["bglu"][0]
    for (t0, nt, rw, mc) in TILES:
        for mo in range(4):
            bank = 2 + mo % 2
            for k in range(4):
                self.mm(PS[bank][:, 0:nt], self.WGLU[:, k, mo * 128:(mo + 1) * 128], GY[:, k, t0:t0 + nt], k == 0, k == 3,
                        ["WGLU", "GY"], ["ps%d" % bank])
            T = self.s_T[mo % 2]
            self.act(T[:, 0:nt], PS[bank][:, 0:nt], AF.Sigmoid, ["ps%d" % bank, "pc"], ["sT%d" % (mo % 2)],
                     bias=PC[:, bg + mo:bg + mo + 1])
            self.tt("dve", self.S5O[:, mo, t0:t0 + nt], GY[:, mo, t0:t0 + nt], T[:, 0:nt], ALU.mult,
                    ["GY", "sT%d" % (mo % 2)], ["S5O"])


Model.s5_out = _s5_out


def _stage1(self, l, xs, MOD, MOD1, LBT):
    X, H = self.m_X[0], self.m_H
    for h in range(4):
        self.S.op("pool", lambda e, h=h: e.memset(self.S32[h][:], 0.0), (), ["S32_%d" % h])
        self.S.op("pool", lambda e, h=h: e.memset(self.SBF[h][:], 0.0), (), ["SBF_%d" % h])
    for it, (t0, nt, rw, mc) in enumerate(TILES):
        hb = it % 2
        self.load("ld_mX", X[:, :, 0:nt], xs[:, t0:t0 + nt].rearrange("(k p) t -> p k t", p=128), ["xs"], ["mX"])
        self.modulate_tile(X, H[hb], nt, "mX", "mH%d" % hb, MOD, MOD1, 0, 8, mc, False)

        def sink(h, ob, ot, t0=t0, nt=nt):
            self.copy("act", self.OF[:, h, t0:t0 + nt], self.PS[ob][:, 0:nt], [ot], ["OF"])
        self.gla_tile(H[hb], "mH%d" % hb, nt, 0, 512, LBT, self.consts, sink)


Model.stage1 = _stage1


def _stage3(self, l, xs, x1dst, MOD, MOD1, LBT, PC, last):
    PS = self.PS
    X, H = self.m_X[0], self.m_H
    o = PC_OFF
    hgn = PC[:, o["hgn"][0]:o["hgn"][0] + 1]
    order = list(range(len(TILES) - 1, 0, -1)) + [0]
    for ii, it in enumerate(order):
        (t0, nt, rw, mc) = TILES[it]
        hb = ii % 2
        if it == 0:
            for h in range(4):
                self.S.op("pool", lambda e, h=h: e.memset(self.S32[h][:], 0.0), (), ["S32_%d" % h])
                self.S.op("pool", lambda e, h=h: e.memset(self.SBF[h][:], 0.0), (), ["SBF_%d" % h])
        self.load("ld_mX", X[:, :, 0:nt], xs[:, t0:t0 + nt].rearrange("(k p) t -> p k t", p=128), ["xs"], ["mX"])
        self.modulate_tile(X, H[hb], nt, "mX", "mH%d" % hb, MOD, MOD1, 0, 8, mc, True)
        htok = "mH%d" % hb

        def sink(h, ob, ot, t0=t0, nt=nt, H=H[hb], htok=htok):
            OS = self.m_OS[h % 2]
            ost = "mOS%d" % (h % 2)
            self.tt("dve", OS[:, 0:nt], PS[ob][:, 0:nt], self.OF[:, h, t0:t0 + nt][:, ::-1], ALU.add, [ot, "OF"], [ost])
            T = self.m_T[h % 2]
            tt_ = "mT%d" % (h % 2)
            self.act(T[:, 0:nt], OS[:, 0:nt], AF.Square, [ost], [tt_])
            self.mm(PS[2][:, 0:nt], self.ones[:], T[:, 0:nt], True, True, [tt_, "ones"], ["ps2"])
            self.act(T[:, 0:nt], PS[2][:, 0:nt], AF.Sqrt, ["ps2"], [tt_], bias=self.epscol[:, 1:2], scale=1.0 / 128)
            self.S.op("dve", lambda e, T=T: e.reciprocal(out=T[:, 0:nt], in_=T[:, 0:nt]), [tt_], [tt_])
            self.stt("dve", OS[:, 0:nt], OS[:, 0:nt], hgn, T[:, 0:nt], ALU.mult, ALU.mult, [ost, tt_, "pc"], [ost])
            pg = self.proj_fm(3, H, htok, nt, 2560 + h * 128)
            self.act(T[:, 0:nt], PS[3][:, 0:nt], AF.Silu, [pg], [tt_])
            self.tt("dve", self.m_HG[:, h, 0:nt], OS[:, 0:nt], T[:, 0:nt], ALU.mult, [ost, tt_], ["mHG"])
        fcol = 1024
        self.gla_tile(H[hb], htok, nt, 1, fcol, LBT, self.consts, sink)
        for mo in range(DC):
            bank = mo % 2
            pt = "ps%d" % bank
            for k in range(4):
                self.mm(PS[bank][:, 0:nt], self.WOUT[:, k, mo * 128:(mo + 1) * 128], self.S5O[:, k, t0:t0 + nt][:, ::-1],
                        k == 0, False, ["WOUT", "S5O"], [pt])
            for k in range(4):
                self.mm(PS[bank][:, 0:nt], self.WOUT[:, 4 + k, mo * 128:(mo + 1) * 128], self.m_HG[:, k, 0:nt],
                        False, k == 3, ["WOUT", "mHG"], [pt])
            T = self.m_T[mo % 2]
            tt_ = "mT%d" % (mo % 2)
            self.act(T[:, 0:nt], PS[bank][:, 0:nt], AF.Copy, [pt, "mod"], [tt_], scale=MOD[:, 16 + mo, mc:mc + 1])
            self.stt("dve", self.m_Z[:, mo, 0:nt], X[:, mo, 0:nt][:, ::-1], ALPHA, T[:, 0:nt], ALU.mult, ALU.add,
                     ["mX", tt_], ["mZ"])
        self.layer_norm(self.m_Z, self.m_ZSQ, nt, PC[:, o["ln1_g"][0]:o["ln1_g"][0] + 8],
                        PC[:, o["ln1_b"][0]:o["ln1_b"][0] + 8],
                        [X[:, k, 0:nt][:, ::-1] for k in range(DC)], 2, 3, self.lntmp, "mZ", ["mX"] * DC, "m")
        self.load("st_mX", x1dst[:, t0:t0 + nt].rearrange("(k p) t -> p k t", p=128), X[:, :, 0:nt], ["mX"], ["x1"])


Model.stage3 = _stage3


def _alloc_stage13(self):
    NT = 256
    self.WIN = self.sb("WIN", [128, DC, 3072], BF16)
    self.WOUT = self.sb("WOUT", [128, DC, D], BF16)
    self.m_X = [self.sb("mX_0", [128, DC, NT], F32)]
    self.m_H = [self.sb("mH_%d" % i, [128, DC, NT], BF16) for i in range(2)]
    self.VT = self.sb("VT", [128, 2, 512], BF16)
    names = ("SG", "FF", "LF", "BC", "D2", "EX", "QQ", "KK")
    self.g_t = [{n: self.sb("g%s_%d" % (n, i), [128, NT], F32) for n in names} for i in range(2)]
    self.g_b = [{n: self.sb("g%s_%d" % (n, i), [128, NT], BF16) for n in ("QD1", "QD2", "KD")} for i in range(2)]
    self.g_KDT = [self.sb("gKDT_%d" % i, [128, 2, 128], BF16) for i in range(2)]
    self.g_KDZ = [self.sb("gKDZ_%d" % i, [128, 2, 128], BF16) for i in range(2)]
    self.g_DEC = [self.sb("gDEC_%d" % i, [128, 8], F32) for i in range(2)]
    self.g_ATT = [self.sb("gATT_%d" % i, [128, 128], BF16) for i in range(2)]
    self.m_HG = self.sb("mHG", [128, 4, NT], BF16)
    self.m_OS = [self.sb("mOS_%d" % i, [128, NT], F32) for i in range(2)]
    self.m_T = [self.sb("mT_%d" % i, [128, NT], F32) for i in range(2)]
    self.m_Z = self.sb("mZ", [128, DC, NT], F32)
    self.m_ZSQ = self.sb("mZSQ", [128, DC, NT], F32)
    self.lntmp = {n: self.sb("ln_" + n, [128, NT], F32) for n in ("mean", "var", "rstd", "nmr")}


Model.alloc_stage13 = _alloc_stage13


def _alloc_ffn2(self):
    NT = 256
    self.WUP = self.sb("WUP", [128, DC, 2 * DFF], BF16)
    self.WDN = self.sb("WDN", [128, FC, D], BF16)
    self.f_X1 = [self.sb("fX1_0", [128, DC, NT], F32)] * 2
    self.f_H2 = [self.sb("fH2_0", [128, DC, NT], BF16)] * 2
    self.f_CVA = [self.sb("fCVA_%d" % i, [128, NT], F32) for i in range(2)]
    self.f_CVG = [self.sb("fCVG_%d" % i, [128, NT], F32) for i in range(2)]
    self.f_ACT = self.sb("fACT", [128, FC, NT], BF16)
    self.f_T = [self.sb("fT_%d" % i, [128, NT], F32) for i in range(2)]
    self.f_Z = self.sb("fZ", [128, DC, NT], F32)
    self.f_ZSQ = self.f_ACT.rearrange("p a b -> p (a b)")[:, 0:DC * NT * 2].bitcast(F32).rearrange("p (a b) -> p a b", b=NT)
    self.lntmp = {n: self.sb("ln_" + n, [128, NT], F32) for n in ("mean", "var", "rstd", "nmr")}


Model.alloc_ffn2 = _alloc_ffn2


def _alloc_s5a(self):
    self.WZT = self.sb("WZT", [128, 32, 2, 128], BF16)
    self.KTOE = self.sb("KTOE", [128, 32, 128], BF16)
    self.TCA = self.sb("TCA", [128, 32, 2, 128], BF16)
    sm = lambda n, w: self.sb("s5_" + n, [128, w], F32)
    self.p_ = {n: sm(n, 32) for n in ("RHO", "PHI", "T1", "T2")}
    self.p_.update({n: sm(n, 512) for n in ("PRE", "PIM")})


def _alloc_s5a_tmp(self):
    sm = lambda n, w: self.sb("s5_" + n, [128, w], F32)
    self.S5P = self.sb("S5P", [128, 2144], F32)
    self.p_.update({n: sm(n, 32) for n in ("DT", "LRDT", "ANG", "DEN", "NR", "CRE", "CIM")})
    self.p_.update({n: sm(n, 512) for n in ("KL", "KA", "MAGK", "SINK", "COSK", "BBR", "BBI", "X1", "X2")})
    self.p_.update({n: sm(n, 2048) for n in ("TA", "TB", "TC")})
    self.TWZ = self.sb("TWZ", [128, 16, 2, 128], BF16)
    self.TM2 = self.sb("TM2", [128, 16, 2, 128], BF16)
    self.KS = self.sb("KS", [128, 2, 128], F32)
    self.XI = self.sb("s5_XI", [128, 512], F32).bitcast(I32)


def _alloc_s5b(self):
    NCH = self.NCH
    self.V = self.sb("Vs5", [128, 32, NCH], BF16)
    self.GY = self.V.rearrange("p g c -> p (g c)").rearrange("p (o t) -> p o t", o=4)
    self.FF = self.sb("FFs5", [128, 2, 2, 16, NCH], BF16)
    self.SELB = self.sb("SELB", [128, 64, 128], BF16)
    self.l2 = {n: self.sb("l2_" + n, [128, NCH], F32) for n in
               ("ZR", "ZI", "CS", "SN", "XA", "A", "B", "GR", "GI", "ER", "EI", "TN")}
    self.s_T = [self.sb("sT_%d" % i, [128, 256], F32) for i in range(2)]
    self.l2I = self.sb("l2_I", [128, NCH], F32).bitcast(I32)
    self.WGLU = self.sb("WGLU", [128, 4, 512], BF16)


Model.alloc_s5a, Model.alloc_s5a_tmp, Model.alloc_s5b = _alloc_s5a, _alloc_s5a_tmp, _alloc_s5b


def _load_sel(self, sel_dram):
    flat = self.SELB.rearrange("p a b -> p (a b)")
    for i in range(8):
        st = self.stage[i % 2]
        tok = "stage%d" % (i % 2)
        self.load("ld_" + tok, st[:, 0:1024], sel_dram[:, i * 1024:(i + 1) * 1024], [], [tok])
        self.copy("pool", flat[:, i * 1024:(i + 1) * 1024], st[:, 0:1024], [tok], ["SELB"])


Model.load_sel = _load_sel


def build_program(ncores=8, nlayers=DEPTH, debug=None, stop_after=None):
    m = Model(nlayers, debug)
    m.setup_common()
    L = DEPTH
    LW = nlayers
    xT = m.inp("xT", [D, TT])
    w_in = m.inp("w_in", [LW, D, 3072]); w_out = m.inp("w_out", [LW, D, D]); w_glu = m.inp("w_glu", [LW, 512, 512])
    w_up = m.inp("w_up", [LW, D, 2 * DFF]); w_down = m.inp("w_down", [LW, DFF, D]); w_mod = m.inp("w_mod", [LW, D, 6 * D])
    pc = m.inp("pc", [L, 128, PC_N]); dsk = m.inp("dsk", [L, 128, 32]); s5p = m.inp("s5p", [L, 128, 2144])
    cin = m.inp("cin", [128, DC, 2]); hglb = m.inp("hglb", [128, L, 8])
    sel = m.inp("sel", [128, 8192]); selT = m.inp("selT", [128, 8192])
    cd = {n: m.inp("k_" + n, [128, w]) for n, w in (("seg", 512), ("gmask", 128), ("eye", 128), ("maskF", 128),
                                                      ("maskB", 128), ("kv", 16), ("midx", 288), ("negpi", 1),
                                                      ("pairsel", 2), ("m96", 1))}
    out = m.outp("outT", [D, NX])
    XS = m.scratch("XS", [D, TT])
    X1S = m.scratch("X1S", [D, TT])
    groups = [[2 * i, 2 * i + 1] for i in range(ncores // 2)]
    C = m.load_consts(cd)
    PC = m.sb("PC", [128, PC_N]); MOD = m.sb("MOD", [128, 48, 2]); MOD1 = m.sb("MOD1", [128, 48, 2])
    m.DSK = m.sb("DSK", [128, 32])
    SC = m.sb("SC", [128, DC, 2]); LBA = m.sb("LBA", [128, L, 8]); OMA = m.sb("OMA", [128, L, 8])
    LSUM = m.sb("LSUM", [128, 8])
    m.S32 = [m.sb("S32_%d" % h, [128, 128], F32) for h in range(4)]
    m.SBF = [m.sb("SBF_%d" % h, [128, 128], BF16) for h in range(4)]
    m.EFIN = m.sb("EFIN", [128, 2, 16]); m.PST = m.sb("PST", [128, 2, 16]); m.APT = m.sb("APT", [128, 2, 16])
    GSRC = m.sb("GSRC", [128, 512]); GDST = m.sb("GDST", [128, 512])
    m.stage = [m.sb("stage%d" % i, [128, 1408], F32) for i in range(2)]
    m.load("ld_sc", SC[:], cin, [], ["SC0"])
    m.act(SC[:], SC[:], AF.Silu, ["SC0"], ["SC"])
    m.load("ld_lb", LBA[:], hglb, [], ["LBA0"])
    m.act(LBA[:], LBA[:], AF.Exp, ["LBA0"], ["LBA0"])
    m.copy("dve", LSUM[:], LBA[:, 0, :], ["LBA0"], ["LSUM"])
    for l in range(1, L):
        m.tt("dve", LSUM[:], LSUM[:], LBA[:, l, :], ALU.add, ["LSUM", "LBA0"], ["LSUM"])
    m.S.op("dve", lambda e: e.reciprocal(out=LSUM[:], in_=LSUM[:]), ["LSUM"], ["LSUM"])
    for l in range(L):
        m.tt("dve", LBA[:, l, :], LBA[:, l, :], LSUM[:], ALU.mult, ["LSUM", "LBA0"], ["LBA0"])
    m.S.op("dve", lambda e: e.memset(LBA[:, 0, :], 0.0), ["LBA0"], ["LBA0"])
    for l in range(2, L):
        m.tt("dve", LBA[:, l, :], LBA[:, l, :], LBA[:, l - 1, :], ALU.add, ["LBA0"], ["LBA0"])
    m.ts("dve", OMA[:], LBA[:], -1.0, 1.0, ALU.mult, ALU.add, ["LBA0"], ["lbt"])
    m.stage_mark()
    base = m.apos

    def finish(dumps):
        m.S.barrier()
        for i, (nm, ap, shape, dt) in enumerate(dumps):
            o = m.outp("dbg_" + nm, shape, dt)
            m.load("dbgs%d" % i, o, ap, [], ["dbgo%d" % i])
        m.S.barrier()
        m.S.emit()
        return m

    for l in range(nlayers):
        last = l == DEPTH - 1
        m.hard_barrier()
        m.S.new_epoch("_L%d" % l)
        src = xT if l == 0 else XS
        LBT = {"lb": LBA[:, l, :], "oml": OMA[:, l, :]}
        m.load("ld_pc", PC[:], pc[l], [], ["pc"])
        m.load("ld_dsk", m.DSK[:], dsk[l], [], ["pcd"])
        m.mod_layer(l, w_mod, PC, SC, MOD, MOD1, m.stage)
        if stop_after == "mod":
            return finish([("MOD", MOD, [128, 48, 2], F32), ("LBA", LBA, [128, L, 8], F32), ("SC", SC, [128, DC, 2], F32)])
        m.S5O = m.sb("S5O", [128, 4, TT], BF16)
        mark1 = m.apos
        m.alloc_s5a()
        mark2 = m.apos
        m.alloc_s5a_tmp()
        m.s5_prep(l, s5p[l], PC, C)
        if stop_after == "s5prep":
            return finish([("WZT", m.WZT, [128, 32, 2, 128], BF16), ("KTOE", m.KTOE, [128, 32, 128], BF16),
                           ("TCA", m.TCA, [128, 32, 2, 128], BF16), ("PRE", m.p_["PRE"], [128, 512], F32),
                           ("PIM", m.p_["PIM"], [128, 512], F32), ("RHO", m.p_["RHO"], [128, 32], F32),
                           ("PHI", m.p_["PHI"], [128, 32], F32), ("BBR", m.p_["BBR"], [128, 512], F32)])
        m.hard_barrier(); m.apos = mark2
        m.UP = m.sb("UP", [128, 4, 8, m.NCH], BF16)
        m.YV = m.UP.rearrange("p o j c -> p (o j) c")
        mark3 = m.apos
        m.WIN = m.sb("WINu", [128, DC, 512], BF16)
        m.m_X = [m.sb("mX_0", [128, DC, 256], F32)]
        m.m_H = [m.sb("mH_%d" % i, [128, DC, 256], BF16) for i in range(2)]
        m.load_win(l, w_in, m.WIN, 0, 512, m.stage)
        m.stage0(l, src, MOD, MOD1)
        if stop_after == "stage0":
            return finish([("UP", m.UP, [128, 4, 8, m.NCH], BF16)])
        m.hard_barrier(); m.apos = mark3
        m.alloc_s5b()
        m.load_sel(sel)
        m.load_weight("WGLU", w_glu[l], m.WGLU, 4, 512, 512, m.stage)

        def xch():
            m.exchange("EFIN", m.EFIN.rearrange("p a b -> p (a b)"), 32, m.PST.rearrange("p a b -> p (a b)"), groups, "PST")
        m.s5_main(l, PC, C, xch)
        if stop_after == "s5main":
            return finish([("YV", m.YV, [128, 32, m.NCH], BF16), ("FF", m.FF, [128, 2, 2, 16, m.NCH], BF16),
                           ("V", m.V, [128, 32, m.NCH], BF16), ("PST", m.PST, [128, 2, 16], F32),
                           ("EFIN", m.EFIN, [128, 2, 16], F32)])
        m.load_sel(selT)
        m.s5_out(l, PC)
        if stop_after == "s5out":
            return finish([("S5O", m.S5O, [128, 4, TT], BF16), ("GY", m.GY, [128, 4, TT], BF16),
                           ("YV", m.YV, [128, 32, m.NCH], BF16)])
        m.hard_barrier(); m.apos = mark1
        if stop_after == "stage1":
            oe = m.outp("dbg_S5Oearly", [128, 4, TT], BF16)
            m.load("dbgearly", oe, m.S5O, [], ["dbgearly"])
            m.S.barrier()
        m.OF = m.sb("OF", [128, 4, TT], BF16)
        m.alloc_stage13()
        m.load_win(l, w_in, m.WIN, 512, 3072, m.stage)
        m.load_weight("WOUT", w_out[l], m.WOUT, DC, D, 1024, m.stage)
        m.stage1(l, src, MOD, MOD1, LBT)
        if stop_after == "stage1":
            return finish([("OF", m.OF, [128, 4, TT], BF16), ("S5O", m.S5O, [128, 4, TT], BF16)] +
                          [("S32_%d" % h, m.S32[h], [128, 128], F32) for h in range(4)])
        for h in range(4):
            m.copy("dve", GSRC[:, h * 128:(h + 1) * 128], m.S32[h][:], ["S32_%d" % h], ["GSRC"])
        m.exchange("GSRC", GSRC[:], 512, GDST[:], groups, "GSRC_p")
        for h in range(4):
            m.copy("dve", m.S32[h][:], GDST[:, h * 128:(h + 1) * 128], ["GSRC_p"], ["S32_%d" % h])
            m.copy("act", m.SBF[h][:], GDST[:, h * 128:(h + 1) * 128], ["GSRC_p"], ["SBF_%d" % h])
        m.stage3(l, src, X1S, MOD, MOD1, LBT, PC, last)
        m.hard_barrier(); m.apos = base
        m.alloc_ffn2()
        m.load_weight("WUP", w_up[l], m.WUP, DC, 2 * DFF, 1408, m.stage)
        m.load_weight("WDN", w_down[l], m.WDN, FC, D, 1024, m.stage)
        tiles = TILES[1:] if last else TILES
        if last:
            dst_fn = lambda t0, nt: [(out[:, t0 - NCTX:t0 - NCTX + nt].rearrange("(k p) t -> p k t", p=128), "xs")]
        else:
            dst_fn = lambda t0, nt: [(XS[:, t0:t0 + nt].rearrange("(k p) t -> p k t", p=128), "xs")]
        m.ffn_stage(l, X1S, dst_fn, PC, MOD, MOD1, tiles)
        m.hard_barrier(); m.apos = base
    if nlayers < DEPTH:
        dx = m.outp("dbgXS", [D, TT]); d1 = m.outp("dbgX1", [D, TT])
        m.load("dbg_a", dx, XS, ["xs"], ["dbg1"])
        m.load("dbg_b", d1, X1S, ["x1"], ["dbg2"])
    m.S.barrier()
    m.S.emit()
    return m


def _const_tables():
    t = np.arange(512)
    seg = np.broadcast_to((t % 32 != 0).astype(np.float32), (128, 512)).copy()
    s = np.arange(128)
    gmask = ((s[:, None] // 32 == s[None, :] // 32) & (s[None, :] >= s[:, None])).astype(np.float32)
    eye = np.eye(128, dtype=np.float32)
    jj = s // 16
    maskF = (jj[None, :] >= jj[:, None]).astype(np.float32)
    maskB = (jj[None, :] <= jj[:, None]).astype(np.float32)
    kv = np.broadcast_to(np.arange(-7, 9, dtype=np.float32), (128, 16)).copy()
    midx = np.broadcast_to(np.arange(1, 289, dtype=np.float32), (128, 288)).copy()
    negpi = np.full((128, 1), -np.pi, np.float32)
    m96 = (np.arange(128) >= 96).astype(np.float32).reshape(128, 1)
    sel = np.zeros((128, 8, 8, 128), np.float32)
    selT = np.zeros((128, 8, 8, 128), np.float32)
    for gi in range(8):
        for j in range(8):
            for h in range(16):
                sel[gi * 16 + h, gi, j, j * 16 + h] = 1.0
                selT[j * 16 + h, gi, j, gi * 16 + h] = 1.0
    return dict(seg=seg, gmask=gmask, eye=eye, maskF=maskF, maskB=maskB, kv=kv, midx=midx, negpi=negpi, m96=m96), \
        sel.reshape(128, 8192), selT.reshape(128, 8192)


def prepare_core_inputs(inputs, b, s):
    f32 = lambda a: np.ascontiguousarray(np.asarray(a, np.float32))
    L = DEPTH
    x, c, ctx, c_ctx = inputs["x"], inputs["c"], inputs["ctx"], inputs["c_ctx"]
    xl = np.asarray(x[b, s * NX:(s + 1) * NX])
    cl = np.asarray(ctx[b])
    if s == 1:
        xl, cl = xl[::-1], cl[::-1]
    dd = [0, 1] if s == 0 else [1, 0]
    m = {}
    m["xT"] = f32(np.concatenate([cl, xl], axis=0).T)
    w_in = np.asarray(inputs["w_in"])
    if s == 1:
        w_in = np.concatenate([w_in[:, :, 0:512], w_in[:, :, 1024:1536], w_in[:, :, 512:1024], w_in[:, :, 1536:]], axis=2)
    m["w_in"] = f32(w_in)
    for n in ("w_out", "w_glu", "w_up", "w_down", "w_mod"):
        m[n] = f32(inputs[n])
    pc = np.zeros((L, 128, PC_N), np.float32)
    dsk = np.zeros((L, 128, 32), np.float32)
    s5p = np.zeros((L, 128, 2144), np.float32)
    for l in range(L):
        cw = np.asarray(inputs["conv_w"][l])
        taps = [cw[0], cw[1], cw[2]] if s == 0 else [cw[2], cw[1], cw[0]]
        vec = {"b_mod": inputs["b_mod"][l], "ln1_g": inputs["ln1_g"][l], "ln1_b": inputs["ln1_b"][l],
               "ln2_g": inputs["ln2_g"][l], "ln2_b": inputs["ln2_b"][l], "cw0": taps[0], "cw1": taps[1],
               "cw2": taps[2], "cb": inputs["conv_b"][l], "s5d": inputs["s5_d"][l], "bglu": inputs["b_glu"][l],
               "hgn": inputs["hg_norm_w"][l]}
        for n, (o_, k) in PC_OFF.items():
            pc[l, :, o_:o_ + k] = cols(vec[n])
        sd = np.asarray(inputs["s5_d"][l]).reshape(32, 16)
        dsk[l] = np.tile(sd.T, (8, 1))
        for dl in range(2):
            d = dd[dl]
            for nm, off in (("s5_lam_re", 0), ("s5_lam_im", 32)):
                a = np.asarray(inputs[nm][l, d]).reshape(16, 2, 64)
                s5p[l, :, off + dl * 16:off + dl * 16 + 16] = a.transpose(1, 2, 0).reshape(128, 16)
            ld = np.asarray(inputs["s5_log_dt"][l, d]).reshape(16, 2)
            s5p[l, :, 64 + dl * 16:64 + dl * 16 + 16] = np.repeat(ld.T[:, None, :], 64, axis=1).reshape(128, 16)
            for nm, off in (("s5_b_re", 96), ("s5_b_im", 608)):
                a = np.asarray(inputs[nm][l, d]).reshape(16, 2, 64, 16)
                s5p[l, :, off + dl * 256:off + dl * 256 + 256] = a.transpose(1, 2, 0, 3).reshape(128, 256)
            for nm, off in (("s5_c_re", 1120), ("s5_c_im", 1632)):
                a = np.asarray(inputs[nm][l, d]).reshape(16, 2, 16, 64)
                s5p[l, :, off + dl * 256:off + dl * 256 + 256] = a.transpose(1, 3, 0, 2).reshape(128, 256)
    m["pc"], m["dsk"], m["s5p"] = pc, dsk, s5p
    m["cin"] = f32(np.stack([cols(c[b]), cols(c_ctx)], axis=-1))
    hg = np.asarray(inputs["hg_lb"])
    hglb = np.zeros((128, L, 8), np.float32)
    for l in range(L):
        for dl in range(2):
            hglb[:, l, dl * 4:dl * 4 + 4] = cols(hg[l, dd[dl]])
    m["hglb"] = hglb
    ct, sel, selT = _const_tables()
    for n, v in ct.items():
        m["k_" + n] = v
    m["k_pairsel"] = np.broadcast_to(np.array([0.0, 1.0] if s == 0 else [1.0, 0.0], np.float32), (128, 2)).copy()
    m["sel"], m["selT"] = sel, selT
    return m


_PROG = {}


def kernel(**inputs):
    ncores = 8
    if "p" not in _PROG:
        _PROG["p"] = build_program(ncores, DEPTH)
    prog = _PROG["p"]
    in_maps = [prepare_core_inputs(inputs, cid // 2, cid % 2) for cid in range(ncores)]
    res = run_bass_kernel_spmd(prog.nc, in_maps, core_ids=list(range(ncores)))
    B = 4
    outp = np.zeros((B, 2 * NX, D), np.float32)
    for cid in range(ncores):
        b, s = cid // 2, cid % 2
        o = np.asarray(res.results[cid]["outT"]).T
        if s == 1:
            o = o[::-1]
        outp[b, s * NX:(s + 1) * NX] = o
    return outp
```

```python
import numpy as np
from contextlib import ExitStack
import concourse.bass as bass
import concourse.mybir as mybir
from concourse.bass_utils import run_bass_kernel_spmd

F32 = mybir.dt.float32
F32R = mybir.dt.float32r
BF16 = mybir.dt.bfloat16
I32 = mybir.dt.int32
AF = mybir.ActivationFunctionType
ALU = mybir.AluOpType

D = 1024
DC = 8
NCTX = 256
NX = 2048
TT = NCTX + NX
DEPTH = 4
DFF = 2816
FC = DFF // 128
ALPHA = (2 * DEPTH) ** 0.25
LN_EPS = 1e-5
RMS_EPS = 1e-6

SELF_SYNC = True
COMPUTE = ("pe", "dve", "act", "pool")


class Sched:
    def __init__(self, nc, es):
        self.nc = nc
        self.es = es
        self.ops = {e: [] for e in ("pe", "dve", "act", "pool", "sp")}
        self.count = {e: 0 for e in COMPUTE}
        self.sems = {}
        self.dma_count = {}
        self.last_write = {}
        self.readers = {}
        self.waited = {e: {} for e in self.ops}
        self.nops = 0
        self.epoch = ""
        self.final_sigs = []

    def sem(self, name):
        if name not in self.sems:
            self.sems[name] = self.es.enter_context(self.nc.semaphore(name))
        return self.sems[name]

    def _deps(self, reads, writes):
        deps = set()
        for t in reads:
            if t in self.last_write:
                deps.add(self.last_write[t])
        for t in writes:
            if t in self.last_write:
                deps.add(self.last_write[t])
            for r in self.readers.get(t, ()):
                deps.add(r)
        return deps

    def _record(self, sig, reads, writes):
        for t in writes:
            self.last_write[t] = sig
            self.readers[t] = []
        for t in reads:
            self.readers.setdefault(t, []).append(sig)

    def _waits(self, eng, deps, own=None):
        waits = []
        for (s, v) in sorted(deps):
            if s == own and (not SELF_SYNC or eng == "pe"):
                continue
            if self.waited[eng].get(s, 0) < v:
                waits.append((s, v))
                self.waited[eng][s] = v
        return waits

    def op(self, eng, fn, reads=(), writes=()):
        reads, writes = tuple(reads), tuple(writes)
        deps = self._deps(reads, writes)
        self.count[eng] += 1
        sname = "c_" + eng + self.epoch
        self.sem(sname)
        sig = (sname, self.count[eng])
        self.ops[eng].append((self._waits(eng, deps, sname), fn, sname, False))
        self._record(sig, reads, writes)
        self.nops += 1

    def dma(self, eng, semname, fn, reads=(), writes=(), ndma=1, inc=16):
        reads, writes = tuple(reads), tuple(writes)
        deps = self._deps(reads, writes)
        self.sem(semname)
        self.dma_count[semname] = self.dma_count.get(semname, 0) + inc * ndma
        sig = (semname, self.dma_count[semname])
        self.ops[eng].append((self._waits(eng, deps), fn, semname, True))
        self._record(sig, reads, writes)
        self.nops += 1

    def barrier(self):
        sigs = [("c_%s%s" % (e, self.epoch), self.count[e]) for e in COMPUTE if self.count[e] > 0]
        sigs += [(k, v) for k, v in self.dma_count.items()]
        sigs += list(self.final_sigs)
        for eng in self.ops:
            w = []
            for (s_, v) in sorted(sigs):
                if self.waited[eng].get(s_, 0) < v:
                    w.append((s_, v))
                    self.waited[eng][s_] = v
            self.ops[eng].append((w, None, None, False))
        self.last_write = {}
        self.readers = {}

    def new_epoch(self, tag):
        self.final_sigs = [("c_%s%s" % (e, self.epoch), self.count[e]) for e in COMPUTE if self.count[e] > 0]
        self.epoch = tag
        self.count = {e: 0 for e in COMPUTE}

    def wait_all(self, eng, tokens):
        deps = self._deps(tuple(tokens), ())
        self.ops[eng].append((self._waits(eng, deps), None, None, False))

    def emit(self):
        nc = self.nc
        with nc.Block() as block:
            def mk(ename):
                def body(e):
                    for (waits, fn, sname, is_dma) in self.ops[ename]:
                        for (s, v) in waits:
                            e.wait_ge(self.sems[s], v)
                        if fn is None:
                            continue
                        if is_dma:
                            fn(e, self.sems[sname])
                        else:
                            fn(e).then_inc(self.sems[sname], 1)
                return body
            block.tensor(mk("pe"))
            block.vector(mk("dve"))
            block.scalar(mk("act"))
            block.gpsimd(mk("pool"))
            block.sync(mk("sp"))


class Builder:
    def __init__(self, nlayers=DEPTH, debug=None):
        self.nl = nlayers
        self.debug = debug or {}
        self.nc = bass.Bass("TRN2", target_bir_lowering=False)
        self.es = ExitStack()
        self.S = Sched(self.nc, self.es)
        self.din = {}
        self.dout = {}
        self.uid = 0

    def inp(self, name, shape, dt=F32):
        t = self.nc.dram_tensor(name, list(shape), dt, kind="ExternalInput").ap()
        self.din[name] = t
        return t

    def outp(self, name, shape, dt=F32):
        t = self.nc.dram_tensor(name, list(shape), dt, kind="ExternalOutput").ap()
        self.dout[name] = t
        return t

    def scratch(self, name, shape, dt=F32):
        return self.nc.dram_tensor(name, list(shape), dt, kind="Internal").ap()

    ARENA = 106000

    def sb(self, name, shape, dt=F32):
        if not hasattr(self, "arena"):
            self.arena = self.es.enter_context(self.nc.sbuf_tensor("arena", [128, self.ARENA], BF16))
            self.apos = 0
            self.amark = 0
        assert shape[0] == 128
        n = int(np.prod(shape[1:]))
        nb = n * (2 if dt == F32 else 1)
        nb = (nb + 15) // 16 * 16
        assert self.apos + nb <= self.ARENA, "SBUF arena overflow at %s (%d + %d)" % (name, self.apos, nb)
        ap = self.arena[:, self.apos:self.apos + n * (2 if dt == F32 else 1)]
        self.apos += nb
        if dt == F32:
            ap = ap.bitcast(F32)
        if len(shape) == 3:
            ap = ap.rearrange("p (a b) -> p a b", b=shape[2])
        elif len(shape) == 4:
            ap = ap.rearrange("p (a b c) -> p a b c", b=shape[2], c=shape[3])
        elif len(shape) == 5:
            ap = ap.rearrange("p (a b c d) -> p a b c d", b=shape[2], c=shape[3], d=shape[4])
        return ap

    def hard_barrier(self):
        self.S.barrier()
        if not hasattr(self, "_hb_dram"):
            self._hb_dram = self.scratch("hb_scratch", [128, 16])
            self._hb_n = 0
        self._hb_n += 1
        self.load("hb_sem", self._hb_dram, self.ones[:, 0:16], [], ["hb%d" % self._hb_n])
        self.S.barrier()

    def stage_mark(self):
        self.amark = self.apos

    def stage_reset(self):
        self.S.barrier()
        self.apos = self.amark

    def ps(self, name, shape, dt=F32):
        return self.es.enter_context(self.nc.psum_tensor(name, list(shape), dt))

    def mm(self, out, lhsT, rhs, start, stop, reads, writes):
        self.S.op("pe", lambda e: e.matmul(out, lhsT=lhsT, rhs=rhs, start=start, stop=stop), reads, writes)

    def act(self, out, in_, func, reads, writes, bias=None, scale=None):
        kw = {}
        if bias is not None:
            kw["bias"] = bias
        if scale is not None:
            kw["scale"] = scale
        self.S.op("act", lambda e: e.activation(out=out, in_=in_, func=func, **kw), reads, writes)

    def tt(self, eng, out, in0, in1, op, reads, writes):
        self.S.op(eng, lambda e: e.tensor_tensor(out=out, in0=in0, in1=in1, op=op), reads, writes)

    def ts(self, eng, out, in0, s1, s2, op0, op1, reads, writes):
        if s2 is None:
            self.S.op(eng, lambda e: e.tensor_scalar(out=out, in0=in0, scalar1=s1, scalar2=None, op0=op0), reads, writes)
        else:
            self.S.op(eng, lambda e: e.tensor_scalar(out=out, in0=in0, scalar1=s1, scalar2=s2, op0=op0, op1=op1), reads, writes)

    def stt(self, eng, out, in0, scalar, in1, op0, op1, reads, writes):
        self.S.op(eng, lambda e: e.scalar_tensor_tensor(out=out, in0=in0, scalar=scalar, in1=in1, op0=op0, op1=op1), reads, writes)

    def copy(self, eng, out, in_, reads, writes):
        if eng == "act":
            self.S.op("act", lambda e: e.copy(out=out, in_=in_), reads, writes)
        else:
            self.S.op(eng, lambda e: e.tensor_copy(out=out, in_=in_), reads, writes)


    def sin_turns(self, out, T, TI, TN, rd_tok, ttok, otok):
        self.copy("dve", TI, T, [ttok], [ttok + "i"])
        self.copy("dve", TN, TI, [ttok + "i"], [ttok + "n"])
        self.tt("dve", T, T, TN, ALU.subtract, [ttok, ttok + "n"], [ttok])
        self.act(out, T, AF.Sin, [ttok], [otok], scale=2.0 * float(np.pi))

    def load(self, semname, out, in_, reads, writes, eng="sp"):
        self.S.dma(eng, semname, lambda e, s: e.dma_start(out=out, in_=in_).then_inc(s, 16), reads, writes)

    def dbg(self, name, ap_sb, shape, reads):
        if name not in self.debug:
            return
        o = self.outp("dbg_" + name, shape)
        self.uid += 1
        self.load("dbg%d" % self.uid, o, ap_sb, reads, ["dbgout_" + name])
        self.dbg_tokens.append("dbgout_" + name)


def _pc_layout():
    off = {}
    o = 0
    for name, n in (("b_mod", 48), ("ln1_g", 8), ("ln1_b", 8), ("ln2_g", 8), ("ln2_b", 8),
                    ("cw0", 44), ("cw1", 44), ("cw2", 44), ("cb", 44), ("s5d", 4), ("bglu", 4),
                    ("hgn", 1)):
        off[name] = (o, n)
        o += n
    return off, o


PC_OFF, PC_N = _pc_layout()


def cols(v):
    v = np.asarray(v, np.float32)
    n = v.shape[0] // 128
    return np.ascontiguousarray(v.reshape(n, 128).T)


class Model(Builder):
    def setup_common(self):
        nc = self.nc
        self.PS = [self.ps("psb%d" % b, [128, 512]) for b in range(8)]
        self.ones = self.sb("ones", [128, 128], F32)
        self.S.op("dve", lambda e: e.memset(self.ones[:], 1.0), (), ["ones"])
        self.epscol = self.sb("epscol", [128, 2], F32)
        self.S.op("dve", lambda e: e.memset(self.epscol[:, 0:1], LN_EPS), (), ["ones"])
        self.S.op("dve", lambda e: e.memset(self.epscol[:, 1:2], RMS_EPS), (), ["ones"])
        self.dbg_tokens = []

    def load_weight(self, name, src, dst, K, N, nstage_cols, stage):
        nst = len(stage)
        i = getattr(self, "_stg_i", 0)
        for k in range(K):
            for c0 in range(0, N, nstage_cols):
                cw = min(nstage_cols, N - c0)
                st = stage[i % nst]
                tok = "stage%d" % (i % nst)
                self.load("ld_" + tok, st[:, 0:cw], src[k * 128:(k + 1) * 128, c0:c0 + cw], [], [tok],
                          eng=("sp" if i % 2 == 0 else "sp"))
                self.copy(("pool", "act", "dve")[i % 3], dst[:, k, c0:c0 + cw], st[:, 0:cw], [tok], [name])
                i += 1
        self._stg_i = i

    def layer_norm(self, Z, ZSQ, nt, gcol, bcol, out_aps, ps1, ps2, tmp, ztok, outtoks, tag, zsqtok=None):
        PS = self.PS
        t1, t2 = "ps%d" % ps1, "ps%d" % ps2
        zq = (lambda k: zsqtok) if zsqtok else (lambda k: tag + "zsq%d" % k)
        for k in range(DC):
            self.mm(PS[ps1][:, 0:nt], self.ones[:], Z[:, k, 0:nt], k == 0, k == DC - 1, [ztok, "ones"], [t1])
        for k in range(DC):
            self.act(ZSQ[:, k, 0:nt], Z[:, k, 0:nt], AF.Square, [ztok], [zq(k)])
            self.mm(PS[ps2][:, 0:nt], self.ones[:], ZSQ[:, k, 0:nt], k == 0, k == DC - 1,
                    [zq(k), "ones"], [t2])
        mean, var, rstd, nmr = tmp["mean"], tmp["var"], tmp["rstd"], tmp["nmr"]
        mt = tag + "lnstat"
        self.act(mean[:, 0:nt], PS[ps1][:, 0:nt], AF.Copy, [t1], [mt + "m"], scale=1.0 / D)
        self.tt("dve", var[:, 0:nt], mean[:, 0:nt], mean[:, 0:nt], ALU.mult, [mt + "m"], [mt + "v"])
        self.stt("dve", var[:, 0:nt], PS[ps2][:, 0:nt], 1.0 / D, var[:, 0:nt], ALU.mult, ALU.subtract,
                 [t2, mt + "v"], [mt + "v"])
        self.act(var[:, 0:nt], var[:, 0:nt], AF.Sqrt, [mt + "v"], [mt + "v"], bias=self.epscol[:, 0:1])
        self.S.op("dve", lambda e: e.reciprocal(out=rstd[:, 0:nt], in_=var[:, 0:nt]), [mt + "v"], [mt + "r"])
        self.stt("dve", nmr[:, 0:nt], mean[:, 0:nt], -1.0, rstd[:, 0:nt], ALU.mult, ALU.mult,
                 [mt + "m", mt + "r"], [mt + "n"])
        for k in range(DC):
            eng = "dve" if k % 2 == 0 else "pool"
            self.tt(eng, ZSQ[:, k, 0:nt], Z[:, k, 0:nt], rstd[:, 0:nt], ALU.mult, [ztok, mt + "r"], [zq(k)])
            self.tt(eng, ZSQ[:, k, 0:nt], ZSQ[:, k, 0:nt], nmr[:, 0:nt], ALU.add, [zq(k), mt + "n"],
                    [zq(k)])
            self.act(out_aps[k], ZSQ[:, k, 0:nt], AF.Identity, [zq(k)], [outtoks[k]],
                     bias=bcol[:, k:k + 1], scale=gcol[:, k:k + 1])

    def alloc_ffn(self):
        self.WUP = self.sb("WUP", [128, DC, 2 * DFF], BF16)
        self.WDN = self.sb("WDN", [128, FC, D], BF16)
        self.stage = [self.sb("stage%d" % i, [128, 1408], F32) for i in range(2)]
        NT = 256
        self.f_X1 = [self.sb("fX1_%d" % i, [128, DC, NT], F32) for i in range(2)]
        self.f_H2 = [self.sb("fH2_%d" % i, [128, DC, NT], BF16) for i in range(2)]
        self.f_CVA = [self.sb("fCVA_%d" % i, [128, NT], F32) for i in range(2)]
        self.f_CVG = [self.sb("fCVG_%d" % i, [128, NT], F32) for i in range(2)]
        self.f_ACT = self.sb("fACT", [128, FC, NT], BF16)
        self.f_T = [self.sb("fT_%d" % i, [128, NT], F32) for i in range(2)]
        self.f_Z = self.sb("fZ", [128, DC, NT], F32)
        self.f_ZSQ = self.sb("fZSQ", [128, DC, NT], F32)
        self.lntmp = {n: self.sb("ln_" + n, [128, 256], F32) for n in ("mean", "var", "rstd", "nmr")}

    def ffn_stage(self, l, src, dst_fn, PC, MOD, MOD1, tiles):
        PS = self.PS
        o = PC_OFF
        for it, (t0, nt, rw, mc) in enumerate(tiles):
            pb = it % 2
            X1, H2 = self.f_X1[pb], self.f_H2[pb]
            xtok, htok = "fX1_%d" % pb, "fH2_%d" % pb
            self.load("ld_" + xtok, X1[:, :, 0:nt], src[:, t0:t0 + nt].rearrange("(k p) t -> p k t", p=128),
                      ["xs_%d" % t0], [xtok])
            for k in range(DC):
                self.act(H2[:, k, 0:nt], X1[:, k, 0:nt], AF.Identity, [xtok, "mod"], [htok],
                         bias=MOD[:, 24 + k, mc:mc + 1], scale=MOD1[:, 32 + k, mc:mc + 1])
            nr = nt // rw
            for m in range(FC):
                pp = m % 2
                for half, (mm_, cv, cvt, bank) in enumerate(((m, self.f_CVA[pp], "fCVA_%d" % pp, 0 + pp),
                                                              (m + FC, self.f_CVG[pp], "fCVG_%d" % pp, 2 + pp))):
                    pt = "ps%d" % bank
                    for k in range(DC):
                        self.mm(PS[bank][:, 0:nt], self.WUP[:, k, mm_ * 128:(mm_ + 1) * 128], H2[:, k, 0:nt],
                                k == 0, k == DC - 1, ["WUP", htok], [pt])
                    c0 = o["cw0"][0] + mm_
                    c1 = o["cw1"][0] + mm_
                    c2 = o["cw2"][0] + mm_
                    cb = o["cb"][0] + mm_
                    self.act(cv[:, 0:nt], PS[bank][:, 0:nt], AF.Identity, [pt, "pc"], [cvt],
                             bias=PC[:, cb:cb + 1], scale=PC[:, c1:c1 + 1])
                    pv = PS[bank][:, 0:nt].rearrange("p (r w) -> p r w", w=rw)
                    cvv = cv[:, 0:nt].rearrange("p (r w) -> p r w", w=rw)
                    self.stt("dve", cvv[:, :, 1:rw], pv[:, :, 0:rw - 1], PC[:, c0:c0 + 1], cvv[:, :, 1:rw],
                             ALU.mult, ALU.add, [pt, cvt, "pc"], [cvt])
                    self.stt("dve", cvv[:, :, 0:rw - 1], pv[:, :, 1:rw], PC[:, c2:c2 + 1], cvv[:, :, 0:rw - 1],
                             ALU.mult, ALU.add, [pt, cvt, "pc"], [cvt])
                T = self.f_T[pp]
                self.act(T[:, 0:nt], self.f_CVA[pp][:, 0:nt], AF.Silu, ["fCVA_%d" % pp], ["fT_%d" % pp])
                self.tt("pool", self.f_ACT[:, m, 0:nt], T[:, 0:nt], self.f_CVG[pp][:, 0:nt], ALU.mult,
                        ["fT_%d" % pp, "fCVG_%d" % pp], ["fACT"])
            for mo in range(DC):
                bank = 4 + mo % 2
                pt = "ps%d" % bank
                for k in range(FC):
                    self.mm(PS[bank][:, 0:nt], self.WDN[:, k, mo * 128:(mo + 1) * 128], self.f_ACT[:, k, 0:nt],
                            k == 0, k == FC - 1, ["WDN", "fACT"], [pt])
                T = self.f_T[mo % 2]
                self.act(T[:, 0:nt], PS[bank][:, 0:nt], AF.Copy, [pt, "mod"], ["fT_%d" % (mo % 2)],
                         scale=MOD[:, 40 + mo, mc:mc + 1])
                self.stt("dve", self.f_Z[:, mo, 0:nt], X1[:, mo, 0:nt], ALPHA, T[:, 0:nt], ALU.mult, ALU.add,
                         [xtok, "fT_%d" % (mo % 2)], ["fZ"])
            OUT = X1
            otok = xtok
            self.layer_norm(self.f_Z, self.f_ZSQ, nt, PC[:, o["ln2_g"][0]:o["ln2_g"][0] + 8],
                            PC[:, o["ln2_b"][0]:o["ln2_b"][0] + 8],
                            [OUT[:, k, 0:nt] for k in range(DC)], 6, 7, self.lntmp, "fZ", [otok] * DC, "f", zsqtok="fACT")
            for (dap, wtok) in dst_fn(t0, nt):
                self.load("st_" + otok, dap, OUT[:, :, 0:nt], [otok], [wtok])

    def alloc_mixer(self):
        NT = 512
        self.WIN = self.sb("WIN", [128, DC, 3072], BF16)
        self.WOUT = self.sb("WOUT", [128, DC, D], BF16)
        self.WGLU = self.sb("WGLU", [128, 4, 512], BF16)
        self.stage = [self.sb("stage%d" % i, [128, 1024], F32) for i in range(2)]
        self.m_X = [self.sb("mX_%d" % i, [128, DC, NT], F32) for i in range(2)]
        self.m_H = [self.sb("mH_%d" % i, [128, DC, NT], BF16) for i in range(2)]
        self.OF = self.sb("OF", [128, 4, TT], BF16)
        self.S5O = self.sb("S5O", [128, 4, TT], BF16)
        self.VT = self.sb("VT", [128, 4, 512], BF16)
        names = ("SG", "FF", "LF", "BC", "D2", "EX", "QQ", "KK")
        self.g_t = [{n: self.sb("g%s_%d" % (n, i), [128, NT], F32) for n in names} for i in range(2)]
        self.g_b = [{n: self.sb("g%s_%d" % (n, i), [128, NT], BF16) for n in ("QD1", "QD2", "KD")} for i in range(2)]
        self.g_KDT = [self.sb("gKDT_%d" % i, [128, 4, 128], BF16) for i in range(2)]
        self.g_DEC = [self.sb("gDEC_%d" % i, [128, 16], F32) for i in range(2)]
        self.g_ATT = [self.sb("gATT_%d" % i, [128, 128], BF16) for i in range(2)]
        self.S32 = [self.sb("S32_%d" % h, [128, 128], F32) for h in range(4)]
        self.SBF = [self.sb("SBF_%d" % h, [128, 128], BF16) for h in range(4)]
        self.m_HG = self.sb("mHG", [128, 4, NT], BF16)
        self.m_OS = [self.sb("mOS_%d" % i, [128, NT], F32) for i in range(2)]
        self.m_T = [self.sb("mT_%d" % i, [128, NT], F32) for i in range(2)]
        self.m_Z = self.sb("mZ", [128, DC, NT], F32)
        self.m_ZSQ = self.sb("mZSQ", [128, DC, NT], F32)
        self.lntmp = {n: self.sb("ln_" + n, [128, NT], F32) for n in ("mean", "var", "rstd", "nmr")}

    def modulate_tile(self, X, H, nt, xtok, htok, MOD, MOD1, sh0, sc0, mc, rev):
        for k in range(DC):
            out = H[:, k, nt - 1::-1] if False else H[:, k, 0:nt]
            src = X[:, k, 0:nt]
            if rev:
                src = X[:, k, 0:nt][:, ::-1]
            self.act(out, src, AF.Identity, [xtok, "mod"], [htok],
                     bias=MOD[:, sh0 + k, mc:mc + 1], scale=MOD1[:, sc0 + k, mc:mc + 1])

    def proj_fm(self, bank, H, htok, nt, col0):
        pt = "ps%d" % bank
        for k in range(DC):
            self.mm(self.PS[bank][:, 0:nt], self.WIN[:, k, col0:col0 + 128], H[:, k, 0:nt], k == 0, k == DC - 1,
                    ["WIN", htok], [pt])
        return pt

    def gla_tile(self, H, htok, nt, d, fcol0, LBT, consts, o_sink):
        PS = self.PS
        nb = nt // 128
        nch = nt // 32
        VCOL, QCOL = 1536, 2048
        SEG, GMASK, IDB = consts["seg"], consts["gmask"], consts["identb"]
        for b in range(nb):
            for k in range(DC):
                self.mm(PS[2][:, 0:512], H[:, k, b * 128:(b + 1) * 128], self.WIN[:, k, VCOL:VCOL + 512],
                        k == 0, k == DC - 1, ["WIN", htok], ["ps2"])
            self.copy("act", self.VT[:, b, :], PS[2][:, 0:512], ["ps2"], ["VT"])
        psb = PS[3][:].bitcast(BF16)
        DSB = (5, 2)
        def prep_gen(h):
            hp = h % 2
            T, B = self.g_t[hp], self.g_b[hp]
            tk = lambda n: "g%s_%d" % (n, hp)
            fb, qb = (0, 1) if hp == 0 else (1, 0)
            pf = self.proj_fm(fb, H, htok, nt, fcol0 + h * 128)
            yield
            self.act(T["SG"][:, 0:nt], PS[fb][:, 0:nt], AF.Sigmoid, [pf], [tk("SG")])
            yield
            pq = self.proj_fm(qb, H, htok, nt, QCOL + h * 128)
            yield
            self.act(T["QQ"][:, 0:nt], PS[qb][:, 0:nt], AF.Silu, [pq], [tk("QQ")])
            yield
            lbc = d * 4 + h
            self.ts("dve", T["FF"][:, 0:nt], T["SG"][:, 0:nt], LBT["oml"][:, lbc:lbc + 1], LBT["lb"][:, lbc:lbc + 1],
                    ALU.mult, ALU.add, [tk("SG"), "lbt"], [tk("FF")])
            yield
            self.act(T["LF"][:, 0:nt], T["FF"][:, 0:nt], AF.Ln, [tk("FF")], [tk("LF")])
            yield
            self.S.op("dve", lambda e, T=T: e.tensor_tensor_scan(out=T["BC"][:, 0:nt], data0=SEG[:, 0:nt],
                                                                 data1=T["LF"][:, 0:nt], initial=0.0,
                                                                 op0=ALU.mult, op1=ALU.add),
                      [tk("LF"), "consts"], [tk("BC")])
            yield
            self.ts("pool", T["KK"][:, 0:nt], T["FF"][:, 0:nt], -1.0, 1.0, ALU.mult, ALU.add, [tk("FF")], [tk("KK")])
            yield
            bc3 = T["BC"][:, 0:nt].rearrange("p (c w) -> p c w", w=32)
            d23 = T["D2"][:, 0:nt].rearrange("p (c w) -> p c w", w=32)
            self.tt("dve", d23, bc3, bc3[:, :, 31:32].to_broadcast([128, nch, 32]), ALU.subtract, [tk("BC")], [tk("D2")])
            yield
            DEC = self.g_DEC[hp]
            self.act(DEC[:, 0:nch], T["BC"][:, 31:nt:32], AF.Exp, [tk("BC")], ["gDEC_%d" % hp])
            yield
            self.act(T["EX"][:, 0:nt], T["D2"][:, 0:nt], AF.Exp, [tk("D2")], [tk("EX")])
            yield
            self.tt("dve", B["QD2"][:, 0:nt], T["QQ"][:, 0:nt], T["EX"][:, 0:nt], ALU.mult, [tk("QQ"), tk("EX")], [tk("QD2")])
            yield
            self.act(T["EX"][:, 0:nt], T["D2"][:, 0:nt], AF.Exp, [tk("D2")], [tk("EX")], scale=-1.0)
            yield
            self.tt("pool", B["KD"][:, 0:nt], T["KK"][:, 0:nt], T["EX"][:, 0:nt], ALU.mult, [tk("KK"), tk("EX")], [tk("KD")])
            yield
            self.act(T["EX"][:, 0:nt], T["BC"][:, 0:nt], AF.Exp, [tk("BC")], [tk("EX")])
            yield
            self.tt("dve", B["QD1"][:, 0:nt], T["QQ"][:, 0:nt], T["EX"][:, 0:nt], ALU.mult, [tk("QQ"), tk("EX")], [tk("QD1")])
            yield

        for h0 in (0, 2):
            gens = [prep_gen(h0), prep_gen(h0 + 1)]
            while gens:
                for g_ in list(gens):
                    try:
                        next(g_)
                    except StopIteration:
                        gens.remove(g_)
            for b in range(nb):
                bs = slice(b * 128, (b + 1) * 128)
                for h in (h0, h0 + 1):
                    hp = h % 2
                    B = self.g_b[hp]
                    tk = lambda n: "g%s_%d" % (n, hp)
                    KDT = self.g_KDT[hp]
                    self.S.op("pe", lambda e, bs=bs, B=B: e.transpose(psb[:, 0:128], B["KD"][:, bs], IDB[:]),
                              [tk("KD"), "consts"], ["ps3"])
                    self.copy("act", KDT[:, b, :], psb[:, 0:128], ["ps3"], ["gKDT_%d" % hp])
                    KDZ = self.g_KDZ[hp]
                    self.ts("dve", KDZ[64:128, b, :], psb[64:128, 0:128], consts["m96"][64:128, 0:1], None, ALU.mult, None,
                            ["ps3", "consts"], ["gKDT_%d" % hp])
                    self.mm(PS[4][:, 0:128], B["KD"][:, bs], B["QD2"][:, bs], True, True, [tk("KD"), tk("QD2")], ["ps4"])
                    ATT = self.g_ATT[hp]
                    self.tt("dve", ATT[:], PS[4][:, 0:128], GMASK[:], ALU.mult, ["ps4", "consts"], ["gATT_%d" % hp])
                for ci in range(4):
                    cs = slice(b * 128 + ci * 32, b * 128 + ci * 32 + 32)
                    rs = slice(ci * 32, ci * 32 + 32)
                    cc = b * 4 + ci
                    for h in (h0, h0 + 1):
                        hp = h % 2
                        B = self.g_b[hp]
                        tk = lambda n: "g%s_%d" % (n, hp)
                        ob, ot = 6 + hp, "ps%d" % (6 + hp)
                        db, dt_ = DSB[hp], "ps%d" % DSB[hp]
                        self.mm(PS[ob][:, cs], self.SBF[h][:], B["QD1"][:, cs], b == 0 and ci == 0, False,
                                ["SBF_%d" % h, tk("QD1")], [ot])
                        if ci == 3:
                            r64 = slice(64, 128)
                            self.mm(PS[db][:, 0:128], self.g_KDZ[hp][r64, b, :], self.VT[r64, b, h * 128:(h + 1) * 128],
                                    True, True, ["gKDT_%d" % hp, "VT"], [dt_])
                        else:
                            self.mm(PS[db][:, 0:128], self.g_KDT[hp][rs, b, :], self.VT[rs, b, h * 128:(h + 1) * 128],
                                    True, True, ["gKDT_%d" % hp, "VT"], [dt_])
                        self.stt("dve", self.S32[h][:], self.S32[h][:], self.g_DEC[hp][:, cc:cc + 1], PS[db][:, 0:128],
                                 ALU.mult, ALU.add, ["S32_%d" % h, "gDEC_%d" % hp, dt_], ["S32_%d" % h])
                        self.copy("dve", self.SBF[h][:], self.S32[h][:], ["S32_%d" % h], ["SBF_%d" % h])
                for h in (h0, h0 + 1):
                    hp = h % 2
                    self.mm(PS[6 + hp][:, bs], self.VT[:, b, h * 128:(h + 1) * 128], self.g_ATT[hp][:], False, b == nb - 1,
                            ["VT", "gATT_%d" % hp], ["ps%d" % (6 + hp)])
            for h in (h0, h0 + 1):
                o_sink(h, 6 + h % 2, "ps%d" % (6 + h % 2))

    NCH = TT // 8

    def alloc_s5(self):
        NCH = self.NCH
        self.UP = self.sb("UP", [128, 4, 8, NCH], BF16)
        self.YV = self.UP[:].rearrange("p o j c -> p (o j) c")
        self.V = self.sb("Vs5", [128, 32, NCH], BF16)
        self.FF = self.sb("FFs5", [128, 2, 2, 16, NCH], BF16)
        self.WZT = self.sb("WZT", [128, 32, 2, 128], BF16)
        self.KTOE = self.sb("KTOE", [128, 32, 128], BF16)
        self.TCA = self.sb("TCA", [128, 32, 2, 128], BF16)
        self.SELB = self.sb("SELB", [128, 64, 128], BF16)
        self.GY = self.sb("GY", [128, 4, TT], BF16)
        self.S5P = self.sb("S5P", [128, 2144], F32)
        sm = lambda n, w: self.sb("s5_" + n, [128, w], F32)
        self.p_ = {n: sm(n, 32) for n in ("DT", "LRDT", "ANG", "DEN", "NR", "CRE", "CIM", "T1", "T2", "RHO", "PHI")}
        self.p_.update({n: sm(n, 512) for n in ("KL", "KA", "MAGK", "SINK", "COSK", "PRE", "PIM", "BBR", "BBI", "X1", "X2")})
        self.p_.update({n: sm(n, 2048) for n in ("TA", "TB", "TC")})
        self.TWZ = self.sb("TWZ", [128, 16, 2, 128], BF16)
        self.TM2 = self.sb("TM2", [128, 16, 2, 128], BF16)
        self.l2 = {n: self.sb("l2_" + n, [128, NCH], F32) for n in
                   ("ZR", "ZI", "CS", "SN", "XA", "A", "B", "GR", "GI", "ER", "EI")}
        self.EFIN = self.sb("EFIN", [128, 2, 16], F32)
        self.PST = self.sb("PST", [128, 2, 16], F32)
        self.APT = self.sb("APT", [128, 2, 16], F32)
        self.KS = self.sb("KS", [128, 2, 128], F32)

    def s5_prep(self, l, s5p_dram, PC, consts):
        P = self.p_
        PS = self.PS
        S5P = self.S5P
        KV = consts["kv"]
        self.load("ld_s5p", S5P[:], s5p_dram, [], ["s5p"])
        LR, LI, LOGDT = S5P[:, 0:32], S5P[:, 32:64], S5P[:, 64:96]
        v3 = lambda a: a.rearrange("p (a h) -> p a h", h=16)
        BRE, BIM = v3(S5P[:, 96:608]), v3(S5P[:, 608:1120])
        CRE_, CIM_ = v3(S5P[:, 1120:1632]), v3(S5P[:, 1632:2144])
        tkn = lambda n: "s5_" + n
        PI = float(np.pi)
        self.act(P["DT"][:], LOGDT, AF.Exp, ["s5p"], [tkn("DT")])
        self.tt("dve", P["LRDT"][:], LR, P["DT"][:], ALU.mult, ["s5p", tkn("DT")], [tkn("LRDT")])
        self.tt("dve", P["ANG"][:], LI, P["DT"][:], ALU.mult, ["s5p", tkn("DT")], [tkn("ANG")])
        b16 = lambda a: a.unsqueeze(2).to_broadcast([128, 32, 16])
        kvb = KV[:, 0:16].unsqueeze(1).to_broadcast([128, 32, 16])
        self.tt("dve", v3(P["KL"][:]), b16(P["LRDT"][:]), kvb, ALU.mult, [tkn("LRDT"), "consts"], [tkn("KL")])
        self.tt("dve", v3(P["KA"][:]), b16(P["ANG"][:]), kvb, ALU.mult, [tkn("ANG"), "consts"], [tkn("KA")])
        self.act(P["MAGK"][:], P["KL"][:], AF.Exp, [tkn("KL")], [tkn("MAGK")])
        I2P = 1.0 / (2.0 * PI)
        self.ts("dve", P["X1"][:], P["KA"][:], I2P, None, ALU.mult, None, [tkn("KA")], [tkn("X1")])
        self.sin_turns(P["SINK"][:], P["X1"][:], self.XI[:], P["X2"][:], None, tkn("X1"), tkn("SINK"))
        self.ts("dve", P["X1"][:], P["KA"][:], I2P, 0.25, ALU.mult, ALU.add, [tkn("KA")], [tkn("X1")])
        self.sin_turns(P["COSK"][:], P["X1"][:], self.XI[:], P["X2"][:], None, tkn("X1"), tkn("COSK"))
        self.tt("dve", P["PRE"][:], P["MAGK"][:], P["COSK"][:], ALU.mult, [tkn("MAGK"), tkn("COSK")], [tkn("PRE")])
        self.tt("dve", P["PIM"][:], P["MAGK"][:], P["SINK"][:], ALU.mult, [tkn("MAGK"), tkn("SINK")], [tkn("PIM")])
        PRE3, PIM3 = v3(P["PRE"][:]), v3(P["PIM"][:])
        AR, AI = PRE3[:, :, 8], PIM3[:, :, 8]
        self.copy("dve", P["RHO"][:], v3(P["MAGK"][:])[:, :, 15], [tkn("MAGK")], [tkn("RHO")])
        self.ts("dve", P["PHI"][:], P["ANG"][:], 8.0 * I2P, None, ALU.mult, None, [tkn("ANG")], [tkn("PHI")])
        self.copy("dve", self.XI[:, 0:32], P["PHI"][:], [tkn("PHI")], ["s5_XI"])
        self.copy("dve", P["T1"][:], self.XI[:, 0:32], ["s5_XI"], [tkn("T1")])
        self.tt("dve", P["PHI"][:], P["PHI"][:], P["T1"][:], ALU.subtract, [tkn("PHI"), tkn("T1")], [tkn("PHI")])
        self.tt("dve", P["DEN"][:], LR, LR, ALU.mult, ["s5p"], [tkn("DEN")])
        self.tt("dve", P["T1"][:], LI, LI, ALU.mult, ["s5p"], [tkn("T1")])
        self.tt("dve", P["DEN"][:], P["DEN"][:], P["T1"][:], ALU.add, [tkn("DEN"), tkn("T1")], [tkn("DEN")])
        self.S.op("dve", lambda e: e.reciprocal(out=P["DEN"][:], in_=P["DEN"][:]), [tkn("DEN")], [tkn("DEN")])
        self.ts("dve", P["NR"][:], AR, -1.0, None, ALU.add, None, [tkn("PRE")], [tkn("NR")])
        self.tt("dve", P["T1"][:], P["NR"][:], LR, ALU.mult, [tkn("NR"), "s5p"], [tkn("T1")])
        self.tt("dve", P["T2"][:], AI, LI, ALU.mult, [tkn("PIM"), "s5p"], [tkn("T2")])
        self.tt("dve", P["T1"][:], P["T1"][:], P["T2"][:], ALU.add, [tkn("T1"), tkn("T2")], [tkn("T1")])
        self.tt("dve", P["CRE"][:], P["T1"][:], P["DEN"][:], ALU.mult, [tkn("T1"), tkn("DEN")], [tkn("CRE")])
        self.tt("dve", P["T1"][:], AI, LR, ALU.mult, [tkn("PIM"), "s5p"], [tkn("T1")])
        self.tt("dve", P["T2"][:], P["NR"][:], LI, ALU.mult, [tkn("NR"), "s5p"], [tkn("T2")])
        self.tt("dve", P["T1"][:], P["T1"][:], P["T2"][:], ALU.subtract, [tkn("T1"), tkn("T2")], [tkn("T1")])
        self.tt("dve", P["CIM"][:], P["T1"][:], P["DEN"][:], ALU.mult, [tkn("T1"), tkn("DEN")], [tkn("CIM")])
        cre_b, cim_b = b16(P["CRE"][:]), b16(P["CIM"][:])
        ta, tb = v3(P["TA"][:, 0:512]), v3(P["TB"][:, 0:512])
        self.tt("dve", ta, cre_b, BRE, ALU.mult, [tkn("CRE"), "s5p"], [tkn("TA")])
        self.tt("dve", tb, cim_b, BIM, ALU.mult, [tkn("CIM"), "s5p"], [tkn("TB")])
        self.tt("dve", v3(P["BBR"][:]), ta, tb, ALU.subtract, [tkn("TA"), tkn("TB")], [tkn("BBR")])
        self.tt("dve", ta, cre_b, BIM, ALU.mult, [tkn("CRE"), "s5p"], [tkn("TA")])
        self.tt("dve", tb, cim_b, BRE, ALU.mult, [tkn("CIM"), "s5p"], [tkn("TB")])
        self.tt("dve", v3(P["BBI"][:]), ta, tb, ALU.add, [tkn("TA"), tkn("TB")], [tkn("BBI")])
        BBR3, BBI3 = v3(P["BBR"][:]), v3(P["BBI"][:])

        def cplx_table(dst, dtok, d, ksl, XR, XI, neg_im):
            ds_ = slice(d * 16, d * 16 + 16)
            pr_ = PRE3[:, ds_, ksl].unsqueeze(3).to_broadcast([128, 16, 8, 16])
            pi_ = PIM3[:, ds_, ksl].unsqueeze(3).to_broadcast([128, 16, 8, 16])
            xr_ = XR[:, ds_, :].unsqueeze(2).to_broadcast([128, 16, 8, 16])
            xi_ = XI[:, ds_, :].unsqueeze(2).to_broadcast([128, 16, 8, 16])
            v4 = lambda a: a.rearrange("p (a j h) -> p a j h", j=8, h=16)
            A_, B_, C_ = v4(P["TA"][:]), v4(P["TB"][:]), v4(P["TC"][:])
            rd = [tkn("PRE"), tkn("PIM"), tkn("BBR"), tkn("BBI"), "s5p"]
            dre = dst[:, :, 0, :].rearrange("p a (j h) -> p a j h", h=16)
            dim = dst[:, :, 1, :].rearrange("p a (j h) -> p a j h", h=16)
            self.tt("dve", A_, pr_, xr_, ALU.mult, rd, [tkn("TA")])
            self.tt("pool", B_, pi_, xi_, ALU.mult, rd, [tkn("TB")])
            self.tt("dve", dre, A_, B_, ALU.subtract, [tkn("TA"), tkn("TB")], [dtok])
            self.tt("dve", A_, pr_, xi_, ALU.mult, rd, [tkn("TA")])
            self.tt("pool", B_, pi_, xr_, ALU.mult, rd, [tkn("TB")])
            if neg_im:
                self.stt("dve", dim, A_, -1.0, B_, ALU.mult, ALU.subtract, [tkn("TA"), tkn("TB")], [dtok])
            else:
                self.tt("dve", dim, A_, B_, ALU.add, [tkn("TA"), tkn("TB")], [dtok])

        IDB = consts["identb"]
        for d in range(2):
            wz_sl = slice(14, 6, -1) if d == 0 else slice(7, 15)
            m2_sl = slice(0, 8) if d == 0 else slice(7, None, -1)
            ca_sl = slice(8, 16) if d == 0 else slice(15, 7, -1)
            cplx_table(self.TWZ, "TWZ", d, wz_sl, BBR3, BBI3, False)
            cplx_table(self.TM2, "TM2", d, m2_sl, CRE_, CIM_, True)
            cplx_table(self.TCA[:, d * 16:(d + 1) * 16], "TCA", d, ca_sl, CRE_, CIM_, True)
            psb = PS[3][:].bitcast(BF16)
            for pr in range(16):
                for ri in range(2):
                    self.S.op("pe", lambda e, pr=pr, ri=ri: e.transpose(psb[:, 0:128], self.TWZ[:, pr, ri, :], IDB[:]),
                              ["TWZ", "consts"], ["ps3"])
                    self.copy("act", self.WZT[:, d * 16 + pr, ri, :], psb[:, 0:128], ["ps3"], ["WZT"])
                for g2 in range(2):
                    rs = slice(g2 * 64, g2 * 64 + 64)
                    g = 2 * pr + g2
                    bank = 4 + (g % 2)
                    self.mm(PS[bank][:, 0:128], self.TWZ[rs, pr, 0, :], self.TM2[rs, pr, 0, :], True, False,
                            ["TWZ", "TM2"], ["ps%d" % bank])
                    self.mm(PS[bank][:, 0:128], self.TWZ[rs, pr, 1, :], self.TM2[rs, pr, 1, :], False, True,
                            ["TWZ", "TM2"], ["ps%d" % bank])
                    msk = consts["maskF"] if d == 0 else consts["maskB"]
                    if d == 0:
                        self.tt("dve", self.KS[:, g % 2, :], PS[bank][:, 0:128], msk[:], ALU.mult,
                                ["ps%d" % bank, "consts"], ["KS%d" % (g % 2)])
                        self.copy("act", self.KTOE[:, g, :], self.KS[:, g % 2, :], ["KS%d" % (g % 2)], ["KTOE%d" % g])
                    else:
                        self.tt("dve", self.KS[:, g % 2, :], PS[bank][:, 0:128], msk[:], ALU.mult,
                                ["ps%d" % bank, "consts"], ["KS%d" % (g % 2)])
                        self.tt("dve", self.KS[:, g % 2, :], self.KS[:, g % 2, :], self.KTOE[:, g, :], ALU.add,
                                ["KS%d" % (g % 2), "KTOE%d" % g], ["KS%d" % (g % 2)])
                        self.stt("dve", self.KTOE[:, g, :], consts["eye"][:], consts_dcol(self, PC, g), self.KS[:, g % 2, :],
                                 ALU.mult, ALU.add, ["KS%d" % (g % 2), "consts", "pcd"], ["KTOE%d" % g])


def consts_dcol(self, PC, g):
    return self.DSK[:, g:g + 1]


def _s5_main(self, l, PC, consts, exchange_fn):
    PS = self.PS
    NCH = self.NCH
    P = self.p_
    L2 = self.l2
    PI = float(np.pi)
    MIDX = consts["midx"]
    for g in range(32):
        oc, gi = g // 8, g % 8
        bank = g % 2
        for j in range(8):
            self.mm(PS[bank][:, 0:NCH], self.SELB[:, gi * 8 + j, :], self.UP[:, oc, j, :], j == 0, j == 7,
                    ["SELB", "UP"], ["ps%d" % bank])
        self.copy("act", self.V[:, g, :], PS[bank][:, 0:NCH], ["ps%d" % bank], ["V"])

    def level2(d, pr, segs):
        dp = d * 16 + pr
        for ri, bank in ((0, 2), (1, 3)):
            for g2 in range(2):
                self.mm(PS[bank][g2 * 64:(g2 + 1) * 64, 0:NCH], self.WZT[:, dp, ri, g2 * 64:(g2 + 1) * 64],
                        self.V[:, 2 * pr + g2, :], True, True, ["WZT", "V"], ["ps%d" % bank])
        self.copy("act", L2["ZR"][:], PS[2][:, 0:NCH], ["ps2"], ["l2ZR"])
        self.copy("act", L2["ZI"][:], PS[3][:, 0:NCH], ["ps3"], ["l2ZI"])
        phi = P["PHI"][:, dp:dp + 1]
        rho = P["RHO"][:, dp:dp + 1]
        for (zs, n, init, fsl, fin) in segs:
            ZR, ZI = L2["ZR"][:, zs], L2["ZI"][:, zs]
            if init == "P":
                self.tt("dve", ZR[:, 0:1], ZR[:, 0:1], self.APT[:, 0, pr:pr + 1], ALU.add, ["l2ZR", "APT"], ["l2ZR"])
                self.tt("dve", ZI[:, 0:1], ZI[:, 0:1], self.APT[:, 1, pr:pr + 1], ALU.add, ["l2ZI", "APT"], ["l2ZI"])
            self.ts("dve", L2["XA"][:, 0:n], MIDX[:, 0:n], phi, None, ALU.mult, None, ["consts", "s5_PHI"], ["l2XA"])
            self.sin_turns(L2["SN"][:, 0:n], L2["XA"][:, 0:n], self.l2I[:, 0:n], L2["TN"][:, 0:n], None, "l2XA", "l2SN")
            self.ts("dve", L2["XA"][:, 0:n], MIDX[:, 0:n], phi, 0.25, ALU.mult, ALU.add, ["consts", "s5_PHI"], ["l2XA"])
            self.sin_turns(L2["CS"][:, 0:n], L2["XA"][:, 0:n], self.l2I[:, 0:n], L2["TN"][:, 0:n], None, "l2XA", "l2CS")
            CS, SN = L2["CS"][:, 0:n], L2["SN"][:, 0:n]
            A, B, GR, GI = L2["A"][:, 0:n], L2["B"][:, 0:n], L2["GR"][:, 0:n], L2["GI"][:, 0:n]
            self.tt("dve", A, CS, ZR, ALU.mult, ["l2CS", "l2ZR"], ["l2A"])
            self.tt("pool", B, SN, ZI, ALU.mult, ["l2SN", "l2ZI"], ["l2B"])
            self.tt("dve", GR, A, B, ALU.add, ["l2A", "l2B"], ["l2GR"])
            self.tt("dve", A, CS, ZI, ALU.mult, ["l2CS", "l2ZI"], ["l2A"])
            self.tt("pool", B, SN, ZR, ALU.mult, ["l2SN", "l2ZR"], ["l2B"])
            self.tt("dve", GI, A, B, ALU.subtract, ["l2A", "l2B"], ["l2GI"])
            rb = rho.to_broadcast([128, n])
            self.S.op("dve", lambda e, GR=GR, rb=rb: e.tensor_tensor_scan(out=GR, data0=rb, data1=GR, initial=0.0,
                                                                         op0=ALU.mult, op1=ALU.add),
                      ["l2GR", "s5_RHO"], ["l2GR"])
            self.S.op("dve", lambda e, GI=GI, rb=rb: e.tensor_tensor_scan(out=GI, data0=rb, data1=GI, initial=0.0,
                                                                         op0=ALU.mult, op1=ALU.add),
                      ["l2GI", "s5_RHO"], ["l2GI"])
            ER, EI = L2["ER"][:, 0:n], L2["EI"][:, 0:n]
            self.tt("dve", A, CS, GR, ALU.mult, ["l2CS", "l2GR"], ["l2A"])
            self.tt("pool", B, SN, GI, ALU.mult, ["l2SN", "l2GI"], ["l2B"])
            self.tt("dve", ER, A, B, ALU.subtract, ["l2A", "l2B"], ["l2ER"])
            self.tt("dve", A, SN, GR, ALU.mult, ["l2SN", "l2GR"], ["l2A"])
            self.tt("pool", B, CS, GI, ALU.mult, ["l2CS", "l2GI"], ["l2B"])
            self.tt("dve", EI, A, B, ALU.add, ["l2A", "l2B"], ["l2EI"])
            FR = self.FF[:, d, 0, pr, :][:, fsl]
            FI = self.FF[:, d, 1, pr, :][:, fsl]
            self.copy("act", FR[:, 1:n], ER[:, 0:n - 1], ["l2ER"], ["FF"])
            self.copy("act", FI[:, 1:n], EI[:, 0:n - 1], ["l2EI"], ["FF"])
            if init == "P":
                self.copy("act", FR[:, 0:1], self.PST[:, 0, pr:pr + 1], ["PST"], ["FF"])
                self.copy("act", FI[:, 0:1], self.PST[:, 1, pr:pr + 1], ["PST"], ["FF"])
            if fin:
                self.copy("act", self.EFIN[:, 0, pr:pr + 1], ER[:, n - 1:n], ["l2ER"], ["EFIN"])
                self.copy("act", self.EFIN[:, 1, pr:pr + 1], EI[:, n - 1:n], ["l2EI"], ["EFIN"])

    self.S.op("pool", lambda e: e.memset(self.FF[:], 0.0), (), ["FF"])
    for pr in range(16):
        level2(0, pr, [(slice(0, NCH), NCH, None, slice(0, NCH), True)])
    exchange_fn()
    v3 = lambda a: a.rearrange("p (a h) -> p a h", h=16)
    ARb, AIb = v3(P["PRE"][:])[:, 16:32, 15], v3(P["PIM"][:])[:, 16:32, 15]
    T1, T2 = P["T1"][:, 0:16], P["T2"][:, 0:16]
    self.tt("dve", T1, ARb, self.PST[:, 0, :], ALU.mult, ["s5_PRE", "PST"], ["s5_T1"])
    self.tt("dve", T2, AIb, self.PST[:, 1, :], ALU.mult, ["s5_PIM", "PST"], ["s5_T2"])
    self.tt("dve", self.APT[:, 0, :], T1, T2, ALU.subtract, ["s5_T1", "s5_T2"], ["APT"])
    self.tt("dve", T1, ARb, self.PST[:, 1, :], ALU.mult, ["s5_PRE", "PST"], ["s5_T1"])
    self.tt("dve", T2, AIb, self.PST[:, 0, :], ALU.mult, ["s5_PIM", "PST"], ["s5_T2"])
    self.tt("dve", self.APT[:, 1, :], T1, T2, ALU.add, ["s5_T1", "s5_T2"], ["APT"])
    NCC = NCTX // 8
    for pr in range(16):
        level2(1, pr, [(slice(NCH - 1, NCC - 1, -1), NCH - NCC, "P", slice(NCH - 1, NCC - 1, -1), False),
                       (slice(NCC - 1, None, -1), NCC, None, slice(NCC - 1, None, -1), False)])
    for g in range(32):
        pr, g2 = g // 2, g % 2
        rs = slice(g2 * 64, g2 * 64 + 64)
        bank = 4 + g % 2
        pt = "ps%d" % bank
        self.mm(PS[bank][:, 0:NCH], self.KTOE[:, g, :], self.V[:, g, :], True, False, ["KTOE%d" % g, "V"], [pt])
        for d in range(2):
            for ri in range(2):
                self.mm(PS[bank][:, 0:NCH], self.TCA[rs, d * 16 + pr, ri, :], self.FF[rs, d, ri, pr, :], False,
                        d == 1 and ri == 1, ["TCA", "FF"], [pt])
        self.copy("act", self.YV[:, g, :], PS[bank][:, 0:NCH], [pt], ["UP"])


Model.s5_main = _s5_main


def _exchange(self, name, src_ap, ncols, dst_sb, groups, outtok):
    self.uid += 1
    u = self.uid
    dsrc = self.scratch("xsrc%d" % u, [128, ncols])
    ddst = self.scratch("xdst%d" % u, [256, ncols])
    both = self.sb("xboth%d" % u, [128, 2, ncols], F32)
    t = "x%d" % u
    self.load("xs%d" % u, dsrc, src_ap, [name], [t + "a"], eng="pool")
    self.S.dma("pool", "xc%d" % u,
               lambda e, s: e.collective_compute("AllGather", ALU.bypass, replica_groups=groups, ins=[dsrc],
                                                 outs=[ddst]).then_inc(s, 1),
               [t + "a"], [t + "b"], inc=1)
    self.load("xl%d" % u, both[:], ddst.rearrange("(k p) f -> p k f", p=128), [t + "b"], [t + "c"], eng="pool")
    PSEL = self.consts["pairsel"]
    self.ts("dve", dst_sb, both[:, 0, :], PSEL[:, 0:1], None, ALU.mult, None, [t + "c", "consts"], [outtok])
    self.stt("dve", dst_sb, both[:, 1, :], PSEL[:, 1:2], dst_sb, ALU.mult, ALU.add, [t + "c", "consts", outtok],
             [outtok])


Model.exchange = _exchange


def _load_consts(self, cd):
    C = {}
    for n, w in (("seg", 512), ("gmask", 128), ("eye", 128), ("maskF", 128), ("maskB", 128), ("kv", 16),
                 ("midx", 288), ("negpi", 1), ("pairsel", 2), ("m96", 1)):
        C[n] = self.sb("c_" + n, [128, w], F32)
        self.load("ld_c_" + n, C[n][:], cd[n], [], ["consts"])
    C["identb"] = self.sb("c_identb", [128, 128], BF16)
    self.copy("dve", C["identb"][:], C["eye"][:], ["consts"], ["consts"])
    self.consts = C
    return C


Model.load_consts = _load_consts


def _mod_layer(self, l, w_mod, PC, SC, MOD, MOD1, stage):
    PS = self.PS
    CB = 1024
    i = 0
    for cb in range(6):
        bank = cb % 2
        for k in range(DC):
            st = stage[i % 2]
            tok = "stage%d" % (i % 2)
            self.load("ld_" + tok, st[:, 0:CB], w_mod[l, k * 128:(k + 1) * 128, cb * CB:(cb + 1) * CB], [], [tok])
            for mt in range(8):
                self.mm(PS[bank][:, mt * 2:mt * 2 + 2], st[:, mt * 128:(mt + 1) * 128], SC[:, k, :], k == 0 and mt == 0,
                        k == DC - 1 and mt == 7, [tok, "SC"], ["ps%d" % bank])
            i += 1
        pv = PS[bank][:, 0:16].rearrange("p (m c) -> p m c", c=2)
        bm = PC[:, cb * 8:(cb + 1) * 8].unsqueeze(2).to_broadcast([128, 8, 2])
        self.tt("dve", MOD[:, cb * 8:(cb + 1) * 8, :], pv, bm, ALU.add, ["ps%d" % bank, "pc"], ["mod0"])
    self.ts("dve", MOD1[:], MOD[:], 1.0, None, ALU.add, None, ["mod0"], ["mod"])


Model.mod_layer = _mod_layer


def _load_win(self, l, w_in, WIN, c0, c1, stage):
    i = 0
    for k in range(DC):
        for cc in range(c0, c1, 1024):
            cw = min(1024, c1 - cc)
            st = stage[i % 2]
            tok = "stage%d" % (i % 2)
            self.load("ld_" + tok, st[:, 0:cw], w_in[l, k * 128:(k + 1) * 128, cc:cc + cw], [], [tok])
            self.copy(("pool", "act", "dve")[i % 3], WIN[:, k, cc:cc + cw], st[:, 0:cw], [tok], ["WIN"])
            i += 1


Model.load_win = _load_win

TILES = [(0, 256, 256, 1)] + [(256 + 256 * i, 256, 64, 0) for i in range(8)]


def _stage0(self, l, xs, MOD, MOD1):
    X, H = self.m_X[0], self.m_H
    for it, (t0, nt, rw, mc) in enumerate(TILES):
        hb = it % 2
        self.load("ld_mX", X[:, :, 0:nt], xs[:, t0:t0 + nt].rearrange("(k p) t -> p k t", p=128), ["xs"], ["mX"])
        self.modulate_tile(X, H[hb], nt, "mX", "mH%d" % hb, MOD, MOD1, 0, 8, mc, False)
        c0, ncn = t0 // 8, nt // 8
        for oc in range(4):
            bank = oc % 2
            pt = self.proj_fm(bank, H[hb], "mH%d" % hb, nt, oc * 128)
            self.copy("act", self.UP[:, oc, :, c0:c0 + ncn].rearrange("p j c -> p c j"),
                      self.PS[bank][:, 0:nt].rearrange("p (c j) -> p c j", j=8), [pt], ["UP"])


Model.stage0 = _stage0


def _s5_out(self, l, PC):
    PS = self.PS
    NCH = self.NCH
    GY = self.GY
    for oc in range(4):
        for j in range(8):
            bank = j % 2
            for gi in range(8):
                self.mm(PS[bank][:, 0:NCH], self.SELB[:, gi * 8 + j, :], self.YV[:, oc * 8 + gi, :], gi == 0, gi == 7,
                        ["SELB", "UP"], ["ps%d" % bank])
            self.act(GY[:, oc, j:TT:8], PS[bank][:, 0:NCH], AF.Gelu, ["ps%d" % bank], ["GY"])
    bg = PC_OFF["bglu"][0]
    for (t0, nt, rw, mc) in TILES:
        for mo in range(4):
            bank = 2 + mo % 2
            for k in range(4):
                self.mm(PS[bank][:, 0:nt], self.WGLU[:, k, mo * 128:(mo + 1) * 128], GY[:, k, t0:t0 + nt], k == 0, k == 3,
                        ["WGLU", "GY"], ["ps%d" % bank])
            T = self.s_T[mo % 2]
            self.act(T[:, 0:nt], PS[bank][:, 0:nt], AF.Sigmoid, ["ps%d" % bank, "pc"], ["sT%d" % (mo % 2)],
                     bias=PC[:, bg + mo:bg + mo + 1])
            self.tt("dve", self.S5O[:, mo, t0:t0 + nt], GY[:, mo, t0:t0 + nt], T[:, 0:nt], ALU.mult,
                    ["GY", "sT%d" % (mo % 2)], ["S5O"])


Model.s5_out = _s5_out


def _stage1(self, l, xs, MOD, MOD1, LBT):
    X, H = self.m_X[0], self.m_H
    for h in range(4):
        self.S.op("pool", lambda e, h=h: e.memset(self.S32[h][:], 0.0), (), ["S32_%d" % h])
        self.S.op("pool", lambda e, h=h: e.memset(self.SBF[h][:], 0.0), (), ["SBF_%d" % h])
    for it, (t0, nt, rw, mc) in enumerate(TILES):
        hb = it % 2
        self.load("ld_mX", X[:, :, 0:nt], xs[:, t0:t0 + nt].rearrange("(k p) t -> p k t", p=128), ["xs"], ["mX"])
        self.modulate_tile(X, H[hb], nt, "mX", "mH%d" % hb, MOD, MOD1, 0, 8, mc, False)

        def sink(h, ob, ot, t0=t0, nt=nt):
            self.copy("act", self.OF[:, h, t0:t0 + nt], self.PS[ob][:, 0:nt], [ot], ["OF"])
        self.gla_tile(H[hb], "mH%d" % hb, nt, 0, 512, LBT, self.consts, sink)


Model.stage1 = _stage1


def _stage3(self, l, xs, x1dst, MOD, MOD1, LBT, PC, last):
    PS = self.PS
    X, H = self.m_X[0], self.m_H
    o = PC_OFF
    hgn = PC[:, o["hgn"][0]:o["hgn"][0] + 1]
    order = list(range(len(TILES) - 1, 0, -1)) + [0]
    for ii, it in enumerate(order):
        (t0, nt, rw, mc) = TILES[it]
        hb = ii % 2
        if it == 0:
            for h in range(4):
                self.S.op("pool", lambda e, h=h: e.memset(self.S32[h][:], 0.0), (), ["S32_%d" % h])
                self.S.op("pool", lambda e, h=h: e.memset(self.SBF[h][:], 0.0), (), ["SBF_%d" % h])
        self.load("ld_mX", X[:, :, 0:nt], xs[:, t0:t0 + nt].rearrange("(k p) t -> p k t", p=128), ["xs"], ["mX"])
        self.modulate_tile(X, H[hb], nt, "mX", "mH%d" % hb, MOD, MOD1, 0, 8, mc, True)
        htok = "mH%d" % hb

        def sink(h, ob, ot, t0=t0, nt=nt, H=H[hb], htok=htok):
            OS = self.m_OS[h % 2]
            ost = "mOS%d" % (h % 2)
            self.tt("dve", OS[:, 0:nt], PS[ob][:, 0:nt], self.OF[:, h, t0:t0 + nt][:, ::-1], ALU.add, [ot, "OF"], [ost])
            T = self.m_T[h % 2]
            tt_ = "mT%d" % (h % 2)
            self.act(T[:, 0:nt], OS[:, 0:nt], AF.Square, [ost], [tt_])
            self.mm(PS[2][:, 0:nt], self.ones[:], T[:, 0:nt], True, True, [tt_, "ones"], ["ps2"])
            self.act(T[:, 0:nt], PS[2][:, 0:nt], AF.Sqrt, ["ps2"], [tt_], bias=self.epscol[:, 1:2], scale=1.0 / 128)
            self.S.op("dve", lambda e, T=T: e.reciprocal(out=T[:, 0:nt], in_=T[:, 0:nt]), [tt_], [tt_])
            self.stt("dve", OS[:, 0:nt], OS[:, 0:nt], hgn, T[:, 0:nt], ALU.mult, ALU.mult, [ost, tt_, "pc"], [ost])
            pg = self.proj_fm(3, H, htok, nt, 2560 + h * 128)
            self.act(T[:, 0:nt], PS[3][:, 0:nt], AF.Silu, [pg], [tt_])
            self.tt("dve", self.m_HG[:, h, 0:nt], OS[:, 0:nt], T[:, 0:nt], ALU.mult, [ost, tt_], ["mHG"])
        fcol = 1024
        self.gla_tile(H[hb], htok, nt, 1, fcol, LBT, self.consts, sink)
        for mo in range(DC):
            bank = mo % 2
            pt = "ps%d" % bank
            for k in range(4):
                self.mm(PS[bank][:, 0:nt], self.WOUT[:, k, mo * 128:(mo + 1) * 128], self.S5O[:, k, t0:t0 + nt][:, ::-1],
                        k == 0, False, ["WOUT", "S5O"], [pt])
            for k in range(4):
                self.mm(PS[bank][:, 0:nt], self.WOUT[:, 4 + k, mo * 128:(mo + 1) * 128], self.m_HG[:, k, 0:nt],
                        False, k == 3, ["WOUT", "mHG"], [pt])
            T = self.m_T[mo % 2]
            tt_ = "mT%d" % (mo % 2)
            self.act(T[:, 0:nt], PS[bank][:, 0:nt], AF.Copy, [pt, "mod"], [tt_], scale=MOD[:, 16 + mo, mc:mc + 1])
            self.stt("dve", self.m_Z[:, mo, 0:nt], X[:, mo, 0:nt][:, ::-1], ALPHA, T[:, 0:nt], ALU.mult, ALU.add,
                     ["mX", tt_], ["mZ"])
        self.layer_norm(self.m_Z, self.m_ZSQ, nt, PC[:, o["ln1_g"][0]:o["ln1_g"][0] + 8],
                        PC[:, o["ln1_b"][0]:o["ln1_b"][0] + 8],
                        [X[:, k, 0:nt][:, ::-1] for k in range(DC)], 2, 3, self.lntmp, "mZ", ["mX"] * DC, "m")
        self.load("st_mX", x1dst[:, t0:t0 + nt].rearrange("(k p) t -> p k t", p=128), X[:, :, 0:nt], ["mX"], ["x1"])


Model.stage3 = _stage3


def _alloc_stage13(self):
    NT = 256
    self.WIN = self.sb("WIN", [128, DC, 3072], BF16)
    self.WOUT = self.sb("WOUT", [128, DC, D], BF16)
    self.m_X = [self.sb("mX_0", [128, DC, NT], F32)]
    self.m_H = [self.sb("mH_%d" % i, [128, DC, NT], BF16) for i in range(2)]
    self.VT = self.sb("VT", [128, 2, 512], BF16)
    names = ("SG", "FF", "LF", "BC", "D2", "EX", "QQ", "KK")
    self.g_t = [{n: self.sb("g%s_%d" % (n, i), [128, NT], F32) for n in names} for i in range(2)]
    self.g_b = [{n: self.sb("g%s_%d" % (n, i), [128, NT], BF16) for n in ("QD1", "QD2", "KD")} for i in range(2)]
    self.g_KDT = [self.sb("gKDT_%d" % i, [128, 2, 128], BF16) for i in range(2)]
    self.g_KDZ = [self.sb("gKDZ_%d" % i, [128, 2, 128], BF16) for i in range(2)]
    self.g_DEC = [self.sb("gDEC_%d" % i, [128, 8], F32) for i in range(2)]
    self.g_ATT = [self.sb("gATT_%d" % i, [128, 128], BF16) for i in range(2)]
    self.m_HG = self.sb("mHG", [128, 4, NT], BF16)
    self.m_OS = [self.sb("mOS_%d" % i, [128, NT], F32) for i in range(2)]
    self.m_T = [self.sb("mT_%d" % i, [128, NT], F32) for i in range(2)]
    self.m_Z = self.sb("mZ", [128, DC, NT], F32)
    self.m_ZSQ = self.sb("mZSQ", [128, DC, NT], F32)
    self.lntmp = {n: self.sb("ln_" + n, [128, NT], F32) for n in ("mean", "var", "rstd", "nmr")}


Model.alloc_stage13 = _alloc_stage13


def _alloc_ffn2(self):
    NT = 256
    self.WUP = self.sb("WUP", [128, DC, 2 * DFF], BF16)
    self.WDN = self.sb("WDN", [128, FC, D], BF16)
    self.f_X1 = [self.sb("fX1_0", [128, DC, NT], F32)] * 2
    self.f_H2 = [self.sb("fH2_0", [128, DC, NT], BF16)] * 2
    self.f_CVA = [self.sb("fCVA_%d" % i, [128, NT], F32) for i in range(2)]
    self.f_CVG = [self.sb("fCVG_%d" % i, [128, NT], F32) for i in range(2)]
    self.f_ACT = self.sb("fACT", [128, FC, NT], BF16)
    self.f_T = [self.sb("fT_%d" % i, [128, NT], F32) for i in range(2)]
    self.f_Z = self.sb("fZ", [128, DC, NT], F32)
    self.f_ZSQ = self.f_ACT.rearrange("p a b -> p (a b)")[:, 0:DC * NT * 2].bitcast(F32).rearrange("p (a b) -> p a b", b=NT)
    self.lntmp = {n: self.sb("ln_" + n, [128, NT], F32) for n in ("mean", "var", "rstd", "nmr")}


Model.alloc_ffn2 = _alloc_ffn2


def _alloc_s5a(self):
    self.WZT = self.sb("WZT", [128, 32, 2, 128], BF16)
    self.KTOE = self.sb("KTOE", [128, 32, 128], BF16)
    self.TCA = self.sb("TCA", [128, 32, 2, 128], BF16)
    sm = lambda n, w: self.sb("s5_" + n, [128, w], F32)
    self.p_ = {n: sm(n, 32) for n in ("RHO", "PHI", "T1", "T2")}
    self.p_.update({n: sm(n, 512) for n in ("PRE", "PIM")})


def _alloc_s5a_tmp(self):
    sm = lambda n, w: self.sb("s5_" + n, [128, w], F32)
    self.S5P = self.sb("S5P", [128, 2144], F32)
    self.p_.update({n: sm(n, 32) for n in ("DT", "LRDT", "ANG", "DEN", "NR", "CRE", "CIM")})
    self.p_.update({n: sm(n, 512) for n in ("KL", "KA", "MAGK", "SINK", "COSK", "BBR", "BBI", "X1", "X2")})
    self.p_.update({n: sm(n, 2048) for n in ("TA", "TB", "TC")})
    self.TWZ = self.sb("TWZ", [128, 16, 2, 128], BF16)
    self.TM2 = self.sb("TM2", [128, 16, 2, 128], BF16)
    self.KS = self.sb("KS", [128, 2, 128], F32)
    self.XI = self.sb("s5_XI", [128, 512], F32).bitcast(I32)


def _alloc_s5b(self):
    NCH = self.NCH
    self.V = self.sb("Vs5", [128, 32, NCH], BF16)
    self.GY = self.V.rearrange("p g c -> p (g c)").rearrange("p (o t) -> p o t", o=4)
    self.FF = self.sb("FFs5", [128, 2, 2, 16, NCH], BF16)
    self.SELB = self.sb("SELB", [128, 64, 128], BF16)
    self.l2 = {n: self.sb("l2_" + n, [128, NCH], F32) for n in
               ("ZR", "ZI", "CS", "SN", "XA", "A", "B", "GR", "GI", "ER", "EI", "TN")}
    self.s_T = [self.sb("sT_%d" % i, [128, 256], F32) for i in range(2)]
    self.l2I = self.sb("l2_I", [128, NCH], F32).bitcast(I32)
    self.WGLU = self.sb("WGLU", [128, 4, 512], BF16)


Model.alloc_s5a, Model.alloc_s5a_tmp, Model.alloc_s5b = _alloc_s5a, _alloc_s5a_tmp, _alloc_s5b


def _load_sel(self, sel_dram):
    flat = self.SELB.rearrange("p a b -> p (a b)")
    for i in range(8):
        st = self.stage[i % 2]
        tok = "stage%d" % (i % 2)
        self.load("ld_" + tok, st[:, 0:1024], sel_dram[:, i * 1024:(i + 1) * 1024], [], [tok])
        self.copy("pool", flat[:, i * 1024:(i + 1) * 1024], st[:, 0:1024], [tok], ["SELB"])


Model.load_sel = _load_sel


def build_program(ncores=8, nlayers=DEPTH, debug=None, stop_after=None):
    m = Model(nlayers, debug)
    m.setup_common()
    L = DEPTH
    LW = nlayers
    xT = m.inp("xT", [D, TT])
    w_in = m.inp("w_in", [LW, D, 3072]); w_out = m.inp("w_out", [LW, D, D]); w_glu = m.inp("w_glu", [LW, 512, 512])
    w_up = m.inp("w_up", [LW, D, 2 * DFF]); w_down = m.inp("w_down", [LW, DFF, D]); w_mod = m.inp("w_mod", [LW, D, 6 * D])
    pc = m.inp("pc", [L, 128, PC_N]); dsk = m.inp("dsk", [L, 128, 32]); s5p = m.inp("s5p", [L, 128, 2144])
    cin = m.inp("cin", [128, DC, 2]); hglb = m.inp("hglb", [128, L, 8])
    sel = m.inp("sel", [128, 8192]); selT = m.inp("selT", [128, 8192])
    cd = {n: m.inp("k_" + n, [128, w]) for n, w in (("seg", 512), ("gmask", 128), ("eye", 128), ("maskF", 128),
                                                      ("maskB", 128), ("kv", 16), ("midx", 288), ("negpi", 1),
                                                      ("pairsel", 2), ("m96", 1))}
    out = m.outp("outT", [D, NX])
    XS = m.scratch("XS", [D, TT])
    X1S = m.scratch("X1S", [D, TT])
    groups = [[2 * i, 2 * i + 1] for i in range(ncores // 2)]
    C = m.load_consts(cd)
    PC = m.sb("PC", [128, PC_N]); MOD = m.sb("MOD", [128, 48, 2]); MOD1 = m.sb("MOD1", [128, 48, 2])
    m.DSK = m.sb("DSK", [128, 32])
    SC = m.sb("SC", [128, DC, 2]); LBA = m.sb("LBA", [128, L, 8]); OMA = m.sb("OMA", [128, L, 8])
    LSUM = m.sb("LSUM", [128, 8])
    m.S32 = [m.sb("S32_%d" % h, [128, 128], F32) for h in range(4)]
    m.SBF = [m.sb("SBF_%d" % h, [128, 128], BF16) for h in range(4)]
    m.EFIN = m.sb("EFIN", [128, 2, 16]); m.PST = m.sb("PST", [128, 2, 16]); m.APT = m.sb("APT", [128, 2, 16])
    GSRC = m.sb("GSRC", [128, 512]); GDST = m.sb("GDST", [128, 512])
    m.stage = [m.sb("stage%d" % i, [128, 1408], F32) for i in range(2)]
    m.load("ld_sc", SC[:], cin, [], ["SC0"])
    m.act(SC[:], SC[:], AF.Silu, ["SC0"], ["SC"])
    m.load("ld_lb", LBA[:], hglb, [], ["LBA0"])
    m.act(LBA[:], LBA[:], AF.Exp, ["LBA0"], ["LBA0"])
    m.copy("dve", LSUM[:], LBA[:, 0, :], ["LBA0"], ["LSUM"])
    for l in range(1, L):
        m.tt("dve", LSUM[:], LSUM[:], LBA[:, l, :], ALU.add, ["LSUM", "LBA0"], ["LSUM"])
    m.S.op("dve", lambda e: e.reciprocal(out=LSUM[:], in_=LSUM[:]), ["LSUM"], ["LSUM"])
    for l in range(L):
        m.tt("dve", LBA[:, l, :], LBA[:, l, :], LSUM[:], ALU.mult, ["LSUM", "LBA0"], ["LBA0"])
    m.S.op("dve", lambda e: e.memset(LBA[:, 0, :], 0.0), ["LBA0"], ["LBA0"])
    for l in range(2, L):
        m.tt("dve", LBA[:, l, :], LBA[:, l, :], LBA[:, l - 1, :], ALU.add, ["LBA0"], ["LBA0"])
    m.ts("dve", OMA[:], LBA[:], -1.0, 1.0, ALU.mult, ALU.add, ["LBA0"], ["lbt"])
    m.stage_mark()
    base = m.apos

    def finish(dumps):
        m.S.barrier()
        for i, (nm, ap, shape, dt) in enumerate(dumps):
            o = m.outp("dbg_" + nm, shape, dt)
            m.load("dbgs%d" % i, o, ap, [], ["dbgo%d" % i])
        m.S.barrier()
        m.S.emit()
        return m

    for l in range(nlayers):
        last = l == DEPTH - 1
        m.hard_barrier()
        m.S.new_epoch("_L%d" % l)
        src = xT if l == 0 else XS
        LBT = {"lb": LBA[:, l, :], "oml": OMA[:, l, :]}
        m.load("ld_pc", PC[:], pc[l], [], ["pc"])
        m.load("ld_dsk", m.DSK[:], dsk[l], [], ["pcd"])
        m.mod_layer(l, w_mod, PC, SC, MOD, MOD1, m.stage)
        if stop_after == "mod":
            return finish([("MOD", MOD, [128, 48, 2], F32), ("LBA", LBA, [128, L, 8], F32), ("SC", SC, [128, DC, 2], F32)])
        m.S5O = m.sb("S5O", [128, 4, TT], BF16)
        mark1 = m.apos
        m.alloc_s5a()
        mark2 = m.apos
        m.alloc_s5a_tmp()
        m.s5_prep(l, s5p[l], PC, C)
        if stop_after == "s5prep":
            return finish([("WZT", m.WZT, [128, 32, 2, 128], BF16), ("KTOE", m.KTOE, [128, 32, 128], BF16),
                           ("TCA", m.TCA, [128, 32, 2, 128], BF16), ("PRE", m.p_["PRE"], [128, 512], F32),
                           ("PIM", m.p_["PIM"], [128, 512], F32), ("RHO", m.p_["RHO"], [128, 32], F32),
                           ("PHI", m.p_["PHI"], [128, 32], F32), ("BBR", m.p_["BBR"], [128, 512], F32)])
        m.hard_barrier(); m.apos = mark2
        m.UP = m.sb("UP", [128, 4, 8, m.NCH], BF16)
        m.YV = m.UP.rearrange("p o j c -> p (o j) c")
        mark3 = m.apos
        m.WIN = m.sb("WINu", [128, DC, 512], BF16)
        m.m_X = [m.sb("mX_0", [128, DC, 256], F32)]
        m.m_H = [m.sb("mH_%d" % i, [128, DC, 256], BF16) for i in range(2)]
        m.load_win(l, w_in, m.WIN, 0, 512, m.stage)
        m.stage0(l, src, MOD, MOD1)
        if stop_after == "stage0":
            return finish([("UP", m.UP, [128, 4, 8, m.NCH], BF16)])
        m.hard_barrier(); m.apos = mark3
        m.alloc_s5b()
        m.load_sel(sel)
        m.load_weight("WGLU", w_glu[l], m.WGLU, 4, 512, 512, m.stage)

        def xch():
            m.exchange("EFIN", m.EFIN.rearrange("p a b -> p (a b)"), 32, m.PST.rearrange("p a b -> p (a b)"), groups, "PST")
        m.s5_main(l, PC, C, xch)
        if stop_after == "s5main":
            return finish([("YV", m.YV, [128, 32, m.NCH], BF16), ("FF", m.FF, [128, 2, 2, 16, m.NCH], BF16),
                           ("V", m.V, [128, 32, m.NCH], BF16), ("PST", m.PST, [128, 2, 16], F32),
                           ("EFIN", m.EFIN, [128, 2, 16], F32)])
        m.load_sel(selT)
        m.s5_out(l, PC)
        if stop_after == "s5out":
            return finish([("S5O", m.S5O, [128, 4, TT], BF16), ("GY", m.GY, [128, 4, TT], BF16),
                           ("YV", m.YV, [128, 32, m.NCH], BF16)])
        m.hard_barrier(); m.apos = mark1
        if stop_after == "stage1":
            oe = m.outp("dbg_S5Oearly", [128, 4, TT], BF16)
            m.load("dbgearly", oe, m.S5O, [], ["dbgearly"])
            m.S.barrier()
        m.OF = m.sb("OF", [128, 4, TT], BF16)
        m.alloc_stage13()
        m.load_win(l, w_in, m.WIN, 512, 3072, m.stage)
        m.load_weight("WOUT", w_out[l], m.WOUT, DC, D, 1024, m.stage)
        m.stage1(l, src, MOD, MOD1, LBT)
        if stop_after == "stage1":
            return finish([("OF", m.OF, [128, 4, TT], BF16), ("S5O", m.S5O, [128, 4, TT], BF16)] +
                          [("S32_%d" % h, m.S32[h], [128, 128], F32) for h in range(4)])
        for h in range(4):
            m.copy("dve", GSRC[:, h * 128:(h + 1) * 128], m.S32[h][:], ["S32_%d" % h], ["GSRC"])
        m.exchange("GSRC", GSRC[:], 512, GDST[:], groups, "GSRC_p")
        for h in range(4):
            m.copy("dve", m.S32[h][:], GDST[:, h * 128:(h + 1) * 128], ["GSRC_p"], ["S32_%d" % h])
            m.copy("act", m.SBF[h][:], GDST[:, h * 128:(h + 1) * 128], ["GSRC_p"], ["SBF_%d" % h])
        m.stage3(l, src, X1S, MOD, MOD1, LBT, PC, last)
        m.hard_barrier(); m.apos = base
        m.alloc_ffn2()
        m.load_weight("WUP", w_up[l], m.WUP, DC, 2 * DFF, 1408, m.stage)
        m.load_weight("WDN", w_down[l], m.WDN, FC, D, 1024, m.stage)
        tiles = TILES[1:] if last else TILES
        if last:
            dst_fn = lambda t0, nt: [(out[:, t0 - NCTX:t0 - NCTX + nt].rearrange("(k p) t -> p k t", p=128), "xs")]
        else:
            dst_fn = lambda t0, nt: [(XS[:, t0:t0 + nt].rearrange("(k p) t -> p k t", p=128), "xs")]
        m.ffn_stage(l, X1S, dst_fn, PC, MOD, MOD1, tiles)
        m.hard_barrier(); m.apos = base
    if nlayers < DEPTH:
        dx = m.outp("dbgXS", [D, TT]); d1 = m.outp("dbgX1", [D, TT])
        m.load("dbg_a", dx, XS, ["xs"], ["dbg1"])
        m.load("dbg_b", d1, X1S, ["x1"], ["dbg2"])
    m.S.barrier()
    m.S.emit()
    return m


def _const_tables():
    t = np.arange(512)
    seg = np.broadcast_to((t % 32 != 0).astype(np.float32), (128, 512)).copy()
    s = np.arange(128)
    gmask = ((s[:, None] // 32 == s[None, :] // 32) & (s[None, :] >= s[:, None])).astype(np.float32)
    eye = np.eye(128, dtype=np.float32)
    jj = s // 16
    maskF = (jj[None, :] >= jj[:, None]).astype(np.float32)
    maskB = (jj[None, :] <= jj[:, None]).astype(np.float32)
    kv = np.broadcast_to(np.arange(-7, 9, dtype=np.float32), (128, 16)).copy()
    midx = np.broadcast_to(np.arange(1, 289, dtype=np.float32), (128, 288)).copy()
    negpi = np.full((128, 1), -np.pi, np.float32)
    m96 = (np.arange(128) >= 96).astype(np.float32).reshape(128, 1)
    sel = np.zeros((128, 8, 8, 128), np.float32)
    selT = np.zeros((128, 8, 8, 128), np.float32)
    for gi in range(8):
        for j in range(8):
            for h in range(16):
                sel[gi * 16 + h, gi, j, j * 16 + h] = 1.0
                selT[j * 16 + h, gi, j, gi * 16 + h] = 1.0
    return dict(seg=seg, gmask=gmask, eye=eye, maskF=maskF, maskB=maskB, kv=kv, midx=midx, negpi=negpi, m96=m96), \
        sel.reshape(128, 8192), selT.reshape(128, 8192)


def prepare_core_inputs(inputs, b, s):
    f32 = lambda a: np.ascontiguousarray(np.asarray(a, np.float32))
    L = DEPTH
    x, c, ctx, c_ctx = inputs["x"], inputs["c"], inputs["ctx"], inputs["c_ctx"]
    xl = np.asarray(x[b, s * NX:(s + 1) * NX])
    cl = np.asarray(ctx[b])
    if s == 1:
        xl, cl = xl[::-1], cl[::-1]
    dd = [0, 1] if s == 0 else [1, 0]
    m = {}
    m["xT"] = f32(np.concatenate([cl, xl], axis=0).T)
    w_in = np.asarray(inputs["w_in"])
    if s == 1:
        w_in = np.concatenate([w_in[:, :, 0:512], w_in[:, :, 1024:1536], w_in[:, :, 512:1024], w_in[:, :, 1536:]], axis=2)
    m["w_in"] = f32(w_in)
    for n in ("w_out", "w_glu", "w_up", "w_down", "w_mod"):
        m[n] = f32(inputs[n])
    pc = np.zeros((L, 128, PC_N), np.float32)
    dsk = np.zeros((L, 128, 32), np.float32)
    s5p = np.zeros((L, 128, 2144), np.float32)
    for l in range(L):
        cw = np.asarray(inputs["conv_w"][l])
        taps = [cw[0], cw[1], cw[2]] if s == 0 else [cw[2], cw[1], cw[0]]
        vec = {"b_mod": inputs["b_mod"][l], "ln1_g": inputs["ln1_g"][l], "ln1_b": inputs["ln1_b"][l],
               "ln2_g": inputs["ln2_g"][l], "ln2_b": inputs["ln2_b"][l], "cw0": taps[0], "cw1": taps[1],
               "cw2": taps[2], "cb": inputs["conv_b"][l], "s5d": inputs["s5_d"][l], "bglu": inputs["b_glu"][l],
               "hgn": inputs["hg_norm_w"][l]}
        for n, (o_, k) in PC_OFF.items():
            pc[l, :, o_:o_ + k] = cols(vec[n])
        sd = np.asarray(inputs["s5_d"][l]).reshape(32, 16)
        dsk[l] = np.tile(sd.T, (8, 1))
        for dl in range(2):
            d = dd[dl]
            for nm, off in (("s5_lam_re", 0), ("s5_lam_im", 32)):
                a = np.asarray(inputs[nm][l, d]).reshape(16, 2, 64)
                s5p[l, :, off + dl * 16:off + dl * 16 + 16] = a.transpose(1, 2, 0).reshape(128, 16)
            ld = np.asarray(inputs["s5_log_dt"][l, d]).reshape(16, 2)
            s5p[l, :, 64 + dl * 16:64 + dl * 16 + 16] = np.repeat(ld.T[:, None, :], 64, axis=1).reshape(128, 16)
            for nm, off in (("s5_b_re", 96), ("s5_b_im", 608)):
                a = np.asarray(inputs[nm][l, d]).reshape(16, 2, 64, 16)
                s5p[l, :, off + dl * 256:off + dl * 256 + 256] = a.transpose(1, 2, 0, 3).reshape(128, 256)
            for nm, off in (("s5_c_re", 1120), ("s5_c_im", 1632)):
                a = np.asarray(inputs[nm][l, d]).reshape(16, 2, 16, 64)
                s5p[l, :, off + dl * 256:off + dl * 256 + 256] = a.transpose(1, 3, 0, 2).reshape(128, 256)
    m["pc"], m["dsk"], m["s5p"] = pc, dsk, s5p
    m["cin"] = f32(np.stack([cols(c[b]), cols(c_ctx)], axis=-1))
    hg = np.asarray(inputs["hg_lb"])
    hglb = np.zeros((128, L, 8), np.float32)
    for l in range(L):
        for dl in range(2):
            hglb[:, l, dl * 4:dl * 4 + 4] = cols(hg[l, dd[dl]])
    m["hglb"] = hglb
    ct, sel, selT = _const_tables()
    for n, v in ct.items():
        m["k_" + n] = v
    m["k_pairsel"] = np.broadcast_to(np.array([0.0, 1.0] if s == 0 else [1.0, 0.0], np.float32), (128, 2)).copy()
    m["sel"], m["selT"] = sel, selT
    return m


_PROG = {}


def kernel(**inputs):
    ncores = 8
    if "p" not in _PROG:
        _PROG["p"] = build_program(ncores, DEPTH)
    prog = _PROG["p"]
    in_maps = [prepare_core_inputs(inputs, cid // 2, cid % 2) for cid in range(ncores)]
    res = run_bass_kernel_spmd(prog.nc, in_maps, core_ids=list(range(ncores)))
    B = 4
    outp = np.zeros((B, 2 * NX, D), np.float32)
    for cid in range(ncores):
        b, s = cid // 2, cid % 2
        o = np.asarray(res.results[cid]["outT"]).T
        if s == 1:
            o = o[::-1]
        outp[b, s * NX:(s + 1) * NX] = o
    return outp
```

```python
import numpy as np
from contextlib import ExitStack
import concourse.bass as bass
import concourse.mybir as mybir
from concourse.bass_utils import run_bass_kernel_spmd

F32 = mybir.dt.float32
F32R = mybir.dt.float32r
BF16 = mybir.dt.bfloat16
I32 = mybir.dt.int32
AF = mybir.ActivationFunctionType
ALU = mybir.AluOpType

D = 1024
DC = 8
NCTX = 256
NX = 2048
TT = NCTX + NX
DEPTH = 4
DFF = 2816
FC = DFF // 128
ALPHA = (2 * DEPTH) ** 0.25
LN_EPS = 1e-5
RMS_EPS = 1e-6

SELF_SYNC = True
COMPUTE = ("pe", "dve", "act", "pool")


class Sched:
    def __init__(self, nc, es):
        self.nc = nc
        self.es = es
        self.ops = {e: [] for e in ("pe", "dve", "act", "pool", "sp")}
        self.count = {e: 0 for e in COMPUTE}
        self.sems = {}
        self.dma_count = {}
        self.last_write = {}
        self.readers = {}
        self.waited = {e: {} for e in self.ops}
        self.nops = 0
        self.epoch = ""
        self.final_sigs = []

    def sem(self, name):
        if name not in self.sems:
            self.sems[name] = self.es.enter_context(self.nc.semaphore(name))
        return self.sems[name]

    def _deps(self, reads, writes):
        deps = set()
        for t in reads:
            if t in self.last_write:
                deps.add(self.last_write[t])
        for t in writes:
            if t in self.last_write:
                deps.add(self.last_write[t])
            for r in self.readers.get(t, ()):
                deps.add(r)
        return deps

    def _record(self, sig, reads, writes):
        for t in writes:
            self.last_write[t] = sig
            self.readers[t] = []
        for t in reads:
            self.readers.setdefault(t, []).append(sig)

    def _waits(self, eng, deps, own=None):
        waits = []
        for (s, v) in sorted(deps):
            if s == own and (not SELF_SYNC or eng == "pe"):
                continue
            if self.waited[eng].get(s, 0) < v:
                waits.append((s, v))
                self.waited[eng][s] = v
        return waits

    def op(self, eng, fn, reads=(), writes=()):
        reads, writes = tuple(reads), tuple(writes)
        deps = self._deps(reads, writes)
        self.count[eng] += 1
        sname = "c_" + eng + self.epoch
        self.sem(sname)
        sig = (sname, self.count[eng])
        self.ops[eng].append((self._waits(eng, deps, sname), fn, sname, False))
        self._record(sig, reads, writes)
        self.nops += 1

    def dma(self, eng, semname, fn, reads=(), writes=(), ndma=1, inc=16):
        reads, writes = tuple(reads), tuple(writes)
        deps = self._deps(reads, writes)
        self.sem(semname)
        self.dma_count[semname] = self.dma_count.get(semname, 0) + inc * ndma
        sig = (semname, self.dma_count[semname])
        self.ops[eng].append((self._waits(eng, deps), fn, semname, True))
        self._record(sig, reads, writes)
        self.nops += 1

    def barrier(self):
        sigs = [("c_%s%s" % (e, self.epoch), self.count[e]) for e in COMPUTE if self.count[e] > 0]
        sigs += [(k, v) for k, v in self.dma_count.items()]
        sigs += list(self.final_sigs)
        for eng in self.ops:
            w = []
            for (s_, v) in sorted(sigs):
                if self.waited[eng].get(s_, 0) < v:
                    w.append((s_, v))
                    self.waited[eng][s_] = v
            self.ops[eng].append((w, None, None, False))
        self.last_write = {}
        self.readers = {}

    def new_epoch(self, tag):
        self.final_sigs = [("c_%s%s" % (e, self.epoch), self.count[e]) for e in COMPUTE if self.count[e] > 0]
        self.epoch = tag
        self.count = {e: 0 for e in COMPUTE}

    def wait_all(self, eng, tokens):
        deps = self._deps(tuple(tokens), ())
        self.ops[eng].append((self._waits(eng, deps), None, None, False))

    def emit(self):
        nc = self.nc
        with nc.Block() as block:
            def mk(ename):
                def body(e):
                    for (waits, fn, sname, is_dma) in self.ops[ename]:
                        for (s, v) in waits:
                            e.wait_ge(self.sems[s], v)
                        if fn is None:
                            continue
                        if is_dma:
                            fn(e, self.sems[sname])
                        else:
                            fn(e).then_inc(self.sems[sname], 1)
                return body
            block.tensor(mk("pe"))
            block.vector(mk("dve"))
            block.scalar(mk("act"))
            block.gpsimd(mk("pool"))
            block.sync(mk("sp"))


class Builder:
    def __init__(self, nlayers=DEPTH, debug=None):
        self.nl = nlayers
        self.debug = debug or {}
        self.nc = bass.Bass("TRN2", target_bir_lowering=False)
        self.es = ExitStack()
        self.S = Sched(self.nc, self.es)
        self.din = {}
        self.dout = {}
        self.uid = 0

    def inp(self, name, shape, dt=F32):
        t = self.nc.dram_tensor(name, list(shape), dt, kind="ExternalInput").ap()
        self.din[name] = t
        return t

    def outp(self, name, shape, dt=F32):
        t = self.nc.dram_tensor(name, list(shape), dt, kind="ExternalOutput").ap()
        self.dout[name] = t
        return t

    def scratch(self, name, shape, dt=F32):
        return self.nc.dram_tensor(name, list(shape), dt, kind="Internal").ap()

    ARENA = 106000

    def sb(self, name, shape, dt=F32):
        if not hasattr(self, "arena"):
            self.arena = self.es.enter_context(self.nc.sbuf_tensor("arena", [128, self.ARENA], BF16))
            self.apos = 0
            self.amark = 0
        assert shape[0] == 128
        n = int(np.prod(shape[1:]))
        nb = n * (2 if dt == F32 else 1)
        nb = (nb + 15) // 16 * 16
        assert self.apos + nb <= self.ARENA, "SBUF arena overflow at %s (%d + %d)" % (name, self.apos, nb)
        ap = self.arena[:, self.apos:self.apos + n * (2 if dt == F32 else 1)]
        self.apos += nb
        if dt == F32:
            ap = ap.bitcast(F32)
        if len(shape) == 3:
            ap = ap.rearrange("p (a b) -> p a b", b=shape[2])
        elif len(shape) == 4:
            ap = ap.rearrange("p (a b c) -> p a b c", b=shape[2], c=shape[3])
        elif len(shape) == 5:
            ap = ap.rearrange("p (a b c d) -> p a b c d", b=shape[2], c=shape[3], d=shape[4])
        return ap

    def hard_barrier(self):
        self.S.barrier()
        if not hasattr(self, "_hb_dram"):
            self._hb_dram = self.scratch("hb_scratch", [128, 16])
            self._hb_n = 0
        self._hb_n += 1
        self.load("hb_sem", self._hb_dram, self.ones[:, 0:16], [], ["hb%d" % self._hb_n])
        self.S.barrier()

    def stage_mark(self):
        self.amark = self.apos

    def stage_reset(self):
        self.S.barrier()
        self.apos = self.amark

    def ps(self, name, shape, dt=F32):
        return self.es.enter_context(self.nc.psum_tensor(name, list(shape), dt))

    def mm(self, out, lhsT, rhs, start, stop, reads, writes):
        self.S.op("pe", lambda e: e.matmul(out, lhsT=lhsT, rhs=rhs, start=start, stop=stop), reads, writes)

    def act(self, out, in_, func, reads, writes, bias=None, scale=None):
        kw = {}
        if bias is not None:
            kw["bias"] = bias
        if scale is not None:
            kw["scale"] = scale
        self.S.op("act", lambda e: e.activation(out=out, in_=in_, func=func, **kw), reads, writes)

    def tt(self, eng, out, in0, in1, op, reads, writes):
        self.S.op(eng, lambda e: e.tensor_tensor(out=out, in0=in0, in1=in1, op=op), reads, writes)

    def ts(self, eng, out, in0, s1, s2, op0, op1, reads, writes):
        if s2 is None:
            self.S.op(eng, lambda e: e.tensor_scalar(out=out, in0=in0, scalar1=s1, scalar2=None, op0=op0), reads, writes)
        else:
            self.S.op(eng, lambda e: e.tensor_scalar(out=out, in0=in0, scalar1=s1, scalar2=s2, op0=op0, op1=op1), reads, writes)

    def stt(self, eng, out, in0, scalar, in1, op0, op1, reads, writes):
        self.S.op(eng, lambda e: e.scalar_tensor_tensor(out=out, in0=in0, scalar=scalar, in1=in1, op0=op0, op1=op1), reads, writes)

    def copy(self, eng, out, in_, reads, writes):
        if eng == "act":
            self.S.op("act", lambda e: e.copy(out=out, in_=in_), reads, writes)
        else:
            self.S.op(eng, lambda e: e.tensor_copy(out=out, in_=in_), reads, writes)


    def sin_turns(self, out, T, TI, TN, rd_tok, ttok, otok):
        self.copy("dve", TI, T, [ttok], [ttok + "i"])
        self.copy("dve", TN, TI, [ttok + "i"], [ttok + "n"])
        self.tt("dve", T, T, TN, ALU.subtract, [ttok, ttok + "n"], [ttok])
        self.act(out, T, AF.Sin, [ttok], [otok], scale=2.0 * float(np.pi))

    def load(self, semname, out, in_, reads, writes, eng="sp"):
        self.S.dma(eng, semname, lambda e, s: e.dma_start(out=out, in_=in_).then_inc(s, 16), reads, writes)

    def dbg(self, name, ap_sb, shape, reads):
        if name not in self.debug:
            return
        o = self.outp("dbg_" + name, shape)
        self.uid += 1
        self.load("dbg%d" % self.uid, o, ap_sb, reads, ["dbgout_" + name])
        self.dbg_tokens.append("dbgout_" + name)


def _pc_layout():
    off = {}
    o = 0
    for name, n in (("b_mod", 48), ("ln1_g", 8), ("ln1_b", 8), ("ln2_g", 8), ("ln2_b", 8),
                    ("cw0", 44), ("cw1", 44), ("cw2", 44), ("cb", 44), ("s5d", 4), ("bglu", 4),
                    ("hgn", 1)):
        off[name] = (o, n)
        o += n
    return off, o


PC_OFF, PC_N = _pc_layout()


def cols(v):
    v = np.asarray(v, np.float32)
    n = v.shape[0] // 128
    return np.ascontiguousarray(v.reshape(n, 128).T)


class Model(Builder):
    def setup_common(self):
        nc = self.nc
        self.PS = [self.ps("psb%d" % b, [128, 512]) for b in range(8)]
        self.ones = self.sb("ones", [128, 128], F32)
        self.S.op("dve", lambda e: e.memset(self.ones[:], 1.0), (), ["ones"])
        self.epscol = self.sb("epscol", [128, 2], F32)
        self.S.op("dve", lambda e: e.memset(self.epscol[:, 0:1], LN_EPS), (), ["ones"])
        self.S.op("dve", lambda e: e.memset(self.epscol[:, 1:2], RMS_EPS), (), ["ones"])
        self.dbg_tokens = []

    def load_weight(self, name, src, dst, K, N, nstage_cols, stage):
        nst = len(stage)
        i = getattr(self, "_stg_i", 0)
        for k in range(K):
            for c0 in range(0, N, nstage_cols):
                cw = min(nstage_cols, N - c0)
                st = stage[i % nst]
                tok = "stage%d" % (i % nst)
                self.load("ld_" + tok, st[:, 0:cw], src[k * 128:(k + 1) * 128, c0:c0 + cw], [], [tok],
                          eng=("sp" if i % 2 == 0 else "sp"))
                self.copy(("pool", "act", "dve")[i % 3], dst[:, k, c0:c0 + cw], st[:, 0:cw], [tok], [name])
                i += 1
        self._stg_i = i

    def layer_norm(self, Z, ZSQ, nt, gcol, bcol, out_aps, ps1, ps2, tmp, ztok, outtoks, tag, zsqtok=None):
        PS = self.PS
        t1, t2 = "ps%d" % ps1, "ps%d" % ps2
        zq = (lambda k: zsqtok) if zsqtok else (lambda k: tag + "zsq%d" % k)
        for k in range(DC):
            self.mm(PS[ps1][:, 0:nt], self.ones[:], Z[:, k, 0:nt], k == 0, k == DC - 1, [ztok, "ones"], [t1])
        for k in range(DC):
            self.act(ZSQ[:, k, 0:nt], Z[:, k, 0:nt], AF.Square, [ztok], [zq(k)])
            self.mm(PS[ps2][:, 0:nt], self.ones[:], ZSQ[:, k, 0:nt], k == 0, k == DC - 1,
                    [zq(k), "ones"], [t2])
        mean, var, rstd, nmr = tmp["mean"], tmp["var"], tmp["rstd"], tmp["nmr"]
        mt = tag + "lnstat"
        self.act(mean[:, 0:nt], PS[ps1][:, 0:nt], AF.Copy, [t1], [mt + "m"], scale=1.0 / D)
        self.tt("dve", var[:, 0:nt], mean[:, 0:nt], mean[:, 0:nt], ALU.mult, [mt + "m"], [mt + "v"])
        self.stt("dve", var[:, 0:nt], PS[ps2][:, 0:nt], 1.0 / D, var[:, 0:nt], ALU.mult, ALU.subtract,
                 [t2, mt + "v"], [mt + "v"])
        self.act(var[:, 0:nt], var[:, 0:nt], AF.Sqrt, [mt + "v"], [mt + "v"], bias=self.epscol[:, 0:1])
        self.S.op("dve", lambda e: e.reciprocal(out=rstd[:, 0:nt], in_=var[:, 0:nt]), [mt + "v"], [mt + "r"])
        self.stt("dve", nmr[:, 0:nt], mean[:, 0:nt], -1.0, rstd[:, 0:nt], ALU.mult, ALU.mult,
                 [mt + "m", mt + "r"], [mt + "n"])
        for k in range(DC):
            eng = "dve" if k % 2 == 0 else "pool"
            self.tt(eng, ZSQ[:, k, 0:nt], Z[:, k, 0:nt], rstd[:, 0:nt], ALU.mult, [ztok, mt + "r"], [zq(k)])
            self.tt(eng, ZSQ[:, k, 0:nt], ZSQ[:, k, 0:nt], nmr[:, 0:nt], ALU.add, [zq(k), mt + "n"],
                    [zq(k)])
            self.act(out_aps[k], ZSQ[:, k, 0:nt], AF.Identity, [zq(k)], [outtoks[k]],
                     bias=bcol[:, k:k + 1], scale=gcol[:, k:k + 1])

    def alloc_ffn(self):
        self.WUP = self.sb("WUP", [128, DC, 2 * DFF], BF16)
        self.WDN = self.sb("WDN", [128, FC, D], BF16)
        self.stage = [self.sb("stage%d" % i, [128, 1408], F32) for i in range(2)]
        NT = 256
        self.f_X1 = [self.sb("fX1_%d" % i, [128, DC, NT], F32) for i in range(2)]
        self.f_H2 = [self.sb("fH2_%d" % i, [128, DC, NT], BF16) for i in range(2)]
        self.f_CVA = [self.sb("fCVA_%d" % i, [128, NT], F32) for i in range(2)]
        self.f_CVG = [self.sb("fCVG_%d" % i, [128, NT], F32) for i in range(2)]
        self.f_ACT = self.sb("fACT", [128, FC, NT], BF16)
        self.f_T = [self.sb("fT_%d" % i, [128, NT], F32) for i in range(2)]
        self.f_Z = self.sb("fZ", [128, DC, NT], F32)
        self.f_ZSQ = self.sb("fZSQ", [128, DC, NT], F32)
        self.lntmp = {n: self.sb("ln_" + n, [128, 256], F32) for n in ("mean", "var", "rstd", "nmr")}

    def ffn_stage(self, l, src, dst_fn, PC, MOD, MOD1, tiles):
        PS = self.PS
        o = PC_OFF
        for it, (t0, nt, rw, mc) in enumerate(tiles):
            pb = it % 2
            X1, H2 = self.f_X1[pb], self.f_H2[pb]
            xtok, htok = "fX1_%d" % pb, "fH2_%d" % pb
            self.load("ld_" + xtok, X1[:, :, 0:nt], src[:, t0:t0 + nt].rearrange("(k p) t -> p k t", p=128),
                      ["xs_%d" % t0], [xtok])
            for k in range(DC):
                self.act(H2[:, k, 0:nt], X1[:, k, 0:nt], AF.Identity, [xtok, "mod"], [htok],
                         bias=MOD[:, 24 + k, mc:mc + 1], scale=MOD1[:, 32 + k, mc:mc + 1])
            nr = nt // rw
            for m in range(FC):
                pp = m % 2
                for half, (mm_, cv, cvt, bank) in enumerate(((m, self.f_CVA[pp], "fCVA_%d" % pp, 0 + pp),
                                                              (m + FC, self.f_CVG[pp], "fCVG_%d" % pp, 2 + pp))):
                    pt = "ps%d" % bank
                    for k in range(DC):
                        self.mm(PS[bank][:, 0:nt], self.WUP[:, k, mm_ * 128:(mm_ + 1) * 128], H2[:, k, 0:nt],
                                k == 0, k == DC - 1, ["WUP", htok], [pt])
                    c0 = o["cw0"][0] + mm_
                    c1 = o["cw1"][0] + mm_
                    c2 = o["cw2"][0] + mm_
                    cb = o["cb"][0] + mm_
                    self.act(cv[:, 0:nt], PS[bank][:, 0:nt], AF.Identity, [pt, "pc"], [cvt],
                             bias=PC[:, cb:cb + 1], scale=PC[:, c1:c1 + 1])
                    pv = PS[bank][:, 0:nt].rearrange("p (r w) -> p r w", w=rw)
                    cvv = cv[:, 0:nt].rearrange("p (r w) -> p r w", w=rw)
                    self.stt("dve", cvv[:, :, 1:rw], pv[:, :, 0:rw - 1], PC[:, c0:c0 + 1], cvv[:, :, 1:rw],
                             ALU.mult, ALU.add, [pt, cvt, "pc"], [cvt])
                    self.stt("dve", cvv[:, :, 0:rw - 1], pv[:, :, 1:rw], PC[:, c2:c2 + 1], cvv[:, :, 0:rw - 1],
                             ALU.mult, ALU.add, [pt, cvt, "pc"], [cvt])
                T = self.f_T[pp]
                self.act(T[:, 0:nt], self.f_CVA[pp][:, 0:nt], AF.Silu, ["fCVA_%d" % pp], ["fT_%d" % pp])
                self.tt("pool", self.f_ACT[:, m, 0:nt], T[:, 0:nt], self.f_CVG[pp][:, 0:nt], ALU.mult,
                        ["fT_%d" % pp, "fCVG_%d" % pp], ["fACT"])
            for mo in range(DC):
                bank = 4 + mo % 2
                pt = "ps%d" % bank
                for k in range(FC):
                    self.mm(PS[bank][:, 0:nt], self.WDN[:, k, mo * 128:(mo + 1) * 128], self.f_ACT[:, k, 0:nt],
                            k == 0, k == FC - 1, ["WDN", "fACT"], [pt])
                T = self.f_T[mo % 2]
                self.act(T[:, 0:nt], PS[bank][:, 0:nt], AF.Copy, [pt, "mod"], ["fT_%d" % (mo % 2)],
                         scale=MOD[:, 40 + mo, mc:mc + 1])
                self.stt("dve", self.f_Z[:, mo, 0:nt], X1[:, mo, 0:nt], ALPHA, T[:, 0:nt], ALU.mult, ALU.add,
                         [xtok, "fT_%d" % (mo % 2)], ["fZ"])
            OUT = X1
            otok = xtok
            self.layer_norm(self.f_Z, self.f_ZSQ, nt, PC[:, o["ln2_g"][0]:o["ln2_g"][0] + 8],
                            PC[:, o["ln2_b"][0]:o["ln2_b"][0] + 8],
                            [OUT[:, k, 0:nt] for k in range(DC)], 6, 7, self.lntmp, "fZ", [otok] * DC, "f", zsqtok="fACT")
            for (dap, wtok) in dst_fn(t0, nt):
                self.load("st_" + otok, dap, OUT[:, :, 0:nt], [otok], [wtok])

    def alloc_mixer(self):
        NT = 512
        self.WIN = self.sb("WIN", [128, DC, 3072], BF16)
        self.WOUT = self.sb("WOUT", [128, DC, D], BF16)
        self.WGLU = self.sb("WGLU", [128, 4, 512], BF16)
        self.stage = [self.sb("stage%d" % i, [128, 1024], F32) for i in range(2)]
        self.m_X = [self.sb("mX_%d" % i, [128, DC, NT], F32) for i in range(2)]
        self.m_H = [self.sb("mH_%d" % i, [128, DC, NT], BF16) for i in range(2)]
        self.OF = self.sb("OF", [128, 4, TT], BF16)
        self.S5O = self.sb("S5O", [128, 4, TT], BF16)
        self.VT = self.sb("VT", [128, 4, 512], BF16)
        names = ("SG", "FF", "LF", "BC", "D2", "EX", "QQ", "KK")
        self.g_t = [{n: self.sb("g%s_%d" % (n, i), [128, NT], F32) for n in names} for i in range(2)]
        self.g_b = [{n: self.sb("g%s_%d" % (n, i), [128, NT], BF16) for n in ("QD1", "QD2", "KD")} for i in range(2)]
        self.g_KDT = [self.sb("gKDT_%d" % i, [128, 4, 128], BF16) for i in range(2)]
        self.g_DEC = [self.sb("gDEC_%d" % i, [128, 16], F32) for i in range(2)]
        self.g_ATT = [self.sb("gATT_%d" % i, [128, 128], BF16) for i in range(2)]
        self.S32 = [self.sb("S32_%d" % h, [128, 128], F32) for h in range(4)]
        self.SBF = [self.sb("SBF_%d" % h, [128, 128], BF16) for h in range(4)]
        self.m_HG = self.sb("mHG", [128, 4, NT], BF16)
        self.m_OS = [self.sb("mOS_%d" % i, [128, NT], F32) for i in range(2)]
        self.m_T = [self.sb("mT_%d" % i, [128, NT], F32) for i in range(2)]
        self.m_Z = self.sb("mZ", [128, DC, NT], F32)
        self.m_ZSQ = self.sb("mZSQ", [128, DC, NT], F32)
        self.lntmp = {n: self.sb("ln_" + n, [128, NT], F32) for n in ("mean", "var", "rstd", "nmr")}

    def modulate_tile(self, X, H, nt, xtok, htok, MOD, MOD1, sh0, sc0, mc, rev):
        for k in range(DC):
            out = H[:, k, nt - 1::-1] if False else H[:, k, 0:nt]
            src = X[:, k, 0:nt]
            if rev:
                src = X[:, k, 0:nt][:, ::-1]
            self.act(out, src, AF.Identity, [xtok, "mod"], [htok],
                     bias=MOD[:, sh0 + k, mc:mc + 1], scale=MOD1[:, sc0 + k, mc:mc + 1])

    def proj_fm(self, bank, H, htok, nt, col0):
        pt = "ps%d" % bank
        for k in range(DC):
            self.mm(self.PS[bank][:, 0:nt], self.WIN[:, k, col0:col0 + 128], H[:, k, 0:nt], k == 0, k == DC - 1,
                    ["WIN", htok], [pt])
        return pt

    def gla_tile(self, H, htok, nt, d, fcol0, LBT, consts, o_sink):
        PS = self.PS
        nb = nt // 128
        nch = nt // 32
        VCOL, QCOL = 1536, 2048
        SEG, GMASK, IDB = consts["seg"], consts["gmask"], consts["identb"]
        for b in range(nb):
            for k in range(DC):
                self.mm(PS[2][:, 0:512], H[:, k, b * 128:(b + 1) * 128], self.WIN[:, k, VCOL:VCOL + 512],
                        k == 0, k == DC - 1, ["WIN", htok], ["ps2"])
            self.copy("act", self.VT[:, b, :], PS[2][:, 0:512], ["ps2"], ["VT"])
        psb = PS[3][:].bitcast(BF16)
        DSB = (5, 2)
        def prep_gen(h):
            hp = h % 2
            T, B = self.g_t[hp], self.g_b[hp]
            tk = lambda n: "g%s_%d" % (n, hp)
            fb, qb = (0, 1) if hp == 0 else (1, 0)
            pf = self.proj_fm(fb, H, htok, nt, fcol0 + h * 128)
            yield
            self.act(T["SG"][:, 0:nt], PS[fb][:, 0:nt], AF.Sigmoid, [pf], [tk("SG")])
            yield
            pq = self.proj_fm(qb, H, htok, nt, QCOL + h * 128)
            yield
            self.act(T["QQ"][:, 0:nt], PS[qb][:, 0:nt], AF.Silu, [pq], [tk("QQ")])
            yield
            lbc = d * 4 + h
            self.ts("dve", T["FF"][:, 0:nt], T["SG"][:, 0:nt], LBT["oml"][:, lbc:lbc + 1], LBT["lb"][:, lbc:lbc + 1],
                    ALU.mult, ALU.add, [tk("SG"), "lbt"], [tk("FF")])
            yield
            self.act(T["LF"][:, 0:nt], T["FF"][:, 0:nt], AF.Ln, [tk("FF")], [tk("LF")])
            yield
            self.S.op("dve", lambda e, T=T: e.tensor_tensor_scan(out=T["BC"][:, 0:nt], data0=SEG[:, 0:nt],
                                                                 data1=T["LF"][:, 0:nt], initial=0.0,
                                                                 op0=ALU.mult, op1=ALU.add),
                      [tk("LF"), "consts"], [tk("BC")])
            yield
            self.ts("pool", T["KK"][:, 0:nt], T["FF"][:, 0:nt], -1.0, 1.0, ALU.mult, ALU.add, [tk("FF")], [tk("KK")])
            yield
            bc3 = T["BC"][:, 0:nt].rearrange("p (c w) -> p c w", w=32)
            d23 = T["D2"][:, 0:nt].rearrange("p (c w) -> p c w", w=32)
            self.tt("dve", d23, bc3, bc3[:, :, 31:32].to_broadcast([128, nch, 32]), ALU.subtract, [tk("BC")], [tk("D2")])
            yield
            DEC = self.g_DEC[hp]
            self.act(DEC[:, 0:nch], T["BC"][:, 31:nt:32], AF.Exp, [tk("BC")], ["gDEC_%d" % hp])
            yield
            self.act(T["EX"][:, 0:nt], T["D2"][:, 0:nt], AF.Exp, [tk("D2")], [tk("EX")])
            yield
            self.tt("dve", B["QD2"][:, 0:nt], T["QQ"][:, 0:nt], T["EX"][:, 0:nt], ALU.mult, [tk("QQ"), tk("EX")], [tk("QD2")])
            yield
            self.act(T["EX"][:, 0:nt], T["D2"][:, 0:nt], AF.Exp, [tk("D2")], [tk("EX")], scale=-1.0)
            yield
            self.tt("pool", B["KD"][:, 0:nt], T["KK"][:, 0:nt], T["EX"][:, 0:nt], ALU.mult, [tk("KK"), tk("EX")], [tk("KD")])
            yield
            self.act(T["EX"][:, 0:nt], T["BC"][:, 0:nt], AF.Exp, [tk("BC")], [tk("EX")])
            yield
            self.tt("dve", B["QD1"][:, 0:nt], T["QQ"][:, 0:nt], T["EX"][:, 0:nt], ALU.mult, [tk("QQ"), tk("EX")], [tk("QD1")])
            yield

        for h0 in (0, 2):
            gens = [prep_gen(h0), prep_gen(h0 + 1)]
            while gens:
                for g_ in list(gens):
                    try:
                        next(g_)
                    except StopIteration:
                        gens.remove(g_)
            for b in range(nb):
                bs = slice(b * 128, (b + 1) * 128)
                for h in (h0, h0 + 1):
                    hp = h % 2
                    B = self.g_b[hp]
                    tk = lambda n: "g%s_%d" % (n, hp)
                    KDT = self.g_KDT[hp]
                    self.S.op("pe", lambda e, bs=bs, B=B: e.transpose(psb[:, 0:128], B["KD"][:, bs], IDB[:]),
                              [tk("KD"), "consts"], ["ps3"])
                    self.copy("act", KDT[:, b, :], psb[:, 0:128], ["ps3"], ["gKDT_%d" % hp])
                    KDZ = self.g_KDZ[hp]
                    self.ts("dve", KDZ[64:128, b, :], psb[64:128, 0:128], consts["m96"][64:128, 0:1], None, ALU.mult, None,
                            ["ps3", "consts"], ["gKDT_%d" % hp])
                    self.mm(PS[4][:, 0:128], B["KD"][:, bs], B["QD2"][:, bs], True, True, [tk("KD"), tk("QD2")], ["ps4"])
                    ATT = self.g_ATT[hp]
                    self.tt("dve", ATT[:], PS[4][:, 0:128], GMASK[:], ALU.mult, ["ps4", "consts"], ["gATT_%d" % hp])
                for ci in range(4):
                    cs = slice(b * 128 + ci * 32, b * 128 + ci * 32 + 32)
                    rs = slice(ci * 32, ci * 32 + 32)
                    cc = b * 4 + ci
                    for h in (h0, h0 + 1):
                        hp = h % 2
                        B = self.g_b[hp]
                        tk = lambda n: "g%s_%d" % (n, hp)
                        ob, ot = 6 + hp, "ps%d" % (6 + hp)
                        db, dt_ = DSB[hp], "ps%d" % DSB[hp]
                        self.mm(PS[ob][:, cs], self.SBF[h][:], B["QD1"][:, cs], b == 0 and ci == 0, False,
                                ["SBF_%d" % h, tk("QD1")], [ot])
                        if ci == 3:
                            r64 = slice(64, 128)
                            self.mm(PS[db][:, 0:128], self.g_KDZ[hp][r64, b, :], self.VT[r64, b, h * 128:(h + 1) * 128],
                                    True, True, ["gKDT_%d" % hp, "VT"], [dt_])
                        else:
                            self.mm(PS[db][:, 0:128], self.g_KDT[hp][rs, b, :], self.VT[rs, b, h * 128:(h + 1) * 128],
                                    True, True, ["gKDT_%d" % hp, "VT"], [dt_])
                        self.stt("dve", self.S32[h][:], self.S32[h][:], self.g_DEC[hp][:, cc:cc + 1], PS[db][:, 0:128],
                                 ALU.mult, ALU.add, ["S32_%d" % h, "gDEC_%d" % hp, dt_], ["S32_%d" % h])
                        self.copy("dve", self.SBF[h][:], self.S32[h][:], ["S32_%d" % h], ["SBF_%d" % h])
                for h in (h0, h0 + 1):
                    hp = h % 2
                    self.mm(PS[6 + hp][:, bs], self.VT[:, b, h * 128:(h + 1) * 128], self.g_ATT[hp][:], False, b == nb - 1,
                            ["VT", "gATT_%d" % hp], ["ps%d" % (6 + hp)])
            gens = [o_sink(h, 6 + h % 2, "ps%d" % (6 + h % 2)) for h in (h0, h0 + 1)]
            gens = [g_ for g_ in gens if g_ is not None]
            while gens:
                for g_ in list(gens):
                    try:
                        next(g_)
                    except StopIteration:
                        gens.remove(g_)

    NCH = TT // 8

    def alloc_s5(self):
        NCH = self.NCH
        self.UP = self.sb("UP", [128, 4, 8, NCH], BF16)
        self.YV = self.UP[:].rearrange("p o j c -> p (o j) c")
        self.V = self.sb("Vs5", [128, 32, NCH], BF16)
        self.FF = self.sb("FFs5", [128, 2, 2, 16, NCH], BF16)
        self.WZT = self.sb("WZT", [128, 32, 2, 128], BF16)
        self.KTOE = self.sb("KTOE", [128, 32, 128], BF16)
        self.TCA = self.sb("TCA", [128, 32, 2, 128], BF16)
        self.SELB = self.sb("SELB", [128, 64, 128], BF16)
        self.GY = self.sb("GY", [128, 4, TT], BF16)
        self.S5P = self.sb("S5P", [128, 2144], F32)
        sm = lambda n, w: self.sb("s5_" + n, [128, w], F32)
        self.p_ = {n: sm(n, 32) for n in ("DT", "LRDT", "ANG", "DEN", "NR", "CRE", "CIM", "T1", "T2", "RHO", "PHI")}
        self.p_.update({n: sm(n, 512) for n in ("KL", "KA", "MAGK", "SINK", "COSK", "PRE", "PIM", "BBR", "BBI", "X1", "X2")})
        self.p_.update({n: sm(n, 2048) for n in ("TA", "TB", "TC")})
        self.TWZ = self.sb("TWZ", [128, 16, 2, 128], BF16)
        self.TM2 = self.sb("TM2", [128, 16, 2, 128], BF16)
        self.l2 = {n: self.sb("l2_" + n, [128, NCH], F32) for n in
                   ("ZR", "ZI", "CS", "SN", "XA", "A", "B", "GR", "GI", "ER", "EI")}
        self.EFIN = self.sb("EFIN", [128, 2, 16], F32)
        self.PST = self.sb("PST", [128, 2, 16], F32)
        self.APT = self.sb("APT", [128, 2, 16], F32)
        self.KS = self.sb("KS", [128, 2, 128], F32)

    def s5_prep(self, l, s5p_dram, PC, consts):
        P = self.p_
        PS = self.PS
        S5P = self.S5P
        KV = consts["kv"]
        self.load("ld_s5p", S5P[:], s5p_dram, [], ["s5p"])
        LR, LI, LOGDT = S5P[:, 0:32], S5P[:, 32:64], S5P[:, 64:96]
        v3 = lambda a: a.rearrange("p (a h) -> p a h", h=16)
        BRE, BIM = v3(S5P[:, 96:608]), v3(S5P[:, 608:1120])
        CRE_, CIM_ = v3(S5P[:, 1120:1632]), v3(S5P[:, 1632:2144])
        tkn = lambda n: "s5_" + n
        PI = float(np.pi)
        self.act(P["DT"][:], LOGDT, AF.Exp, ["s5p"], [tkn("DT")])
        self.tt("dve", P["LRDT"][:], LR, P["DT"][:], ALU.mult, ["s5p", tkn("DT")], [tkn("LRDT")])
        self.tt("dve", P["ANG"][:], LI, P["DT"][:], ALU.mult, ["s5p", tkn("DT")], [tkn("ANG")])
        b16 = lambda a: a.unsqueeze(2).to_broadcast([128, 32, 16])
        kvb = KV[:, 0:16].unsqueeze(1).to_broadcast([128, 32, 16])
        self.tt("dve", v3(P["KL"][:]), b16(P["LRDT"][:]), kvb, ALU.mult, [tkn("LRDT"), "consts"], [tkn("KL")])
        self.tt("dve", v3(P["KA"][:]), b16(P["ANG"][:]), kvb, ALU.mult, [tkn("ANG"), "consts"], [tkn("KA")])
        self.act(P["MAGK"][:], P["KL"][:], AF.Exp, [tkn("KL")], [tkn("MAGK")])
        I2P = 1.0 / (2.0 * PI)
        self.ts("dve", P["X1"][:], P["KA"][:], I2P, None, ALU.mult, None, [tkn("KA")], [tkn("X1")])
        self.sin_turns(P["SINK"][:], P["X1"][:], self.XI[:], P["X2"][:], None, tkn("X1"), tkn("SINK"))
        self.ts("dve", P["X1"][:], P["KA"][:], I2P, 0.25, ALU.mult, ALU.add, [tkn("KA")], [tkn("X1")])
        self.sin_turns(P["COSK"][:], P["X1"][:], self.XI[:], P["X2"][:], None, tkn("X1"), tkn("COSK"))
        self.tt("dve", P["PRE"][:], P["MAGK"][:], P["COSK"][:], ALU.mult, [tkn("MAGK"), tkn("COSK")], [tkn("PRE")])
        self.tt("dve", P["PIM"][:], P["MAGK"][:], P["SINK"][:], ALU.mult, [tkn("MAGK"), tkn("SINK")], [tkn("PIM")])
        PRE3, PIM3 = v3(P["PRE"][:]), v3(P["PIM"][:])
        AR, AI = PRE3[:, :, 8], PIM3[:, :, 8]
        self.copy("dve", P["RHO"][:], v3(P["MAGK"][:])[:, :, 15], [tkn("MAGK")], [tkn("RHO")])
        self.ts("dve", P["PHI"][:], P["ANG"][:], 8.0 * I2P, None, ALU.mult, None, [tkn("ANG")], [tkn("PHI")])
        self.copy("dve", self.XI[:, 0:32], P["PHI"][:], [tkn("PHI")], ["s5_XI"])
        self.copy("dve", P["T1"][:], self.XI[:, 0:32], ["s5_XI"], [tkn("T1")])
        self.tt("dve", P["PHI"][:], P["PHI"][:], P["T1"][:], ALU.subtract, [tkn("PHI"), tkn("T1")], [tkn("PHI")])
        self.tt("dve", P["DEN"][:], LR, LR, ALU.mult, ["s5p"], [tkn("DEN")])
        self.tt("dve", P["T1"][:], LI, LI, ALU.mult, ["s5p"], [tkn("T1")])
        self.tt("dve", P["DEN"][:], P["DEN"][:], P["T1"][:], ALU.add, [tkn("DEN"), tkn("T1")], [tkn("DEN")])
        self.S.op("dve", lambda e: e.reciprocal(out=P["DEN"][:], in_=P["DEN"][:]), [tkn("DEN")], [tkn("DEN")])
        self.ts("dve", P["NR"][:], AR, -1.0, None, ALU.add, None, [tkn("PRE")], [tkn("NR")])
        self.tt("dve", P["T1"][:], P["NR"][:], LR, ALU.mult, [tkn("NR"), "s5p"], [tkn("T1")])
        self.tt("dve", P["T2"][:], AI, LI, ALU.mult, [tkn("PIM"), "s5p"], [tkn("T2")])
        self.tt("dve", P["T1"][:], P["T1"][:], P["T2"][:], ALU.add, [tkn("T1"), tkn("T2")], [tkn("T1")])
        self.tt("dve", P["CRE"][:], P["T1"][:], P["DEN"][:], ALU.mult, [tkn("T1"), tkn("DEN")], [tkn("CRE")])
        self.tt("dve", P["T1"][:], AI, LR, ALU.mult, [tkn("PIM"), "s5p"], [tkn("T1")])
        self.tt("dve", P["T2"][:], P["NR"][:], LI, ALU.mult, [tkn("NR"), "s5p"], [tkn("T2")])
        self.tt("dve", P["T1"][:], P["T1"][:], P["T2"][:], ALU.subtract, [tkn("T1"), tkn("T2")], [tkn("T1")])
        self.tt("dve", P["CIM"][:], P["T1"][:], P["DEN"][:], ALU.mult, [tkn("T1"), tkn("DEN")], [tkn("CIM")])
        cre_b, cim_b = b16(P["CRE"][:]), b16(P["CIM"][:])
        ta, tb = v3(P["TA"][:, 0:512]), v3(P["TB"][:, 0:512])
        self.tt("dve", ta, cre_b, BRE, ALU.mult, [tkn("CRE"), "s5p"], [tkn("TA")])
        self.tt("dve", tb, cim_b, BIM, ALU.mult, [tkn("CIM"), "s5p"], [tkn("TB")])
        self.tt("dve", v3(P["BBR"][:]), ta, tb, ALU.subtract, [tkn("TA"), tkn("TB")], [tkn("BBR")])
        self.tt("dve", ta, cre_b, BIM, ALU.mult, [tkn("CRE"), "s5p"], [tkn("TA")])
        self.tt("dve", tb, cim_b, BRE, ALU.mult, [tkn("CIM"), "s5p"], [tkn("TB")])
        self.tt("dve", v3(P["BBI"][:]), ta, tb, ALU.add, [tkn("TA"), tkn("TB")], [tkn("BBI")])
        BBR3, BBI3 = v3(P["BBR"][:]), v3(P["BBI"][:])

        def cplx_table(dst, dtok, d, ksl, XR, XI, neg_im):
            ds_ = slice(d * 16, d * 16 + 16)
            pr_ = PRE3[:, ds_, ksl].unsqueeze(3).to_broadcast([128, 16, 8, 16])
            pi_ = PIM3[:, ds_, ksl].unsqueeze(3).to_broadcast([128, 16, 8, 16])
            xr_ = XR[:, ds_, :].unsqueeze(2).to_broadcast([128, 16, 8, 16])
            xi_ = XI[:, ds_, :].unsqueeze(2).to_broadcast([128, 16, 8, 16])
            v4 = lambda a: a.rearrange("p (a j h) -> p a j h", j=8, h=16)
            A_, B_, C_ = v4(P["TA"][:]), v4(P["TB"][:]), v4(P["TC"][:])
            rd = [tkn("PRE"), tkn("PIM"), tkn("BBR"), tkn("BBI"), "s5p"]
            dre = dst[:, :, 0, :].rearrange("p a (j h) -> p a j h", h=16)
            dim = dst[:, :, 1, :].rearrange("p a (j h) -> p a j h", h=16)
            self.tt("dve", A_, pr_, xr_, ALU.mult, rd, [tkn("TA")])
            self.tt("pool", B_, pi_, xi_, ALU.mult, rd, [tkn("TB")])
            self.tt("dve", dre, A_, B_, ALU.subtract, [tkn("TA"), tkn("TB")], [dtok])
            self.tt("dve", A_, pr_, xi_, ALU.mult, rd, [tkn("TA")])
            self.tt("pool", B_, pi_, xr_, ALU.mult, rd, [tkn("TB")])
            if neg_im:
                self.stt("dve", dim, A_, -1.0, B_, ALU.mult, ALU.subtract, [tkn("TA"), tkn("TB")], [dtok])
            else:
                self.tt("dve", dim, A_, B_, ALU.add, [tkn("TA"), tkn("TB")], [dtok])

        IDB = consts["identb"]
        for d in range(2):
            wz_sl = slice(14, 6, -1) if d == 0 else slice(7, 15)
            m2_sl = slice(0, 8) if d == 0 else slice(7, None, -1)
            ca_sl = slice(8, 16) if d == 0 else slice(15, 7, -1)
            cplx_table(self.TWZ, "TWZ", d, wz_sl, BBR3, BBI3, False)
            cplx_table(self.TM2, "TM2", d, m2_sl, CRE_, CIM_, True)
            cplx_table(self.TCA[:, d * 16:(d + 1) * 16], "TCA", d, ca_sl, CRE_, CIM_, True)
            psb = PS[3][:].bitcast(BF16)
            for pr in range(16):
                for ri in range(2):
                    self.S.op("pe", lambda e, pr=pr, ri=ri: e.transpose(psb[:, 0:128], self.TWZ[:, pr, ri, :], IDB[:]),
                              ["TWZ", "consts"], ["ps3"])
                    self.copy("act", self.WZT[:, d * 16 + pr, ri, :], psb[:, 0:128], ["ps3"], ["WZT"])
                for g2 in range(2):
                    rs = slice(g2 * 64, g2 * 64 + 64)
                    g = 2 * pr + g2
                    bank = 4 + (g % 2)
                    self.mm(PS[bank][:, 0:128], self.TWZ[rs, pr, 0, :], self.TM2[rs, pr, 0, :], True, False,
                            ["TWZ", "TM2"], ["ps%d" % bank])
                    self.mm(PS[bank][:, 0:128], self.TWZ[rs, pr, 1, :], self.TM2[rs, pr, 1, :], False, True,
                            ["TWZ", "TM2"], ["ps%d" % bank])
                    msk = consts["maskF"] if d == 0 else consts["maskB"]
                    if d == 0:
                        self.tt("dve", self.KS[:, g % 2, :], PS[bank][:, 0:128], msk[:], ALU.mult,
                                ["ps%d" % bank, "consts"], ["KS%d" % (g % 2)])
                        self.copy("act", self.KTOE[:, g, :], self.KS[:, g % 2, :], ["KS%d" % (g % 2)], ["KTOE%d" % g])
                    else:
                        self.tt("dve", self.KS[:, g % 2, :], PS[bank][:, 0:128], msk[:], ALU.mult,
                                ["ps%d" % bank, "consts"], ["KS%d" % (g % 2)])
                        self.tt("dve", self.KS[:, g % 2, :], self.KS[:, g % 2, :], self.KTOE[:, g, :], ALU.add,
                                ["KS%d" % (g % 2), "KTOE%d" % g], ["KS%d" % (g % 2)])
                        self.stt("dve", self.KTOE[:, g, :], consts["eye"][:], consts_dcol(self, PC, g), self.KS[:, g % 2, :],
                                 ALU.mult, ALU.add, ["KS%d" % (g % 2), "consts", "pcd"], ["KTOE%d" % g])


def consts_dcol(self, PC, g):
    return self.DSK[:, g:g + 1]


def _s5_main(self, l, PC, consts, exchange_fn):
    PS = self.PS
    NCH = self.NCH
    P = self.p_
    L2 = self.l2
    PI = float(np.pi)
    MIDX = consts["midx"]
    for g in range(32):
        oc, gi = g // 8, g % 8
        bank = g % 2
        for j in range(8):
            self.mm(PS[bank][:, 0:NCH], self.SELB[:, gi * 8 + j, :], self.UP[:, oc, j, :], j == 0, j == 7,
                    ["SELB", "UP"], ["ps%d" % bank])
        self.copy("act", self.V[:, g, :], PS[bank][:, 0:NCH], ["ps%d" % bank], ["V"])

    def level2(d, pr, segs):
        dp = d * 16 + pr
        for ri, bank in ((0, 2), (1, 3)):
            for g2 in range(2):
                self.mm(PS[bank][g2 * 64:(g2 + 1) * 64, 0:NCH], self.WZT[:, dp, ri, g2 * 64:(g2 + 1) * 64],
                        self.V[:, 2 * pr + g2, :], True, True, ["WZT", "V"], ["ps%d" % bank])
        self.copy("act", L2["ZR"][:], PS[2][:, 0:NCH], ["ps2"], ["l2ZR"])
        self.copy("act", L2["ZI"][:], PS[3][:, 0:NCH], ["ps3"], ["l2ZI"])
        phi = P["PHI"][:, dp:dp + 1]
        rho = P["RHO"][:, dp:dp + 1]
        for (zs, n, init, fsl, fin) in segs:
            ZR, ZI = L2["ZR"][:, zs], L2["ZI"][:, zs]
            if init == "P":
                self.tt("dve", ZR[:, 0:1], ZR[:, 0:1], self.APT[:, 0, pr:pr + 1], ALU.add, ["l2ZR", "APT"], ["l2ZR"])
                self.tt("dve", ZI[:, 0:1], ZI[:, 0:1], self.APT[:, 1, pr:pr + 1], ALU.add, ["l2ZI", "APT"], ["l2ZI"])
            self.ts("dve", L2["XA"][:, 0:n], MIDX[:, 0:n], phi, None, ALU.mult, None, ["consts", "s5_PHI"], ["l2XA"])
            self.sin_turns(L2["SN"][:, 0:n], L2["XA"][:, 0:n], self.l2I[:, 0:n], L2["TN"][:, 0:n], None, "l2XA", "l2SN")
            self.ts("dve", L2["XA"][:, 0:n], MIDX[:, 0:n], phi, 0.25, ALU.mult, ALU.add, ["consts", "s5_PHI"], ["l2XA"])
            self.sin_turns(L2["CS"][:, 0:n], L2["XA"][:, 0:n], self.l2I[:, 0:n], L2["TN"][:, 0:n], None, "l2XA", "l2CS")
            CS, SN = L2["CS"][:, 0:n], L2["SN"][:, 0:n]
            A, B, GR, GI = L2["A"][:, 0:n], L2["B"][:, 0:n], L2["GR"][:, 0:n], L2["GI"][:, 0:n]
            self.tt("dve", A, CS, ZR, ALU.mult, ["l2CS", "l2ZR"], ["l2A"])
            self.tt("pool", B, SN, ZI, ALU.mult, ["l2SN", "l2ZI"], ["l2B"])
            self.tt("dve", GR, A, B, ALU.add, ["l2A", "l2B"], ["l2GR"])
            self.tt("dve", A, CS, ZI, ALU.mult, ["l2CS", "l2ZI"], ["l2A"])
            self.tt("pool", B, SN, ZR, ALU.mult, ["l2SN", "l2ZR"], ["l2B"])
            self.tt("dve", GI, A, B, ALU.subtract, ["l2A", "l2B"], ["l2GI"])
            rb = rho.to_broadcast([128, n])
            self.S.op("dve", lambda e, GR=GR, rb=rb: e.tensor_tensor_scan(out=GR, data0=rb, data1=GR, initial=0.0,
                                                                         op0=ALU.mult, op1=ALU.add),
                      ["l2GR", "s5_RHO"], ["l2GR"])
            self.S.op("dve", lambda e, GI=GI, rb=rb: e.tensor_tensor_scan(out=GI, data0=rb, data1=GI, initial=0.0,
                                                                         op0=ALU.mult, op1=ALU.add),
                      ["l2GI", "s5_RHO"], ["l2GI"])
            ER, EI = L2["ER"][:, 0:n], L2["EI"][:, 0:n]
            self.tt("dve", A, CS, GR, ALU.mult, ["l2CS", "l2GR"], ["l2A"])
            self.tt("pool", B, SN, GI, ALU.mult, ["l2SN", "l2GI"], ["l2B"])
            self.tt("dve", ER, A, B, ALU.subtract, ["l2A", "l2B"], ["l2ER"])
            self.tt("dve", A, SN, GR, ALU.mult, ["l2SN", "l2GR"], ["l2A"])
            self.tt("pool", B, CS, GI, ALU.mult, ["l2CS", "l2GI"], ["l2B"])
            self.tt("dve", EI, A, B, ALU.add, ["l2A", "l2B"], ["l2EI"])
            FR = self.FF[:, d, 0, pr, :][:, fsl]
            FI = self.FF[:, d, 1, pr, :][:, fsl]
            self.copy("act", FR[:, 1:n], ER[:, 0:n - 1], ["l2ER"], ["FF"])
            self.copy("act", FI[:, 1:n], EI[:, 0:n - 1], ["l2EI"], ["FF"])
            if init == "P":
                self.copy("act", FR[:, 0:1], self.PST[:, 0, pr:pr + 1], ["PST"], ["FF"])
                self.copy("act", FI[:, 0:1], self.PST[:, 1, pr:pr + 1], ["PST"], ["FF"])
            if fin:
                self.copy("act", self.EFIN[:, 0, pr:pr + 1], ER[:, n - 1:n], ["l2ER"], ["EFIN"])
                self.copy("act", self.EFIN[:, 1, pr:pr + 1], EI[:, n - 1:n], ["l2EI"], ["EFIN"])

    self.S.op("pool", lambda e: e.memset(self.FF[:], 0.0), (), ["FF"])
    for pr in range(16):
        level2(0, pr, [(slice(0, NCH), NCH, None, slice(0, NCH), True)])
    exchange_fn()
    v3 = lambda a: a.rearrange("p (a h) -> p a h", h=16)
    ARb, AIb = v3(P["PRE"][:])[:, 16:32, 15], v3(P["PIM"][:])[:, 16:32, 15]
    T1, T2 = P["T1"][:, 0:16], P["T2"][:, 0:16]
    self.tt("dve", T1, ARb, self.PST[:, 0, :], ALU.mult, ["s5_PRE", "PST"], ["s5_T1"])
    self.tt("dve", T2, AIb, self.PST[:, 1, :], ALU.mult, ["s5_PIM", "PST"], ["s5_T2"])
    self.tt("dve", self.APT[:, 0, :], T1, T2, ALU.subtract, ["s5_T1", "s5_T2"], ["APT"])
    self.tt("dve", T1, ARb, self.PST[:, 1, :], ALU.mult, ["s5_PRE", "PST"], ["s5_T1"])
    self.tt("dve", T2, AIb, self.PST[:, 0, :], ALU.mult, ["s5_PIM", "PST"], ["s5_T2"])
    self.tt("dve", self.APT[:, 1, :], T1, T2, ALU.add, ["s5_T1", "s5_T2"], ["APT"])
    NCC = NCTX // 8
    for pr in range(16):
        level2(1, pr, [(slice(NCH - 1, NCC - 1, -1), NCH - NCC, "P", slice(NCH - 1, NCC - 1, -1), False),
                       (slice(NCC - 1, None, -1), NCC, None, slice(NCC - 1, None, -1), False)])
    for g in range(32):
        pr, g2 = g // 2, g % 2
        rs = slice(g2 * 64, g2 * 64 + 64)
        bank = 4 + g % 2
        pt = "ps%d" % bank
        self.mm(PS[bank][:, 0:NCH], self.KTOE[:, g, :], self.V[:, g, :], True, False, ["KTOE%d" % g, "V"], [pt])
        for d in range(2):
            for ri in range(2):
                self.mm(PS[bank][:, 0:NCH], self.TCA[rs, d * 16 + pr, ri, :], self.FF[rs, d, ri, pr, :], False,
                        d == 1 and ri == 1, ["TCA", "FF"], [pt])
        self.copy("act", self.YV[:, g, :], PS[bank][:, 0:NCH], [pt], ["UP"])


Model.s5_main = _s5_main


def _exchange(self, name, src_ap, ncols, dst_sb, groups, outtok):
    self.uid += 1
    u = self.uid
    dsrc = self.scratch("xsrc%d" % u, [128, ncols])
    ddst = self.scratch("xdst%d" % u, [256, ncols])
    both = self.sb("xboth%d" % u, [128, 2, ncols], F32)
    t = "x%d" % u
    self.load("xs%d" % u, dsrc, src_ap, [name], [t + "a"], eng="pool")
    self.S.dma("pool", "xc%d" % u,
               lambda e, s: e.collective_compute("AllGather", ALU.bypass, replica_groups=groups, ins=[dsrc],
                                                 outs=[ddst]).then_inc(s, 1),
               [t + "a"], [t + "b"], inc=1)
    self.load("xl%d" % u, both[:], ddst.rearrange("(k p) f -> p k f", p=128), [t + "b"], [t + "c"], eng="pool")
    PSEL = self.consts["pairsel"]
    self.ts("dve", dst_sb, both[:, 0, :], PSEL[:, 0:1], None, ALU.mult, None, [t + "c", "consts"], [outtok])
    self.stt("dve", dst_sb, both[:, 1, :], PSEL[:, 1:2], dst_sb, ALU.mult, ALU.add, [t + "c", "consts", outtok],
             [outtok])


Model.exchange = _exchange


def _load_consts(self, cd):
    C = {}
    for n, w in (("seg", 512), ("gmask", 128), ("eye", 128), ("maskF", 128), ("maskB", 128), ("kv", 16),
                 ("midx", 288), ("negpi", 1), ("pairsel", 2), ("m96", 1)):
        C[n] = self.sb("c_" + n, [128, w], F32)
        self.load("ld_c_" + n, C[n][:], cd[n], [], ["consts"])
    C["identb"] = self.sb("c_identb", [128, 128], BF16)
    self.copy("dve", C["identb"][:], C["eye"][:], ["consts"], ["consts"])
    self.consts = C
    return C


Model.load_consts = _load_consts


def _mod_layer(self, l, w_mod, PC, SC, MOD, MOD1, stage):
    PS = self.PS
    CB = 1024
    i = 0
    for cb in range(6):
        bank = cb % 2
        for k in range(DC):
            st = stage[i % 2]
            tok = "stage%d" % (i % 2)
            self.load("ld_" + tok, st[:, 0:CB], w_mod[l, k * 128:(k + 1) * 128, cb * CB:(cb + 1) * CB], [], [tok])
            for mt in range(8):
                self.mm(PS[bank][:, mt * 2:mt * 2 + 2], st[:, mt * 128:(mt + 1) * 128], SC[:, k, :], k == 0 and mt == 0,
                        k == DC - 1 and mt == 7, [tok, "SC"], ["ps%d" % bank])
            i += 1
        pv = PS[bank][:, 0:16].rearrange("p (m c) -> p m c", c=2)
        bm = PC[:, cb * 8:(cb + 1) * 8].unsqueeze(2).to_broadcast([128, 8, 2])
        self.tt("dve", MOD[:, cb * 8:(cb + 1) * 8, :], pv, bm, ALU.add, ["ps%d" % bank, "pc"], ["mod0"])
    self.ts("dve", MOD1[:], MOD[:], 1.0, None, ALU.add, None, ["mod0"], ["mod"])


Model.mod_layer = _mod_layer


def _load_win(self, l, w_in, WIN, c0, c1, stage):
    i = 0
    for k in range(DC):
        for cc in range(c0, c1, 1024):
            cw = min(1024, c1 - cc)
            st = stage[i % 2]
            tok = "stage%d" % (i % 2)
            self.load("ld_" + tok, st[:, 0:cw], w_in[l, k * 128:(k + 1) * 128, cc:cc + cw], [], [tok])
            self.copy(("pool", "act", "dve")[i % 3], WIN[:, k, cc:cc + cw], st[:, 0:cw], [tok], ["WIN"])
            i += 1


Model.load_win = _load_win

TILES = [(0, 256, 256, 1)] + [(256 + 256 * i, 256, 64, 0) for i in range(8)]


def _stage0(self, l, xs, MOD, MOD1):
    X, H = self.m_X[0], self.m_H
    for it, (t0, nt, rw, mc) in enumerate(TILES):
        hb = it % 2
        self.load("ld_mX", X[:, :, 0:nt], xs[:, t0:t0 + nt].rearrange("(k p) t -> p k t", p=128), ["xs"], ["mX"])
        self.modulate_tile(X, H[hb], nt, "mX", "mH%d" % hb, MOD, MOD1, 0, 8, mc, False)
        c0, ncn = t0 // 8, nt // 8
        for oc in range(4):
            bank = oc % 2
            pt = self.proj_fm(bank, H[hb], "mH%d" % hb, nt, oc * 128)
            self.copy("act", self.UP[:, oc, :, c0:c0 + ncn].rearrange("p j c -> p c j"),
                      self.PS[bank][:, 0:nt].rearrange("p (c j) -> p c j", j=8), [pt], ["UP"])


Model.stage0 = _stage0


def _s5_out(self, l, PC):
    PS = self.PS
    NCH = self.NCH
    GY = self.GY
    for oc in range(4):
        for j in range(8):
            bank = j % 2
            for gi in range(8):
                self.mm(PS[bank][:, 0:NCH], self.SELB[:, gi * 8 + j, :], self.YV[:, oc * 8 + gi, :], gi == 0, gi == 7,
                        ["SELB", "UP"], ["ps%d" % bank])
            self.act(GY[:, oc, j:TT:8], PS[bank][:, 0:NCH], AF.Gelu, ["ps%d" % bank], ["GY"])
    bg = PC_OFF["bglu"][0]
    for (t0, nt, rw, mc) in TILES:
        for mo in range(4):
            bank = 2 + mo % 2
            for k in range(4):
                self.mm(PS[bank][:, 0:nt], self.WGLU[:, k, mo * 128:(mo + 1) * 128], GY[:, k, t0:t0 + nt], k == 0, k == 3,
                        ["WGLU", "GY"], ["ps%d" % bank])
            T = self.s_T[mo % 2]
            self.act(T[:, 0:nt], PS[bank][:, 0:nt], AF.Sigmoid, ["ps%d" % bank, "pc"], ["sT%d" % (mo % 2)],
                     bias=PC[:, bg + mo:bg + mo + 1])
            self.tt("dve", self.S5O[:, mo, t0:t0 + nt], GY[:, mo, t0:t0 + nt], T[:, 0:nt], ALU.mult,
                    ["GY", "sT%d" % (mo % 2)], ["S5O"])


Model.s5_out = _s5_out


def _stage1(self, l, xs, MOD, MOD1, LBT):
    X, H = self.m_X[0], self.m_H
    for h in range(4):
        self.S.op("pool", lambda e, h=h: e.memset(self.S32[h][:], 0.0), (), ["S32_%d" % h])
        self.S.op("pool", lambda e, h=h: e.memset(self.SBF[h][:], 0.0), (), ["SBF_%d" % h])
    for it, (t0, nt, rw, mc) in enumerate(TILES):
        hb = it % 2
        self.load("ld_mX", X[:, :, 0:nt], xs[:, t0:t0 + nt].rearrange("(k p) t -> p k t", p=128), ["xs"], ["mX"])
        self.modulate_tile(X, H[hb], nt, "mX", "mH%d" % hb, MOD, MOD1, 0, 8, mc, False)

        def sink(h, ob, ot, t0=t0, nt=nt):
            self.copy("act", self.OF[:, h, t0:t0 + nt], self.PS[ob][:, 0:nt], [ot], ["OF"])
        self.gla_tile(H[hb], "mH%d" % hb, nt, 0, 512, LBT, self.consts, sink)


Model.stage1 = _stage1


def _stage3(self, l, xs, x1dst, MOD, MOD1, LBT, PC, last):
    PS = self.PS
    X, H = self.m_X[0], self.m_H
    o = PC_OFF
    hgn = PC[:, o["hgn"][0]:o["hgn"][0] + 1]
    order = list(range(len(TILES) - 1, 0, -1)) + [0]
    for ii, it in enumerate(order):
        (t0, nt, rw, mc) = TILES[it]
        hb = ii % 2
        if it == 0:
            for h in range(4):
                self.S.op("pool", lambda e, h=h: e.memset(self.S32[h][:], 0.0), (), ["S32_%d" % h])
                self.S.op("pool", lambda e, h=h: e.memset(self.SBF[h][:], 0.0), (), ["SBF_%d" % h])
        self.load("ld_mX", X[:, :, 0:nt], xs[:, t0:t0 + nt].rearrange("(k p) t -> p k t", p=128), ["xs"], ["mX"])
        self.modulate_tile(X, H[hb], nt, "mX", "mH%d" % hb, MOD, MOD1, 0, 8, mc, True)
        htok = "mH%d" % hb

        def sink(h, ob, ot, t0=t0, nt=nt, H=H[hb], htok=htok):
            OS = self.m_OS[h % 2]
            ost = "mOS%d" % (h % 2)
            rb, gb = (2, 3) if h % 2 == 0 else (0, 1)
            self.tt("dve", OS[:, 0:nt], PS[ob][:, 0:nt], self.OF[:, h, t0:t0 + nt][:, ::-1], ALU.add, [ot, "OF"], [ost])
            yield
            T = self.m_T[h % 2]
            tt_ = "mT%d" % (h % 2)
            self.act(T[:, 0:nt], OS[:, 0:nt], AF.Square, [ost], [tt_])
            yield
            self.mm(PS[rb][:, 0:nt], self.ones[:], T[:, 0:nt], True, True, [tt_, "ones"], ["ps%d" % rb])
            yield
            self.act(T[:, 0:nt], PS[rb][:, 0:nt], AF.Sqrt, ["ps%d" % rb], [tt_], bias=self.epscol[:, 1:2], scale=1.0 / 128)
            yield
            self.S.op("dve", lambda e, T=T: e.reciprocal(out=T[:, 0:nt], in_=T[:, 0:nt]), [tt_], [tt_])
            yield
            self.stt("dve", OS[:, 0:nt], OS[:, 0:nt], hgn, T[:, 0:nt], ALU.mult, ALU.mult, [ost, tt_, "pc"], [ost])
            yield
            pg = self.proj_fm(gb, H, htok, nt, 2560 + h * 128)
            yield
            self.act(T[:, 0:nt], PS[gb][:, 0:nt], AF.Silu, [pg], [tt_])
            yield
            self.tt("dve", self.m_HG[:, h, 0:nt], OS[:, 0:nt], T[:, 0:nt], ALU.mult, [ost, tt_], ["mHG"])
            yield
        fcol = 1024
        self.gla_tile(H[hb], htok, nt, 1, fcol, LBT, self.consts, sink)
        for mo in range(DC):
            bank = mo % 2
            pt = "ps%d" % bank
            for k in range(4):
                self.mm(PS[bank][:, 0:nt], self.WOUT[:, k, mo * 128:(mo + 1) * 128], self.S5O[:, k, t0:t0 + nt][:, ::-1],
                        k == 0, False, ["WOUT", "S5O"], [pt])
            for k in range(4):
                self.mm(PS[bank][:, 0:nt], self.WOUT[:, 4 + k, mo * 128:(mo + 1) * 128], self.m_HG[:, k, 0:nt],
                        False, k == 3, ["WOUT", "mHG"], [pt])
            T = self.m_T[mo % 2]
            tt_ = "mT%d" % (mo % 2)
            self.act(T[:, 0:nt], PS[bank][:, 0:nt], AF.Copy, [pt, "mod"], [tt_], scale=MOD[:, 16 + mo, mc:mc + 1])
            self.stt("dve", self.m_Z[:, mo, 0:nt], X[:, mo, 0:nt][:, ::-1], ALPHA, T[:, 0:nt], ALU.mult, ALU.add,
                     ["mX", tt_], ["mZ"])
        self.layer_norm(self.m_Z, self.m_ZSQ, nt, PC[:, o["ln1_g"][0]:o["ln1_g"][0] + 8],
                        PC[:, o["ln1_b"][0]:o["ln1_b"][0] + 8],
                        [X[:, k, 0:nt][:, ::-1] for k in range(DC)], 2, 3, self.lntmp, "mZ", ["mX"] * DC, "m")
        self.load("st_mX", x1dst[:, t0:t0 + nt].rearrange("(k p) t -> p k t", p=128), X[:, :, 0:nt], ["mX"], ["x1"])


Model.stage3 = _stage3


def _alloc_stage13(self):
    NT = 256
    self.WIN = self.sb("WIN", [128, DC, 3072], BF16)
    self.WOUT = self.sb("WOUT", [128, DC, D], BF16)
    self.m_X = [self.sb("mX_0", [128, DC, NT], F32)]
    self.m_H = [self.sb("mH_%d" % i, [128, DC, NT], BF16) for i in range(2)]
    self.VT = self.sb("VT", [128, 2, 512], BF16)
    names = ("SG", "FF", "LF", "BC", "D2", "EX", "QQ", "KK")
    self.g_t = [{n: self.sb("g%s_%d" % (n, i), [128, NT], F32) for n in names} for i in range(2)]
    self.g_b = [{n: self.sb("g%s_%d" % (n, i), [128, NT], BF16) for n in ("QD1", "QD2", "KD")} for i in range(2)]
    self.g_KDT = [self.sb("gKDT_%d" % i, [128, 2, 128], BF16) for i in range(2)]
    self.g_KDZ = [self.sb("gKDZ_%d" % i, [128, 2, 128], BF16) for i in range(2)]
    self.g_DEC = [self.sb("gDEC_%d" % i, [128, 8], F32) for i in range(2)]
    self.g_ATT = [self.sb("gATT_%d" % i, [128, 128], BF16) for i in range(2)]
    self.m_HG = self.sb("mHG", [128, 4, NT], BF16)
    self.m_OS = [self.sb("mOS_%d" % i, [128, NT], F32) for i in range(2)]
    self.m_T = [self.sb("mT_%d" % i, [128, NT], F32) for i in range(2)]
    self.m_Z = self.sb("mZ", [128, DC, NT], F32)
    self.m_ZSQ = self.sb("mZSQ", [128, DC, NT], F32)
    self.lntmp = {n: self.sb("ln_" + n, [128, NT], F32) for n in ("mean", "var", "rstd", "nmr")}


Model.alloc_stage13 = _alloc_stage13


def _alloc_ffn2(self):
    NT = 256
    self.WUP = self.sb("WUP", [128, DC, 2 * DFF], BF16)
    self.WDN = self.sb("WDN", [128, FC, D], BF16)
    self.f_X1 = [self.sb("fX1_0", [128, DC, NT], F32)] * 2
    self.f_H2 = [self.sb("fH2_0", [128, DC, NT], BF16)] * 2
    self.f_CVA = [self.sb("fCVA_%d" % i, [128, NT], F32) for i in range(2)]
    self.f_CVG = [self.sb("fCVG_%d" % i, [128, NT], F32) for i in range(2)]
    self.f_ACT = self.sb("fACT", [128, FC, NT], BF16)
    self.f_T = [self.sb("fT_%d" % i, [128, NT], F32) for i in range(2)]
    self.f_Z = self.sb("fZ", [128, DC, NT], F32)
    self.f_ZSQ = self.f_ACT.rearrange("p a b -> p (a b)")[:, 0:DC * NT * 2].bitcast(F32).rearrange("p (a b) -> p a b", b=NT)
    self.lntmp = {n: self.sb("ln_" + n, [128, NT], F32) for n in ("mean", "var", "rstd", "nmr")}


Model.alloc_ffn2 = _alloc_ffn2


def _alloc_s5a(self):
    self.WZT = self.sb("WZT", [128, 32, 2, 128], BF16)
    self.KTOE = self.sb("KTOE", [128, 32, 128], BF16)
    self.TCA = self.sb("TCA", [128, 32, 2, 128], BF16)
    sm = lambda n, w: self.sb("s5_" + n, [128, w], F32)
    self.p_ = {n: sm(n, 32) for n in ("RHO", "PHI", "T1", "T2")}
    self.p_.update({n: sm(n, 512) for n in ("PRE", "PIM")})


def _alloc_s5a_tmp(self):
    sm = lambda n, w: self.sb("s5_" + n, [128, w], F32)
    self.S5P = self.sb("S5P", [128, 2144], F32)
    self.p_.update({n: sm(n, 32) for n in ("DT", "LRDT", "ANG", "DEN", "NR", "CRE", "CIM")})
    self.p_.update({n: sm(n, 512) for n in ("KL", "KA", "MAGK", "SINK", "COSK", "BBR", "BBI", "X1", "X2")})
    self.p_.update({n: sm(n, 2048) for n in ("TA", "TB", "TC")})
    self.TWZ = self.sb("TWZ", [128, 16, 2, 128], BF16)
    self.TM2 = self.sb("TM2", [128, 16, 2, 128], BF16)
    self.KS = self.sb("KS", [128, 2, 128], F32)
    self.XI = self.sb("s5_XI", [128, 512], F32).bitcast(I32)


def _alloc_s5b(self):
    NCH = self.NCH
    self.V = self.sb("Vs5", [128, 32, NCH], BF16)
    self.GY = self.V.rearrange("p g c -> p (g c)").rearrange("p (o t) -> p o t", o=4)
    self.FF = self.sb("FFs5", [128, 2, 2, 16, NCH], BF16)
    self.SELB = self.sb("SELB", [128, 64, 128], BF16)
    self.l2 = {n: self.sb("l2_" + n, [128, NCH], F32) for n in
               ("ZR", "ZI", "CS", "SN", "XA", "A", "B", "GR", "GI", "ER", "EI", "TN")}
    self.s_T = [self.sb("sT_%d" % i, [128, 256], F32) for i in range(2)]
    self.l2I = self.sb("l2_I", [128, NCH], F32).bitcast(I32)
    self.WGLU = self.sb("WGLU", [128, 4, 512], BF16)


Model.alloc_s5a, Model.alloc_s5a_tmp, Model.alloc_s5b = _alloc_s5a, _alloc_s5a_tmp, _alloc_s5b


def _load_sel(self, sel_dram):
    flat = self.SELB.rearrange("p a b -> p (a b)")
    for i in range(8):
        st = self.stage[i % 2]
        tok = "stage%d" % (i % 2)
        self.load("ld_" + tok, st[:, 0:1024], sel_dram[:, i * 1024:(i + 1) * 1024], [], [tok])
        self.copy("pool", flat[:, i * 1024:(i + 1) * 1024], st[:, 0:1024], [tok], ["SELB"])


Model.load_sel = _load_sel


def build_program(ncores=8, nlayers=DEPTH, debug=None, stop_after=None):
    m = Model(nlayers, debug)
    m.setup_common()
    L = DEPTH
    LW = nlayers
    xT = m.inp("xT", [D, TT])
    w_in = m.inp("w_in", [LW, D, 3072]); w_out = m.inp("w_out", [LW, D, D]); w_glu = m.inp("w_glu", [LW, 512, 512])
    w_up = m.inp("w_up", [LW, D, 2 * DFF]); w_down = m.inp("w_down", [LW, DFF, D]); w_mod = m.inp("w_mod", [LW, D, 6 * D])
    pc = m.inp("pc", [L, 128, PC_N]); dsk = m.inp("dsk", [L, 128, 32]); s5p = m.inp("s5p", [L, 128, 2144])
    cin = m.inp("cin", [128, DC, 2]); hglb = m.inp("hglb", [128, L, 8])
    sel = m.inp("sel", [128, 8192]); selT = m.inp("selT", [128, 8192])
    cd = {n: m.inp("k_" + n, [128, w]) for n, w in (("seg", 512), ("gmask", 128), ("eye", 128), ("maskF", 128),
                                                      ("maskB", 128), ("kv", 16), ("midx", 288), ("negpi", 1),
                                                      ("pairsel", 2), ("m96", 1))}
    out = m.outp("outT", [D, NX])
    XS = m.scratch("XS", [D, TT])
    X1S = m.scratch("X1S", [D, TT])
    groups = [[2 * i, 2 * i + 1] for i in range(ncores // 2)]
    C = m.load_consts(cd)
    PC = m.sb("PC", [128, PC_N]); MOD = m.sb("MOD", [128, 48, 2]); MOD1 = m.sb("MOD1", [128, 48, 2])
    m.DSK = m.sb("DSK", [128, 32])
    SC = m.sb("SC", [128, DC, 2]); LBA = m.sb("LBA", [128, L, 8]); OMA = m.sb("OMA", [128, L, 8])
    LSUM = m.sb("LSUM", [128, 8])
    m.S32 = [m.sb("S32_%d" % h, [128, 128], F32) for h in range(4)]
    m.SBF = [m.sb("SBF_%d" % h, [128, 128], BF16) for h in range(4)]
    m.EFIN = m.sb("EFIN", [128, 2, 16]); m.PST = m.sb("PST", [128, 2, 16]); m.APT = m.sb("APT", [128, 2, 16])
    GSRC = m.sb("GSRC", [128, 512]); GDST = m.sb("GDST", [128, 512])
    m.stage = [m.sb("stage%d" % i, [128, 1408], F32) for i in range(2)]
    m.load("ld_sc", SC[:], cin, [], ["SC0"])
    m.act(SC[:], SC[:], AF.Silu, ["SC0"], ["SC"])
    m.load("ld_lb", LBA[:], hglb, [], ["LBA0"])
    m.act(LBA[:], LBA[:], AF.Exp, ["LBA0"], ["LBA0"])
    m.copy("dve", LSUM[:], LBA[:, 0, :], ["LBA0"], ["LSUM"])
    for l in range(1, L):
        m.tt("dve", LSUM[:], LSUM[:], LBA[:, l, :], ALU.add, ["LSUM", "LBA0"], ["LSUM"])
    m.S.op("dve", lambda e: e.reciprocal(out=LSUM[:], in_=LSUM[:]), ["LSUM"], ["LSUM"])
    for l in range(L):
        m.tt("dve", LBA[:, l, :], LBA[:, l, :], LSUM[:], ALU.mult, ["LSUM", "LBA0"], ["LBA0"])
    m.S.op("dve", lambda e: e.memset(LBA[:, 0, :], 0.0), ["LBA0"], ["LBA0"])
    for l in range(2, L):
        m.tt("dve", LBA[:, l, :], LBA[:, l, :], LBA[:, l - 1, :], ALU.add, ["LBA0"], ["LBA0"])
    m.ts("dve", OMA[:], LBA[:], -1.0, 1.0, ALU.mult, ALU.add, ["LBA0"], ["lbt"])
    m.stage_mark()
    base = m.apos

    def finish(dumps):
        m.S.barrier()
        for i, (nm, ap, shape, dt) in enumerate(dumps):
            o = m.outp("dbg_" + nm, shape, dt)
            m.load("dbgs%d" % i, o, ap, [], ["dbgo%d" % i])
        m.S.barrier()
        m.S.emit()
        return m

    for l in range(nlayers):
        last = l == DEPTH - 1
        m.hard_barrier()
        m.S.new_epoch("_L%d" % l)
        src = xT if l == 0 else XS
        LBT = {"lb": LBA[:, l, :], "oml": OMA[:, l, :]}
        m.load("ld_pc", PC[:], pc[l], [], ["pc"])
        m.load("ld_dsk", m.DSK[:], dsk[l], [], ["pcd"])
        m.mod_layer(l, w_mod, PC, SC, MOD, MOD1, m.stage)
        if stop_after == "mod":
            return finish([("MOD", MOD, [128, 48, 2], F32), ("LBA", LBA, [128, L, 8], F32), ("SC", SC, [128, DC, 2], F32)])
        m.S5O = m.sb("S5O", [128, 4, TT], BF16)
        mark1 = m.apos
        m.alloc_s5a()
        mark2 = m.apos
        m.alloc_s5a_tmp()
        m.s5_prep(l, s5p[l], PC, C)
        if stop_after == "s5prep":
            return finish([("WZT", m.WZT, [128, 32, 2, 128], BF16), ("KTOE", m.KTOE, [128, 32, 128], BF16),
                           ("TCA", m.TCA, [128, 32, 2, 128], BF16), ("PRE", m.p_["PRE"], [128, 512], F32),
                           ("PIM", m.p_["PIM"], [128, 512], F32), ("RHO", m.p_["RHO"], [128, 32], F32),
                           ("PHI", m.p_["PHI"], [128, 32], F32), ("BBR", m.p_["BBR"], [128, 512], F32)])
        m.hard_barrier(); m.apos = mark2
        m.UP = m.sb("UP", [128, 4, 8, m.NCH], BF16)
        m.YV = m.UP.rearrange("p o j c -> p (o j) c")
        mark3 = m.apos
        m.WIN = m.sb("WINu", [128, DC, 512], BF16)
        m.m_X = [m.sb("mX_0", [128, DC, 256], F32)]
        m.m_H = [m.sb("mH_%d" % i, [128, DC, 256], BF16) for i in range(2)]
        m.load_win(l, w_in, m.WIN, 0, 512, m.stage)
        m.stage0(l, src, MOD, MOD1)
        if stop_after == "stage0":
            return finish([("UP", m.UP, [128, 4, 8, m.NCH], BF16)])
        m.hard_barrier(); m.apos = mark3
        m.alloc_s5b()
        m.load_sel(sel)
        m.load_weight("WGLU", w_glu[l], m.WGLU, 4, 512, 512, m.stage)

        def xch():
            m.exchange("EFIN", m.EFIN.rearrange("p a b -> p (a b)"), 32, m.PST.rearrange("p a b -> p (a b)"), groups, "PST")
        m.s5_main(l, PC, C, xch)
        if stop_after == "s5main":
            return finish([("YV", m.YV, [128, 32, m.NCH], BF16), ("FF", m.FF, [128, 2, 2, 16, m.NCH], BF16),
                           ("V", m.V, [128, 32, m.NCH], BF16), ("PST", m.PST, [128, 2, 16], F32),
                           ("EFIN", m.EFIN, [128, 2, 16], F32)])
        m.load_sel(selT)
        m.s5_out(l, PC)
        if stop_after == "s5out":
            return finish([("S5O", m.S5O, [128, 4, TT], BF16), ("GY", m.GY, [128, 4, TT], BF16),
                           ("YV", m.YV, [128, 32, m.NCH], BF16)])
        m.hard_barrier(); m.apos = mark1
        if stop_after == "stage1":
            oe = m.outp("dbg_S5Oearly", [128, 4, TT], BF16)
            m.load("dbgearly", oe, m.S5O, [], ["dbgearly"])
            m.S.barrier()
        m.OF = m.sb("OF", [128, 4, TT], BF16)
        m.alloc_stage13()
        m.load_win(l, w_in, m.WIN, 512, 3072, m.stage)
        m.load_weight("WOUT", w_out[l], m.WOUT, DC, D, 1024, m.stage)
        m.stage1(l, src, MOD, MOD1, LBT)
        if stop_after == "stage1":
            return finish([("OF", m.OF, [128, 4, TT], BF16), ("S5O", m.S5O, [128, 4, TT], BF16)] +
                          [("S32_%d" % h, m.S32[h], [128, 128], F32) for h in range(4)])
        for h in range(4):
            m.copy("dve", GSRC[:, h * 128:(h + 1) * 128], m.S32[h][:], ["S32_%d" % h], ["GSRC"])
        m.exchange("GSRC", GSRC[:], 512, GDST[:], groups, "GSRC_p")
        for h in range(4):
            m.copy("dve", m.S32[h][:], GDST[:, h * 128:(h + 1) * 128], ["GSRC_p"], ["S32_%d" % h])
            m.copy("act", m.SBF[h][:], GDST[:, h * 128:(h + 1) * 128], ["GSRC_p"], ["SBF_%d" % h])
        m.stage3(l, src, X1S, MOD, MOD1, LBT, PC, last)
        m.hard_barrier(); m.apos = base
        m.alloc_ffn2()
        m.load_weight("WUP", w_up[l], m.WUP, DC, 2 * DFF, 1408, m.stage)
        m.load_weight("WDN", w_down[l], m.WDN, FC, D, 1024, m.stage)
        tiles = TILES[1:] if last else TILES
        if last:
            dst_fn = lambda t0, nt: [(out[:, t0 - NCTX:t0 - NCTX + nt].rearrange("(k p) t -> p k t", p=128), "xs")]
        else:
            dst_fn = lambda t0, nt: [(XS[:, t0:t0 + nt].rearrange("(k p) t -> p k t", p=128), "xs")]
        m.ffn_stage(l, X1S, dst_fn, PC, MOD, MOD1, tiles)
        m.hard_barrier(); m.apos = base
    if nlayers < DEPTH:
        dx = m.outp("dbgXS", [D, TT]); d1 = m.outp("dbgX1", [D, TT])
        m.load("dbg_a", dx, XS, ["xs"], ["dbg1"])
        m.load("dbg_b", d1, X1S, ["x1"], ["dbg2"])
    m.S.barrier()
    m.S.emit()
    return m


def _const_tables():
    t = np.arange(512)
    seg = np.broadcast_to((t % 32 != 0).astype(np.float32), (128, 512)).copy()
    s = np.arange(128)
    gmask = ((s[:, None] // 32 == s[None, :] // 32) & (s[None, :] >= s[:, None])).astype(np.float32)
    eye = np.eye(128, dtype=np.float32)
    jj = s // 16
    maskF = (jj[None, :] >= jj[:, None]).astype(np.float32)
    maskB = (jj[None, :] <= jj[:, None]).astype(np.float32)
    kv = np.broadcast_to(np.arange(-7, 9, dtype=np.float32), (128, 16)).copy()
    midx = np.broadcast_to(np.arange(1, 289, dtype=np.float32), (128, 288)).copy()
    negpi = np.full((128, 1), -np.pi, np.float32)
    m96 = (np.arange(128) >= 96).astype(np.float32).reshape(128, 1)
    sel = np.zeros((128, 8, 8, 128), np.float32)
    selT = np.zeros((128, 8, 8, 128), np.float32)
    for gi in range(8):
        for j in range(8):
            for h in range(16):
                sel[gi * 16 + h, gi, j, j * 16 + h] = 1.0
                selT[j * 16 + h, gi, j, gi * 16 + h] = 1.0
    return dict(seg=seg, gmask=gmask, eye=eye, maskF=maskF, maskB=maskB, kv=kv, midx=midx, negpi=negpi, m96=m96), \
        sel.reshape(128, 8192), selT.reshape(128, 8192)


def prepare_core_inputs(inputs, b, s):
    f32 = lambda a: np.ascontiguousarray(np.asarray(a, np.float32))
    L = DEPTH
    x, c, ctx, c_ctx = inputs["x"], inputs["c"], inputs["ctx"], inputs["c_ctx"]
    xl = np.asarray(x[b, s * NX:(s + 1) * NX])
    cl = np.asarray(ctx[b])
    if s == 1:
        xl, cl = xl[::-1], cl[::-1]
    dd = [0, 1] if s == 0 else [1, 0]
    m = {}
    m["xT"] = f32(np.concatenate([cl, xl], axis=0).T)
    w_in = np.asarray(inputs["w_in"])
    if s == 1:
        w_in = np.concatenate([w_in[:, :, 0:512], w_in[:, :, 1024:1536], w_in[:, :, 512:1024], w_in[:, :, 1536:]], axis=2)
    m["w_in"] = f32(w_in)
    for n in ("w_out", "w_glu", "w_up", "w_down", "w_mod"):
        m[n] = f32(inputs[n])
    pc = np.zeros((L, 128, PC_N), np.float32)
    dsk = np.zeros((L, 128, 32), np.float32)
    s5p = np.zeros((L, 128, 2144), np.float32)
    for l in range(L):
        cw = np.asarray(inputs["conv_w"][l])
        taps = [cw[0], cw[1], cw[2]] if s == 0 else [cw[2], cw[1], cw[0]]
        vec = {"b_mod": inputs["b_mod"][l], "ln1_g": inputs["ln1_g"][l], "ln1_b": inputs["ln1_b"][l],
               "ln2_g": inputs["ln2_g"][l], "ln2_b": inputs["ln2_b"][l], "cw0": taps[0], "cw1": taps[1],
               "cw2": taps[2], "cb": inputs["conv_b"][l], "s5d": inputs["s5_d"][l], "bglu": inputs["b_glu"][l],
               "hgn": inputs["hg_norm_w"][l]}
        for n, (o_, k) in PC_OFF.items():
            pc[l, :, o_:o_ + k] = cols(vec[n])
        sd = np.asarray(inputs["s5_d"][l]).reshape(32, 16)
        dsk[l] = np.tile(sd.T, (8, 1))
        for dl in range(2):
            d = dd[dl]
            for nm, off in (("s5_lam_re", 0), ("s5_lam_im", 32)):
                a = np.asarray(inputs[nm][l, d]).reshape(16, 2, 64)
                s5p[l, :, off + dl * 16:off + dl * 16 + 16] = a.transpose(1, 2, 0).reshape(128, 16)
            ld = np.asarray(inputs["s5_log_dt"][l, d]).reshape(16, 2)
            s5p[l, :, 64 + dl * 16:64 + dl * 16 + 16] = np.repeat(ld.T[:, None, :], 64, axis=1).reshape(128, 16)
            for nm, off in (("s5_b_re", 96), ("s5_b_im", 608)):
                a = np.asarray(inputs[nm][l, d]).reshape(16, 2, 64, 16)
                s5p[l, :, off + dl * 256:off + dl * 256 + 256] = a.transpose(1, 2, 0, 3).reshape(128, 256)
            for nm, off in (("s5_c_re", 1120), ("s5_c_im", 1632)):
                a = np.asarray(inputs[nm][l, d]).reshape(16, 2, 16, 64)
                s5p[l, :, off + dl * 256:off + dl * 256 + 256] = a.transpose(1, 3, 0, 2).reshape(128, 256)
    m["pc"], m["dsk"], m["s5p"] = pc, dsk, s5p
    m["cin"] = f32(np.stack([cols(c[b]), cols(c_ctx)], axis=-1))
    hg = np.asarray(inputs["hg_lb"])
    hglb = np.zeros((128, L, 8), np.float32)
    for l in range(L):
        for dl in range(2):
            hglb[:, l, dl * 4:dl * 4 + 4] = cols(hg[l, dd[dl]])
    m["hglb"] = hglb
    ct, sel, selT = _const_tables()
    for n, v in ct.items():
        m["k_" + n] = v
    m["k_pairsel"] = np.broadcast_to(np.array([0.0, 1.0] if s == 0 else [1.0, 0.0], np.float32), (128, 2)).copy()
    m["sel"], m["selT"] = sel, selT
    return m


_PROG = {}


def kernel(**inputs):
    ncores = 8
    if "p" not in _PROG:
        _PROG["p"] = build_program(ncores, DEPTH)
    prog = _PROG["p"]
    in_maps = [prepare_core_inputs(inputs, cid // 2, cid % 2) for cid in range(ncores)]
    res = run_bass_kernel_spmd(prog.nc, in_maps, core_ids=list(range(ncores)))
    B = 4
    outp = np.zeros((B, 2 * NX, D), np.float32)
    for cid in range(ncores):
        b, s = cid // 2, cid % 2
        o = np.asarray(res.results[cid]["outT"]).T
        if s == 1:
            o = o[::-1]
        outp[b, s * NX:(s + 1) * NX] = o
    return outp
```
